# Optimizing a Trainium2 kernel written in Bass

```python
import math
import jax, jax.numpy as jnp
from jax import lax
import numpy as np

D_MODEL = 1024
BATCH = 8
SEQ = 8192
DEPTH = 1
DEC_BATCH = 8
DEC_SEQ = 64
PAST_LEN = 1024

CHUNK = 64
Q_BLOCK = 128
D_MIX = 2 * D_MODEL
A_WIDTH = D_MIX // 2
B_WIDTH = D_MIX - A_WIDTH
A_V_DIM = 128
A_HEADS = A_WIDTH // A_V_DIM
A_QK_DIM = A_V_DIM // 2
ROT_DIM = A_QK_DIM // 4
ROPE_THETA = 500000.0
B_HEAD_DIM = 64
B_HEADS = B_WIDTH // B_HEAD_DIM
W_RANK = 64
A_RANK = 64
NORM_EPS = 1e-6
SUBLN_EPS = 1e-5
GN_EPS = 64e-5

A_Q = 0
A_K = A_Q + A_WIDTH
A_V = A_K + A_WIDTH
A_G = A_V + A_WIDTH
B_R = A_G + A_WIDTH
B_K = B_R + B_WIDTH
B_V = B_K + B_WIDTH
B_WD = B_V + B_WIDTH
B_AD = B_WD + W_RANK
B_G = B_AD + A_RANK
IN_TOTAL = B_G + B_WIDTH
SHIFT_W = B_G - B_R

kernel_name = 'hybrid_diffattn_rwkv7_stream_step'

F32 = jnp.float32


def _rms_norm(x, g, eps):
    xf = x.astype(F32)
    y = xf * lax.rsqrt(jnp.mean(xf * xf, axis=-1, keepdims=True) + eps)
    return (y * g.astype(F32)).astype(x.dtype)


def _partial_rope(x, pos):
    half = ROT_DIM // 2
    inv = jnp.power(jnp.float32(ROPE_THETA), -jnp.arange(half, dtype=F32) * (2.0 / ROT_DIM))
    ang = pos.astype(F32)[:, None] * inv[None, :]
    cos = jnp.cos(ang)[None, :, None, None, :]
    sin = jnp.sin(ang)[None, :, None, None, :]
    xf = x.astype(F32)
    x1 = xf[..., :half]
    x2 = xf[..., half:ROT_DIM]
    out = jnp.concatenate([x1 * cos - x2 * sin, x2 * cos + x1 * sin, xf[..., ROT_DIM:]], axis=-1)
    return out.astype(x.dtype)


def _diff_attend(q, k, v, q_pos, k_pos, lam):
    s = jnp.einsum('bqhcd,bkhcd->bchqk', q, k).astype(F32) * (A_QK_DIM ** -0.5)
    visible = (k_pos[None, :] // CHUNK) <= (q_pos[:, None] // CHUNK)
    s = jnp.where(visible[None, None, None], s, -jnp.inf)
    p = jax.nn.softmax(s, axis=-1)
    wgt = p[:, 0] - lam * p[:, 1]
    return jnp.einsum('bhqk,bkhd->bqhd', wgt.astype(v.dtype), v)


def _wkv_scan(r, w, k, v, a, b, s0):
    def step(s, inp):
        r_t, w_t, k_t, v_t, a_t, b_t = inp
        sa = jnp.einsum('bhij,bhj->bhi', s, a_t)
        s = s * w_t[:, :, None, :] + sa[..., None] * b_t[:, :, None, :] + v_t[..., None] * k_t[:, :, None, :]
        return s, jnp.einsum('bhij,bhj->bhi', s, r_t)
    xs = tuple(jnp.moveaxis(t, 1, 0) for t in (r, w, k, v, a, b))
    s_T, ys = lax.scan(step, s0, xs)
    return jnp.moveaxis(ys, 0, 1), s_T


def _layer(x, pos, past_k, past_v, past_pos, wkv0, shift0, lw, lambda_init):
    (norm_pre, w_in, lam_q1, lam_k1, lam_q2, lam_k2, subln, mu_shift, w0, w_up, a0, a_up,
     k_k, k_a, r_k, ln_x_w, ln_x_b, w_out, norm_post) = lw
    bsz, t_len, _ = x.shape
    h = _rms_norm(x, norm_pre, NORM_EPS)
    proj = h @ w_in

    q = _partial_rope(proj[..., A_Q:A_K].reshape(bsz, t_len, A_HEADS, 2, A_QK_DIM), pos)
    k = _partial_rope(proj[..., A_K:A_V].reshape(bsz, t_len, A_HEADS, 2, A_QK_DIM), pos)
    v = proj[..., A_V:A_G].reshape(bsz, t_len, A_HEADS, A_V_DIM)
    lam = (jnp.exp(jnp.sum(lam_q1.astype(F32) * lam_k1.astype(F32)))
           - jnp.exp(jnp.sum(lam_q2.astype(F32) * lam_k2.astype(F32))) + lambda_init)
    if past_k is None:
        nblk = t_len // Q_BLOCK
        qb = jnp.moveaxis(q.reshape(bsz, nblk, Q_BLOCK, A_HEADS, 2, A_QK_DIM), 1, 0)
        pb = pos.reshape(nblk, Q_BLOCK)
        o = lax.map(lambda qp: _diff_attend(qp[0], k, v, qp[1], pos, lam), (qb, pb))
        o = jnp.moveaxis(o, 0, 1).reshape(bsz, t_len, A_HEADS, A_V_DIM)
    else:
        k_all = jnp.concatenate([past_k.astype(k.dtype), k], axis=1)
        v_all = jnp.concatenate([past_v.astype(v.dtype), v], axis=1)
        k_pos = jnp.concatenate([past_pos, pos])
        o = _diff_attend(q, k_all, v_all, pos, k_pos, lam)
    o = _rms_norm(o, subln, SUBLN_EPS).astype(F32) * (1.0 - lambda_init)
    y_a = o.reshape(bsz, t_len, A_WIDTH) * jax.nn.silu(proj[..., A_G:B_R].astype(F32))

    ps = proj[..., B_R:B_G]
    prev = jnp.concatenate([shift0.astype(ps.dtype), ps[:, :-1]], axis=1)
    m = (ps + (prev - ps) * mu_shift).astype(F32)
    r = m[..., 0:B_WIDTH]
    kb = m[..., B_WIDTH:2 * B_WIDTH]
    vb = m[..., 2 * B_WIDTH:3 * B_WIDTH]
    wd = m[..., 3 * B_WIDTH:3 * B_WIDTH + W_RANK]
    ad = m[..., 3 * B_WIDTH + W_RANK:]
    w_log = -jax.nn.softplus(-(w0.astype(F32) + jnp.tanh(wd) @ w_up.astype(F32))) - 0.5
    decay = jnp.exp(-jnp.exp(w_log))
    a = jax.nn.sigmoid(a0.astype(F32) + ad @ a_up.astype(F32))
    hs = lambda t: t.reshape(bsz, t_len, B_HEADS, B_HEAD_DIM)
    kk = hs(kb * k_k.astype(F32))
    kk = kk / jnp.maximum(jnp.sqrt(jnp.sum(kk * kk, axis=-1, keepdims=True)), 1e-12)
    kb = kb * (1.0 + (a - 1.0) * k_a.astype(F32))
    rh, kh, vh, ah = hs(r), hs(kb), hs(vb), hs(a)
    yb, s_T = _wkv_scan(rh, hs(decay), kh, vh, -kk, kk * ah, wkv0.astype(F32))
    mean = jnp.mean(yb, axis=-1, keepdims=True)
    var = jnp.mean(jnp.square(yb - mean), axis=-1, keepdims=True)
    yb = ((yb - mean) * lax.rsqrt(var + GN_EPS)).reshape(bsz, t_len, B_WIDTH)
    yb = yb * ln_x_w.astype(F32) + ln_x_b.astype(F32)
    bonus = jnp.sum(rh * kh * r_k.astype(F32), axis=-1, keepdims=True) * vh
    yb = (yb + bonus.reshape(bsz, t_len, B_WIDTH)) * jax.nn.silu(proj[..., B_G:IN_TOTAL].astype(F32))

    mix = jnp.concatenate([y_a, yb], axis=-1).astype(x.dtype)
    out = mix @ w_out
    x_new = x + _rms_norm(out, norm_post, NORM_EPS)
    return x_new, k, v, s_T.astype(x.dtype), ps[:, -1:]


def setup_inputs(seed: int = 0) -> dict:
    key = jax.random.key(seed)
    ks = jax.random.split(key, 32)
    L = DEPTH
    nrm = lambda k, shape, scale: jax.random.normal(k, shape, F32) * scale
    return {
        'x_prompt': nrm(ks[0], (BATCH, SEQ, D_MODEL), 1.0),
        'x_sample': nrm(ks[1], (DEC_BATCH, DEC_SEQ, D_MODEL), 1.0),
        'cache_k': nrm(ks[2], (L, DEC_BATCH, PAST_LEN, A_HEADS, 2, A_QK_DIM), 1.0),
        'cache_v': nrm(ks[3], (L, DEC_BATCH, PAST_LEN, A_HEADS, A_V_DIM), 1.0),
        'state_wkv': nrm(ks[4], (L, DEC_BATCH, B_HEADS, B_HEAD_DIM, B_HEAD_DIM), 0.3),
        'state_shift': nrm(ks[5], (L, DEC_BATCH, 1, SHIFT_W), 1.0),
        'norm_pre': 1.0 + nrm(ks[6], (L, D_MODEL), 0.05),
        'w_in': nrm(ks[7], (L, D_MODEL, IN_TOTAL), D_MODEL ** -0.5),
        'lam_q1': nrm(ks[8], (L, A_QK_DIM), 0.1),
        'lam_k1': nrm(ks[9], (L, A_QK_DIM), 0.1),
        'lam_q2': nrm(ks[10], (L, A_QK_DIM), 0.1),
        'lam_k2': nrm(ks[11], (L, A_QK_DIM), 0.1),
        'subln': 1.0 + nrm(ks[12], (L, A_V_DIM), 0.05),
        'mu_shift': jax.random.uniform(ks[13], (L, SHIFT_W), F32, 0.0, 1.0),
        'w0': jax.random.uniform(ks[14], (L, B_WIDTH), F32, -2.0, 2.0),
        'w_up': nrm(ks[15], (L, W_RANK, B_WIDTH), 0.1 * W_RANK ** -0.5),
        'a0': nrm(ks[16], (L, B_WIDTH), 0.1),
        'a_up': nrm(ks[17], (L, A_RANK, B_WIDTH), 0.5 * A_RANK ** -0.5),
        'k_k': 0.85 + nrm(ks[18], (L, B_WIDTH), 0.05),
        'k_a': 1.0 + nrm(ks[19], (L, B_WIDTH), 0.05),
        'r_k': nrm(ks[20], (L, B_HEADS, B_HEAD_DIM), 0.1),
        'ln_x_w': 1.0 + nrm(ks[21], (L, B_WIDTH), 0.05),
        'ln_x_b': nrm(ks[22], (L, B_WIDTH), 0.02),
        'w_out': nrm(ks[23], (L, D_MIX, D_MODEL), D_MIX ** -0.5),
        'norm_post': 1.0 + nrm(ks[24], (L, D_MODEL), 0.05),
    }


def reference(x_prompt, x_sample, cache_k, cache_v, state_wkv, state_shift, norm_pre, w_in,
              lam_q1, lam_k1, lam_q2, lam_k2, subln, mu_shift, w0, w_up, a0, a_up, k_k, k_a,
              r_k, ln_x_w, ln_x_b, w_out, norm_post):
    b_p, t_p, _ = x_prompt.shape
    b_s, t_s, _ = x_sample.shape
    past_len = cache_k.shape[2]
    pos_p = jnp.arange(t_p, dtype=jnp.int32)
    pos_s = past_len + jnp.arange(t_s, dtype=jnp.int32)
    past_pos = jnp.arange(past_len, dtype=jnp.int32)
    xp, xs = x_prompt, x_sample
    kp_l, vp_l, wp_l, sp_l = [], [], [], []
    ks_l, vs_l, ws_l, ss_l = [], [], [], []
    for l in range(DEPTH):
        lambda_init = 0.8 - 0.6 * math.exp(-0.3 * l)
        lw = (norm_pre[l], w_in[l], lam_q1[l], lam_k1[l], lam_q2[l], lam_k2[l], subln[l],
              mu_shift[l], w0[l], w_up[l], a0[l], a_up[l], k_k[l], k_a[l], r_k[l],
              ln_x_w[l], ln_x_b[l], w_out[l], norm_post[l])
        wkv_zero = jnp.zeros((b_p, B_HEADS, B_HEAD_DIM, B_HEAD_DIM), F32)
        shift_zero = jnp.zeros((b_p, 1, SHIFT_W), xp.dtype)
        xp, kp, vp, wp, sp = _layer(xp, pos_p, None, None, None, wkv_zero, shift_zero, lw, lambda_init)
        xs, k_s, v_s, w_s, s_s = _layer(xs, pos_s, cache_k[l], cache_v[l], past_pos,
                                       state_wkv[l], state_shift[l], lw, lambda_init)
        kp_l.append(kp); vp_l.append(vp); wp_l.append(wp); sp_l.append(sp)
        ks_l.append(k_s); vs_l.append(v_s); ws_l.append(w_s.astype(state_wkv.dtype)); ss_l.append(s_s)
    return (xp, xs,
            jnp.stack(kp_l), jnp.stack(vp_l), jnp.stack(wp_l), jnp.stack(sp_l),
            jnp.stack(ks_l), jnp.stack(vs_l), jnp.stack(ws_l), jnp.stack(ss_l))
```

```python
import contextlib
import os
import numpy as np
import concourse.bass as bass
import concourse.mybir as mybir
from concourse.bass_utils import run_bass_kernel_spmd

F32 = mybir.dt.float32
BF16 = mybir.dt.bfloat16
AF = mybir.ActivationFunctionType
ALU = mybir.AluOpType
AX = mybir.AxisListType

ARENA_W = 45056
D = 1024
SEQ = 8192
PAST = 1024
DEC = 64
NCORES = 8
LAMBDA_INIT = 0.2
NORM_EPS = 1e-6
SUBLN_EPS = 1e-5
GN_EPS = 64e-5


class Sched:
    ENGS = ['pe', 'act', 'dve', 'pool', 'sp']

    def __init__(self, nc, same_engine_sync=('act', 'dve', 'pool')):
        self.nc = nc
        self.ops = {e: [] for e in self.ENGS}
        self.res = {}
        self.chan_count = {}
        self.same_engine_sync = set(same_engine_sync)
        self.last_real = {}
        self.nrec = 0
        self.limit = int(os.environ.get('KMAXOPS', '1000000000'))

    def _deps(self, reads, writes):
        deps = set()
        for k in reads:
            st = self.res.get(k)
            if st and st['w'] is not None:
                deps.add(st['w'])
        for k in writes:
            st = self.res.get(k)
            if st:
                if st['w'] is not None:
                    deps.add(st['w'])
                for r in st['r']:
                    deps.add(('war',) + r)
        return deps

    def _commit(self, tok, reads, writes):
        for k in writes:
            self.res[k] = {'w': tok, 'r': []}
        for k in reads:
            st = self.res.setdefault(k, {'w': None, 'r': []})
            st['r'].append(tok)

    def op(self, eng, fn, reads=(), writes=()):
        self.nrec += 1
        if self.nrec > self.limit:
            return
        idx = len(self.ops[eng])
        deps = set()
        for d in self._deps(reads, writes):
            war = d[0] == 'war'
            if war:
                d = d[1:]
            if d[0] == 'eng' and d[1] == eng:
                if war or eng not in self.same_engine_sync:
                    continue
            deps.add(d)
        self.ops[eng].append({'fn': fn, 'deps': deps, 'dma': None})
        self.last_real[eng] = idx
        self._commit(('eng', eng, idx), reads, writes)

    def dma(self, eng, fn, reads=(), writes=(), chan=None):
        assert chan is not None
        self.nrec += 1
        if self.nrec > self.limit:
            return
        deps = set()
        for d in self._deps(reads, writes):
            if d[0] == 'war':
                d = d[1:]
            deps.add(d)
        n = self.chan_count.get(chan, 0) + 1
        self.chan_count[chan] = n
        self.ops[eng].append({'fn': fn, 'deps': deps, 'dma': (chan, n)})
        self._commit(('dma', chan, n), reads, writes)

    def barrier(self):
        toks = set()
        for e in self.ENGS:
            if e in self.last_real:
                toks.add(('eng', e, self.last_real[e]))
        for c, n in self.chan_count.items():
            toks.add(('dma', c, n))
        for e in self.ENGS:
            deps = set(t for t in toks if not (t[0] == 'eng' and t[1] == e))
            self.ops[e].append({'fn': None, 'deps': deps, 'dma': None})

    def emit(self):
        nc = self.nc
        sig = {e: set() for e in self.ENGS}
        for e in self.ENGS:
            for o in self.ops[e]:
                for d in o['deps']:
                    if d[0] == 'eng':
                        sig[d[1]].add(d[2])
        sigcount = {}
        for e in self.ENGS:
            c = 0
            m = {}
            for i, o in enumerate(self.ops[e]):
                if i in sig[e]:
                    assert o['fn'] is not None and o['dma'] is None
                    c += 1
                    m[i] = c
            sigcount[e] = m
        chans = sorted(self.chan_count.keys(), key=str)
        with contextlib.ExitStack() as es:
            esem = {e: es.enter_context(nc.semaphore("s_" + e)) for e in self.ENGS}
            csem = {c: es.enter_context(nc.semaphore("c%d" % i)) for i, c in enumerate(chans)}
            block = es.enter_context(nc.Block())
            engobj = {'pe': block.tensor, 'act': block.scalar, 'dve': block.vector,
                      'pool': block.gpsimd, 'sp': block.sync}

            def make(e):
                def body(eng):
                    waited = {}
                    for i, o in enumerate(self.ops[e]):
                        need = {}
                        for d in o['deps']:
                            if d[0] == 'eng':
                                s = esem[d[1]]
                                v = sigcount[d[1]][d[2]]
                            else:
                                s = csem[d[1]]
                                v = 16 * d[2]
                            if v > need.get(s.num, (None, 0))[1]:
                                need[s.num] = (s, v)
                        for key, (s, v) in need.items():
                            if waited.get(key, 0) >= v:
                                continue
                            waited[key] = v
                            eng.wait_ge(s, v)
                        if o['fn'] is None:
                            continue
                        ins = o['fn'](eng)
                        if o['dma'] is not None:
                            ins.then_inc(csem[o['dma'][0]], 16)
                        elif i in sig[e]:
                            ins.then_inc(esem[e], 1)
                    if e == 'sp':
                        for c, n in self.chan_count.items():
                            if waited.get(csem[c].num, 0) < 16 * n:
                                eng.wait_ge(csem[c], 16 * n)
                return body
            for e in self.ENGS:
                engobj[e](make(e))


def build(T):
    nc = bass.Bass("TRN2", target_bir_lowering=False)
    S = Sched(nc)
    gstack = contextlib.ExitStack()
    arena = gstack.enter_context(nc.sbuf_tensor("arena", [128, ARENA_W], F32))
    psum_all = gstack.enter_context(nc.psum_tensor("psum_all", [128, 4096], F32))
    banks_g = [psum_all[:, i * 512:(i + 1) * 512] for i in range(8)]

    def _shape_view(v, shape):
        free = int(np.prod(shape[1:]))
        v = v[:shape[0], :free]
        if len(shape) > 2:
            names = "abcd"[:len(shape) - 1]
            pat = "p (" + " ".join(names) + ") -> p " + " ".join(names)
            kw = {names[i]: int(shape[1 + i]) for i in range(1, len(names))}
            v = v.rearrange(pat, **kw)
        return v

    def mk_alloc():
        st = {'off': 0, 'bank': 0}

        def sb(name, shape, dt=F32):
            esz = 2 if dt == BF16 else 4
            free = int(np.prod(shape[1:]))
            n4 = (free * esz + 3) // 4
            assert st['off'] + n4 <= ARENA_W, (name, st['off'], n4)
            v = arena[:, st['off']:st['off'] + n4]
            st['off'] += n4
            if dt == BF16:
                v = v.bitcast(BF16)
            return _shape_view(v, shape)

        def ps(name, shape, dt=F32):
            esz = 2 if dt == BF16 else 4
            free = int(np.prod(shape[1:]))
            assert free * esz <= 2048 and st['bank'] < 8, name
            v = banks_g[st['bank']][:, :]
            st['bank'] += 1
            if dt == BF16:
                v = v.bitcast(BF16)
            return _shape_view(v, shape)
        return sb, ps

    def din(name, shape, dt=F32):
        return nc.dram_tensor(name, shape, dt, kind="ExternalInput").ap()

    def dout(name, shape):
        return nc.dram_tensor(name, shape, F32, kind="ExternalOutput").ap()

    def dscr(name, shape, dt):
        return nc.dram_tensor(name, shape, dt, kind="Internal").ap()

    x_p = din("x_p", [T, D]); x_s = din("x_s", [DEC, D])
    cache_k = din("cache_k", [PAST, D]); cache_v = din("cache_v", [PAST, D])
    state_wkv = din("state_wkv", [16, 64, 64]); state_shift = din("state_shift", [3200])
    norm_pre = din("norm_pre", [D]); w_in = din("w_in", [D, 8320])
    lam_q1 = din("lam_q1", [64]); lam_k1 = din("lam_k1", [64])
    lam_q2 = din("lam_q2", [64]); lam_k2 = din("lam_k2", [64])
    subln = din("subln", [128]); mu_shift = din("mu_shift", [3200])
    w0 = din("w0", [D]); w_up = din("w_up", [64, D]); a0 = din("a0", [D]); a_up = din("a_up", [64, D])
    k_k = din("k_k", [D]); k_a = din("k_a", [D]); r_k = din("r_k", [D])
    ln_x_w = din("ln_x_w", [D]); ln_x_b = din("ln_x_b", [D])
    w_out = din("w_out", [2048, D]); norm_post = din("norm_post", [D])
    c_ident = din("c_ident", [128, 128]); c_bones = din("c_bones", [128, 128])
    c_masks = din("c_masks", [64, 4, 64])
    c_cs_p = din("c_cs_p", [T, 16]); c_cs_s = din("c_cs_s", [DEC, 16])

    y_p = dout("y_p", [T, D]); y_s = dout("y_s", [DEC, D])
    k_po = dout("k_p", [T, D]); v_po = dout("v_p", [T, D])
    wkv_p = dout("wkv_p", [16, 64, 64]); shift_p = dout("shift_p", [3200])
    k_so = dout("k_s", [DEC, D]); v_so = dout("v_s", [DEC, D])
    wkv_s = dout("wkv_s", [16, 64, 64]); shift_s = dout("shift_s", [3200])

    NKS = PAST + DEC
    seqs = {
        'p': dict(x=x_p, n=T, cs=c_cs_p, k_out=k_po, v_out=v_po, y=y_p, nkeys=T, koff=0,
                  wkv_out=wkv_p, shift_out=shift_p),
        's': dict(x=x_s, n=DEC, cs=c_cs_s, k_out=k_so, v_out=v_so, y=y_s, nkeys=NKS, koff=PAST,
                  wkv_out=wkv_s, shift_out=shift_s),
    }
    for sn, sq in seqs.items():
        sq['qT'] = dscr("qT_" + sn, [8, 128, sq['n']], BF16)
        sq['kT'] = dscr("kT_" + sn, [8, 128, sq['nkeys']], BF16)
        sq['vb'] = dscr("vb_" + sn, [sq['nkeys'], D], BF16)
        sq['ga'] = dscr("ga_" + sn, [sq['n'], D], F32)
        sq['psT'] = dscr("psT_" + sn, [33, 128, sq['n']], F32)
        sq['mixT'] = dscr("mixT_" + sn, [16, 128, sq['n']], BF16)

    def tiles_of(sn):
        n = seqs[sn]['n']
        out = []
        t0 = 0
        while t0 < n:
            nt = min(128, n - t0)
            out.append((sn, t0, nt))
            t0 += nt
        return out
    all_tiles = tiles_of('p') + tiles_of('s')

    def OP(eng, meth, reads, writes, *a, **kw):
        S.op(eng, lambda e: getattr(e, meth)(*a, **kw), reads=reads, writes=writes)

    def DMA(eng, chan, reads, writes, out, in_, slow=False):
        if slow:
            S.dma(eng, lambda e: e.dma_start(out=out, in_=in_, allow_slow_non_contiguous=True),
                  reads=reads, writes=writes, chan=chan)
        else:
            S.dma(eng, lambda e: e.dma_start(out=out, in_=in_), reads=reads, writes=writes, chan=chan)

    def rsqrt_col(dst, src, scale, eps, key_r, key_w, n):
        OP('act', 'activation', key_r, key_w, out=dst[:n], in_=src[:n], func=AF.Ln, scale=scale, bias=eps)
        OP('act', 'activation', key_w, key_w, out=dst[:n], in_=dst[:n], func=AF.Exp, scale=-0.5)

    def phase1(which):
        ncols = 4096 if which == 'A' else 4224
        col0 = 0 if which == 'A' else 4096
        if True:
            sb, ps = mk_alloc()
            wres = sb("wres", [128, 8, ncols], BF16)
            wst = [sb("wst%d" % i, [128, 1024]) for i in range(2)]
            gpre = sb("gpre", [128, D])
            idf = sb("idf", [128, 128]); idb = sb("idb", [128, 128], BF16)
            xt = [sb("xt%d" % i, [128, D]) for i in range(2)]
            junk = sb("junk", [128, D], BF16)
            ss = sb("ss", [128, 1]); rstd = sb("rstd", [128, 1])
            hb = sb("hb", [128, D], BF16)
            hT = sb("hT", [128, 8, 128], BF16)
            pT = ps("pT", [128, 8, 128], BF16)
            DMA('sp', 'gpre', [], ['gpre'], gpre[:], norm_pre.partition_broadcast(128))
            DMA('sp', 'idf', [], ['idf'], idf[:], c_ident)
            OP('dve', 'tensor_copy', ['idf'], ['idb'], out=idb[:], in_=idf[:])
            i = 0
            for kc in range(8):
                for c0 in range(0, ncols, 1024):
                    cw = min(1024, ncols - c0)
                    b = i % 2
                    DMA('sp' if b == 0 else 'pool', 'wst%d' % b, [], ['wst%d' % b], wst[b][:, :cw],
                        w_in[kc * 128:(kc + 1) * 128, col0 + c0:col0 + c0 + cw])
                    OP('dve' if b == 0 else 'pool', 'tensor_copy', ['wst%d' % b], ['wres'],
                       out=wres[:, kc, c0:c0 + cw], in_=wst[b][:, :cw])
                    i += 1
            if which == 'A':
                pp = [ps("pp%d" % i, [128, 512]) for i in range(3)]
                pT2 = ps("pT2", [128, 8, 128], BF16)
                q_sb = sb("q_sb", [128, D]); k_sb = sb("k_sb", [128, D]); v_sb = sb("v_sb", [128, D])
                ga_sb = sb("ga_sb", [128, D]); gtmp = sb("gtmp", [128, 512])
                qb = sb("qb", [128, D], BF16); kb = sb("kb", [128, D], BF16); vbt = sb("vbt", [128, D], BF16)
                qTs = sb("qTs", [128, 8, 128], BF16); kTs = sb("kTs", [128, 8, 128], BF16)
                cs = sb("cs", [128, 16]); rt = sb("rt", [128, 4, 16, 8])
                ck = sb("ck", [128, D])
            else:
                pq = [ps("pq%d" % i, [128, 4, 128]) for i in range(2)]
                stg = sb("stg", [128, 33, 128])

            def norm_T(sn, t0, nt, it):
                sq = seqs[sn]
                b = it % 2
                xk = 'xt%d' % b
                DMA('sp', xk, [], [xk], xt[b][:nt], sq['x'][t0:t0 + nt, :])
                OP('act', 'activation', [xk], ['junk', 'ss'], out=junk[:nt], in_=xt[b][:nt], func=AF.Square,
                   accum_out=ss[:nt])
                rsqrt_col(rstd, ss, 1.0 / D, NORM_EPS, ['ss'], ['rstd'], nt)
                OP('dve', 'scalar_tensor_tensor', [xk, 'rstd', 'gpre'], ['hb'], out=hb[:nt], in0=xt[b][:nt],
                   scalar=rstd[:nt, 0:1], in1=gpre[:nt], op0=ALU.mult, op1=ALU.mult)
                for c in range(8):
                    OP('pe', 'transpose', ['hb', 'idb'], ['pT'], out=pT[:, c, :nt],
                       in_=hb[:nt, c * 128:(c + 1) * 128], identity=idb[:nt, :nt])
                OP('act', 'copy', ['pT'], ['hT'], out=hT[:, :, :nt], in_=pT[:, :, :nt])

            def transposes_to(src_b, srckey, pst, stage, stagekey, nt):
                for h in range(8):
                    OP('pe', 'transpose', [srckey, 'idb'], ['pT2'], out=pst[:, h, :nt],
                       in_=src_b[:nt, h * 128:(h + 1) * 128], identity=idb[:nt, :nt])
                OP('act', 'copy', ['pT2'], [stagekey], out=stage[:, :, :nt], in_=pst[:, :, :nt])

            it = 0
            if which == 'A':
                sq = seqs['s']
                for t0 in range(0, PAST, 128):
                    DMA('sp', 'ck', [], ['ck'], ck[:], cache_k[t0:t0 + 128, :])
                    OP('pool', 'tensor_copy', ['ck'], ['kb'], out=kb[:], in_=ck[:])
                    transposes_to(kb, 'kb', pT2, kTs, 'kTs', 128)
                    DMA('sp', 'kTs_st', ['kTs'], [], sq['kT'][:, :, t0:t0 + 128].rearrange("h p t -> p h t"), kTs[:])
                    DMA('sp', 'ck', [], ['ck'], ck[:], cache_v[t0:t0 + 128, :])
                    OP('pool', 'tensor_copy', ['ck'], ['vbt'], out=vbt[:], in_=ck[:])
                    DMA('sp', 'vbt_st', ['vbt'], [], sq['vb'][t0:t0 + 128, :], vbt[:])
            for (sn, t0, nt) in all_tiles:
                sq = seqs[sn]
                norm_T(sn, t0, nt, it)
                if which == 'A':
                    DMA('pool', 'cs', [], ['cs'], cs[:nt], sq['cs'][t0:t0 + nt, :])
                    for grp in range(8):
                        pk = 'pp%d' % (grp % 3)
                        P = pp[grp % 3]
                        for kc in range(8):
                            OP('pe', 'matmul', ['hT', 'wres'], [pk], P[:nt, :], lhsT=hT[:, kc, :nt],
                               rhs=wres[:, kc, grp * 512:(grp + 1) * 512], start=(kc == 0), stop=(kc == 7))
                        half = (grp % 2) * 512
                        if grp < 6:
                            dst, dk = [(q_sb, 'q_sb'), (k_sb, 'k_sb'), (v_sb, 'v_sb')][grp // 2]
                            OP('act', 'copy', [pk], [dk], out=dst[:nt, half:half + 512], in_=P[:nt, :])
                        else:
                            OP('act', 'activation', [pk], ['gtmp'], out=gtmp[:nt], in_=P[:nt, :], func=AF.Exp, scale=-1.0)
                            OP('dve', 'tensor_scalar_add', ['gtmp'], ['gtmp'], out=gtmp[:nt], in0=gtmp[:nt], scalar1=1.0)
                            OP('dve', 'reciprocal', ['gtmp'], ['gtmp'], out=gtmp[:nt], in_=gtmp[:nt])
                            OP('dve', 'tensor_tensor', ['gtmp', pk], ['ga_sb'], out=ga_sb[:nt, half:half + 512],
                               in0=P[:nt, :], in1=gtmp[:nt], op=ALU.mult)
                        if grp in (1, 3):
                            X, xk = (q_sb, 'q_sb') if grp == 1 else (k_sb, 'k_sb')
                            Xv = X[:nt].rearrange("p (a d) -> p a d", d=64)
                            x1 = Xv[:, :, 0:8]; x2 = Xv[:, :, 8:16]
                            cosb = cs[:nt, 0:8].unsqueeze(1).to_broadcast([nt, 16, 8])
                            sinb = cs[:nt, 8:16].unsqueeze(1).to_broadcast([nt, 16, 8])
                            e = 'dve'
                            OP(e, 'tensor_tensor', [xk, 'cs'], ['rt'], out=rt[:nt, 0], in0=x1, in1=cosb, op=ALU.mult)
                            OP(e, 'tensor_tensor', [xk, 'cs'], ['rt'], out=rt[:nt, 1], in0=x2, in1=sinb, op=ALU.mult)
                            OP(e, 'tensor_tensor', [xk, 'cs'], ['rt'], out=rt[:nt, 2], in0=x2, in1=cosb, op=ALU.mult)
                            OP(e, 'tensor_tensor', [xk, 'cs'], ['rt'], out=rt[:nt, 3], in0=x1, in1=sinb, op=ALU.mult)
                            OP(e, 'tensor_tensor', ['rt'], [xk], out=x1, in0=rt[:nt, 0], in1=rt[:nt, 1], op=ALU.subtract)
                            OP(e, 'tensor_tensor', ['rt'], [xk], out=x2, in0=rt[:nt, 2], in1=rt[:nt, 3], op=ALU.add)
                            if grp == 1:
                                OP('pool', 'tensor_copy', ['q_sb'], ['qb'], out=qb[:nt], in_=q_sb[:nt])
                                transposes_to(qb, 'qb', pT2, qTs, 'qTs', nt)
                                DMA('sp', 'qTs_st', ['qTs'], [], sq['qT'][:, :, t0:t0 + nt].rearrange("h p t -> p h t"),
                                    qTs[:, :, :nt])
                            else:
                                DMA('sp', 'k_st', ['k_sb'], [], sq['k_out'][t0:t0 + nt, :], k_sb[:nt])
                                OP('pool', 'tensor_copy', ['k_sb'], ['kb'], out=kb[:nt], in_=k_sb[:nt])
                                transposes_to(kb, 'kb', pT2, kTs, 'kTs', nt)
                                ko = sq['koff'] + t0
                                DMA('sp', 'kTs_st', ['kTs'], [], sq['kT'][:, :, ko:ko + nt].rearrange("h p t -> p h t"),
                                    kTs[:, :, :nt])
                        if grp == 5:
                            DMA('sp', 'v_st', ['v_sb'], [], sq['v_out'][t0:t0 + nt, :], v_sb[:nt])
                            OP('pool', 'tensor_copy', ['v_sb'], ['vbt'], out=vbt[:nt], in_=v_sb[:nt])
                            ko = sq['koff'] + t0
                            DMA('sp', 'vbt_st', ['vbt'], [], sq['vb'][ko:ko + nt, :], vbt[:nt])
                        if grp == 7:
                            DMA('sp', 'ga_st', ['ga_sb'], [], sq['ga'][t0:t0 + nt, :], ga_sb[:nt])
                else:
                    for cq in range(9):
                        pk = 'pq%d' % (cq % 2)
                        P = pq[cq % 2]
                        ncq = min(4, 33 - cq * 4)
                        for j in range(ncq):
                            cc = cq * 4 + j
                            for kc in range(8):
                                OP('pe', 'matmul', ['hT', 'wres'], [pk], P[:, j, :nt], lhsT=wres[:, kc, cc * 128:(cc + 1) * 128],
                                   rhs=hT[:, kc, :nt], start=(kc == 0), stop=(kc == 7))
                        OP('act' if cq % 2 == 0 else 'dve', 'copy' if cq % 2 == 0 else 'tensor_copy', [pk], ['stg%d' % cq],
                           out=stg[:, cq * 4:cq * 4 + ncq, :nt], in_=P[:, :ncq, :nt])
                        DMA('sp', 'stg_st%d' % cq, ['stg%d' % cq], [], sq['psT'][cq * 4:cq * 4 + ncq, :, t0:t0 + nt].rearrange("c p t -> p c t"),
                            stg[:, cq * 4:cq * 4 + ncq, :nt])
                it += 1
        S.barrier()

    def phase2():
        if True:
            sb, ps = mk_alloc()
            NKT = (T + 127) // 128
            NKTS = (NKS + 127) // 128
            MAXKT = max(NKT, NKTS)
            MAXK = max(T, NKS)
            kT = [sb("kT%d" % i, [128, MAXK], BF16) for i in range(2)]
            qT = [sb("qT%d" % i, [128, T], BF16) for i in range(2)]
            V = [sb("V%d" % i, [128, MAXKT, 132], BF16) for i in range(2)]
            pt = [sb("pt%d" % i, [128, 2, 128], BF16) for i in range(3)]
            pss = [psum_all[:, (2 * i) * 512:(2 * i + 2) * 512].rearrange("p (c x) -> p c x", c=2)[:, :, 0:128] for i in range(2)]
            ps("r0", [128, 512]); ps("r1", [128, 512]); ps("r2", [128, 512]); ps("r3", [128, 512])
            acc = [[ps("acc%d%d" % (i, c), [128, 132]) for c in range(2)] for i in range(1)]
            pty = ps("pty", [128, 4, 128], BF16)
            idf = sb("idf", [128, 128]); idb = sb("idb", [128, 128], BF16)
            lamt = sb("lamt", [128, 4, 64]); lamp = sb("lamp", [128, 2, 64]); lams = sb("lams", [128, 2])
            neglam = sb("neglam", [128, 1])
            sublnb = sb("sublnb", [128, 128])
            gat = sb("gat", [128, 4, 128])
            rs_ = [sb("rs%d" % i, [128, 2]) for i in range(2)]; nl2_ = [sb("nl2%d" % i, [128, 1]) for i in range(2)]
            o1_ = [sb("o1%d" % i, [128, 128]) for i in range(2)]; o_ = [sb("o%d" % i, [128, 128]) for i in range(2)]
            junk_ = [sb("junk%d" % i, [128, 128]) for i in range(2)]
            ssq_ = [sb("ssq%d" % i, [128, 1]) for i in range(2)]; rstd_ = [sb("rstd%d" % i, [128, 1]) for i in range(2)]
            yb_ = [sb("yb%d" % i, [128, 128], BF16) for i in range(2)]
            pending = []
            epc = [0]

            def run_due(p):
                keep = []
                for it_ in pending:
                    if it_[1] <= p:
                        it_[2]()
                    else:
                        keep.append(it_)
                pending[:] = keep

            def flush_ep(kmax):
                keep = []
                for it_ in pending:
                    if it_[0] <= kmax:
                        it_[2]()
                    else:
                        keep.append(it_)
                pending[:] = keep
            ystage = sb("ystage", [128, 512], BF16)

            DMA('sp', 'idf', [], ['idf'], idf[:], c_ident)
            OP('dve', 'tensor_copy', ['idf'], ['idb'], out=idb[:], in_=idf[:])
            for i, l in enumerate([lam_q1, lam_k1, lam_q2, lam_k2]):
                DMA('sp', 'lamt', [], ['lamt'], lamt[:, i, :], l.partition_broadcast(128))
            DMA('sp', 'sublnb', [], ['sublnb'], sublnb[:], subln.partition_broadcast(128))
            OP('dve', 'tensor_scalar_mul', ['sublnb'], ['sublnb'], out=sublnb[:], in0=sublnb[:], scalar1=1.0 - LAMBDA_INIT)
            OP('dve', 'tensor_tensor', ['lamt'], ['lamp'], out=lamp[:, 0, :], in0=lamt[:, 0, :], in1=lamt[:, 1, :], op=ALU.mult)
            OP('dve', 'tensor_tensor', ['lamt'], ['lamp'], out=lamp[:, 1, :], in0=lamt[:, 2, :], in1=lamt[:, 3, :], op=ALU.mult)
            OP('dve', 'tensor_reduce', ['lamp'], ['lams'], out=lams[:], in_=lamp[:], axis=AX.X, op=ALU.add)
            OP('act', 'activation', ['lams'], ['lams'], out=lams[:], in_=lams[:], func=AF.Exp)
            OP('dve', 'tensor_tensor', ['lams'], ['neglam'], out=neglam[:], in0=lams[:, 1:2], in1=lams[:, 0:1], op=ALU.subtract)
            OP('dve', 'tensor_scalar_add', ['neglam'], ['neglam'], out=neglam[:], in0=neglam[:], scalar1=-LAMBDA_INIT)
            for i in range(2):
                OP('pool', 'memset', [], ['V%d' % i], V[i][:, :, 128:132], 1.0)

            jobs = [(sn, h) for sn in ('p', 's') for h in range(8)]
            sidx = 0
            qidx = 0
            for ji, (sn, h) in enumerate(jobs):
                sq = seqs[sn]
                b = ji % 2
                nkeys = sq['nkeys']; nq = sq['n']
                nkt = (nkeys + 127) // 128
                kk_, qk_, vk_ = 'kT%d' % b, 'qT%d' % b, 'V%d' % b
                DMA('sp', kk_, [], [kk_], kT[b][:, :nkeys], sq['kT'][h])
                DMA('sp', qk_, [], [qk_], qT[b][:, :nq], sq['qT'][h])
                nfull = nkeys // 128
                for k0 in range(0, nfull, 8):
                    k1 = min(nfull, k0 + 8)
                    DMA('pool', vk_, [], [vk_], V[b][:, k0:k1, 0:128],
                        sq['vb'][k0 * 128:k1 * 128, h * 128:(h + 1) * 128].rearrange("(kt p) d -> p kt d", p=128))
                if nkeys % 128:
                    r = nkeys % 128
                    DMA('pool', vk_, [], [vk_], V[b][:r, nfull, 0:128], sq['vb'][nfull * 128:nkeys, h * 128:(h + 1) * 128])
                nqt = (nq + 127) // 128
                pairs = []
                for qt in range(nqt):
                    jmax = qt if sn == 'p' else nkt - 1
                    for j in range(jmax + 1):
                        pairs.append((qt, j, jmax))

                def emit_qk(pidx):
                    qt, j, jmax = pairs[pidx]
                    q0 = qt * 128
                    nqs = min(128, nq - q0)
                    nk = min(128, nkeys - j * 128)
                    si = (sidx0 + pidx) % 2
                    for c in range(2):
                        OP('pe', 'matmul', [kk_, qk_], ['pss%d' % si], pss[si][:nk, c, :nqs],
                           lhsT=kT[b][c * 64:(c + 1) * 64, j * 128:j * 128 + nk],
                           rhs=qT[b][c * 64:(c + 1) * 64, q0:q0 + nqs], start=True, stop=True)
                sidx0 = sidx
                sidx += len(pairs)
                emit_qk(0)
                for pidx, (qt, j, jmax) in enumerate(pairs):
                    q0 = qt * 128
                    nqs = min(128, nq - q0)
                    nk = min(128, nkeys - j * 128)
                    ab = 0
                    ak = ['acc%d%d' % (ab, c) for c in range(2)]
                    si = (sidx0 + pidx) % 2
                    pi = (sidx0 + pidx) % 3
                    sk = 'pss%d' % si; pk = 'pt%d' % pi
                    if pidx + 1 < len(pairs):
                        emit_qk(pidx + 1)
                    OP('act', 'activation', [sk], [pk], out=pt[pi][:nk, :, :nqs], in_=pss[si][:nk, :, :nqs], func=AF.Exp, scale=0.125)
                    if sn == 'p' and j == qt:
                        OP('pool', 'memset', [], [pk], pt[pi][64:128, :, 0:64], 0.0)
                    for c in range(2):
                        OP('pe', 'matmul', [pk, vk_], [ak[c]], acc[ab][c][:nqs, 0:130],
                           lhsT=pt[pi][:nk, c, :nqs], rhs=V[b][:nk, j, 0:130], start=(j == 0), stop=(j == jmax))
                    run_due(pidx)
                    if j != jmax:
                        continue
                    ek = epc[0]
                    epc[0] += 1
                    flush_ep(ek - 2)
                    e2 = ek % 2
                    A1 = acc[ab][0][:nqs, :]; A2 = acc[ab][1][:nqs, :]
                    qi = qt % 4
                    if qi == 0:
                        nqg = min(512, nq - q0)
                        nsub = (nqg + 127) // 128
                        pp_ = min(128, nqg)
                        DMA('sp', 'gat', [], ['gat'], gat[:pp_, :nsub, :],
                            sq['ga'][q0:q0 + nqg, h * 128:(h + 1) * 128].rearrange("(s p) d -> p s d", p=pp_))
                        g0 = q0
                    rs = rs_[e2]; nl2 = nl2_[e2]; o1 = o1_[e2]; o = o_[e2]; junk = junk_[e2]; ssq = ssq_[e2]; rstd = rstd_[e2]; yb = yb_[e2]
                    kr, kn, ko1, ko, kj, ks_, krs, kyb = ['%s%d' % (n_, e2) for n_ in ('rs', 'nl2', 'o1', 'o', 'junk', 'ssq', 'rstd', 'yb')]
                    OP('dve', 'tensor_copy', [ak[0]], [kr], out=rs[:nqs, 0:1], in_=A1[:, 128:129])
                    OP('dve', 'tensor_copy', [ak[1], kr], [kr], out=rs[:nqs, 1:2], in_=A2[:, 128:129])
                    OP('dve', 'reciprocal', [kr], [kr], out=rs[:nqs, :], in_=rs[:nqs, :])
                    OP('dve', 'tensor_tensor', [kr, 'neglam'], [kn], out=nl2[:nqs], in0=rs[:nqs, 1:2], in1=neglam[:nqs], op=ALU.mult)
                    OP('dve', 'tensor_scalar_mul', [ak[0], kr], [ko1], out=o1[:nqs], in0=A1[:, 0:128], scalar1=rs[:nqs, 0:1])
                    OP('dve', 'scalar_tensor_tensor', [ak[1], kn, ko1], [ko], out=o[:nqs], in0=A2[:, 0:128],
                       scalar=nl2[:nqs, 0:1], in1=o1[:nqs], op0=ALU.mult, op1=ALU.add)

                    def st2(nqs=nqs, o=o, junk=junk, ssq=ssq, rstd=rstd, ko=ko, kj=kj, ks_=ks_, krs=krs):
                        OP('act', 'activation', [ko], [kj, ks_], out=junk[:nqs], in_=o[:nqs], func=AF.Square, accum_out=ssq[:nqs])
                        rsqrt_col(rstd, ssq, 1.0 / 128, SUBLN_EPS, [ks_], [krs], nqs)

                    def st3(nqs=nqs, o=o, rstd=rstd, yb=yb, ko=ko, krs=krs, kyb=kyb, qi=qi):
                        OP('dve', 'scalar_tensor_tensor', [ko, krs, 'sublnb'], [ko], out=o[:nqs], in0=o[:nqs], scalar=rstd[:nqs, 0:1],
                           in1=sublnb[:nqs], op0=ALU.mult, op1=ALU.mult)
                        OP('dve', 'tensor_tensor', [ko, 'gat'], [kyb], out=yb[:nqs], in0=o[:nqs], in1=gat[:nqs, qi, :], op=ALU.mult)

                    def st4(nqs=nqs, yb=yb, kyb=kyb, qi=qi):
                        OP('pe', 'transpose', [kyb, 'idb'], ['pty'], out=pty[:, qi, :nqs], in_=yb[:nqs, :], identity=idb[:nqs, :nqs])

                    def st5(nqs=nqs, qi=qi, qt=qt, q0=q0, g0=g0, h=h, sq=sq, nqt=nqt):
                        OP('act', 'copy', ['pty'], ['ystage'], out=ystage[:, qi * 128:qi * 128 + nqs], in_=pty[:, qi, :nqs])
                        if qi == 3 or qt == nqt - 1:
                            ng = q0 + nqs - g0
                            DMA('sp', 'ystage_st', ['ystage'], [], sq['mixT'][h, :, g0:g0 + ng], ystage[:, :ng])
                    pending.append((ek, pidx + 2, st2))
                    pending.append((ek, pidx + 4, st3))
                    pending.append((ek, pidx + 6, st4))
                    pending.append((ek, pidx + 7, st5))
                flush_ep(10 ** 9)
        S.barrier()

    def phase3():
        if True:
            sb, ps = mk_alloc()
            banks = banks_g

            idf = sb("idf", [128, 128]); bones = sb("bones", [128, 128])
            masks = sb("masks", [64, 4, 64])
            cvec = sb("cvec", [128, 11, 8])
            muwa = sb("muwa", [128, 1])
            LR = sb("LR", [128, D])
            rmask = sb("rmask", [128, D])
            psb = [sb("psb%d" % i, [128, 33, 129]) for i in range(2)]
            tmp = sb("tmp", [128, D]); tmp2 = sb("tmp2", [128, D])
            m_r = sb("m_r", [128, D]); m_k = sb("m_k", [128, D]); m_v = sb("m_v", [128, D])
            m_wa = sb("m_wa", [128, 128]); th = sb("th", [128, 128])
            ewt = sb("ewt", [128, D]); Lt = sb("Lt", [128, D]); E1 = sb("E1", [128, D])
            kk = sb("kk", [128, D]); alpha = sb("alpha", [128, D]); kmod = sb("kmod", [128, D])
            bonusT = sb("bonusT", [128, D]); sgB = sb("sgB", [128, D])
            pc = sb("pc", [128, 8, 2])
            AR = sb("AR", [128, 8, 2, 2, 64]); BK = sb("BK", [128, 8, 2, 2, 64])
            Vtm = sb("Vtm", [64, D]); Btm = sb("Btm", [64, D]); Ktm = sb("Ktm", [64, D])
            XX = [sb("XX%d" % i, [64, 4, 2, 64]) for i in range(2)]
            TT = sb("TT", [64, 4, 64])
            ARBT = sb("ARBT", [64, 4, 64]); AKT = sb("AKT", [64, 4, 64]); ARKT = sb("ARKT", [64, 4, 64])
            Wsb = sb("Wsb", [64, 4, 64]); Usb = sb("Usb", [64, 4, 64])
            H = sb("H", [128, 8, 64])
            Ytm = sb("Ytm", [64, 16, 64]); Ysq = sb("Ysq", [64, 16, 64])
            gst = sb("gst", [64, 4, 16])
            yo = sb("yo", [128, 8, 64]); yob = sb("yob", [128, 8, 64], BF16)
            Ssb = sb("Ssb", [64, 16, 64])

            def v3(t, nt, g0=0, g1=8):
                return t[:, g0 * nt:g1 * nt].rearrange("p (g t) -> p g t", t=nt)

            def cbc(i, nt):
                return cvec[:, i, :].unsqueeze(2).to_broadcast([128, 8, nt])

            DMA('sp', 'idf', [], ['idf'], idf[:], c_ident)
            DMA('sp', 'bones', [], ['bones'], bones[:], c_bones)
            DMA('sp', 'masks', [], ['masks'], masks[:], c_masks)
            for i in range(3):
                DMA('sp', 'cvec', [], ['cvec'], cvec[:, i, :], mu_shift[i * 1024:(i + 1) * 1024].rearrange("(g p) -> p g", p=128), slow=True)
            for i, vec in enumerate([w0, a0, k_k, k_a, r_k, ln_x_w, ln_x_b]):
                DMA('sp', 'cvec', [], ['cvec'], cvec[:, 3 + i, :], vec.rearrange("(g p) -> p g", p=128), slow=True)
            DMA('sp', 'muwa', [], ['muwa'], muwa[:], mu_shift[3072:3200].rearrange("(p o) -> p o", o=1), slow=True)
            DMA('sp', 'LR', [], ['LR'], LR[0:64, :], w_up)
            DMA('sp', 'LR', [], ['LR'], LR[64:128, :], a_up)
            OP('pool', 'memset', [], ['rmask'], rmask[:], 1.0)
            OP('pool', 'memset', ['rmask'], ['rmask'], rmask[:].rearrange("p (a b) -> p a b", b=64)[:, :, 0:1], 0.0)
            CI = dict(mu_r=0, mu_k=1, mu_v=2, w0=3, a0=4, k_k=5, k_a=6, r_k=7, lnw=8, lnb=9)
            m_cr_lt = masks[:, 0, :]; m_rc_lt = masks[:, 1, :]; m_rc_le = masks[:, 2, :]; id64 = masks[:, 3, :]

            def bc4(m):
                return m.unsqueeze(1).to_broadcast([64, 4, 64])

            it = 0
            for sn in ('p', 's'):
                sq = seqs[sn]
                n = sq['n']
                for c0 in range(0, 25, 5):
                    DMA('pool', 'shift_st', [], [], sq['shift_out'][c0 * 128:(c0 + 5) * 128].rearrange("(c p o) -> c p o", p=128, o=1),
                        sq['psT'][c0:c0 + 5, :, n - 1:n], slow=True)
                if sn == 'p':
                    OP('pool', 'memset', [], ['H0', 'H1', 'H2', 'H3'], H[:], 0.0)
                else:
                    DMA('sp', 'Ssb', [], ['Ssb'], Ssb[:], state_wkv.rearrange("h i j -> i h j"))
                    for g in range(8):
                        OP('pe', 'transpose', ['Ssb', 'idf'], ['bk0'], out=banks[0][:, g * 64:(g + 1) * 64],
                           in_=Ssb[:, 2 * g:2 * g + 2, :].rearrange("p a t -> p (a t)"), identity=idf[:64, :64])
                    OP('dve', 'tensor_copy', ['bk0'], ['H0', 'H1', 'H2', 'H3'], out=H[:], in_=banks[0][:, :].rearrange("p (g i) -> p g i", i=64))
                for (_, t0, nt) in tiles_of(sn):
                    b = it % 2
                    it += 1
                    pk = 'psb%d' % b
                    P = psb[b]
                    nch = nt // 64
                    for c0 in range(0, 33, 4):
                        c1 = min(33, c0 + 4)
                        if t0 == 0:
                            DMA('sp', pk, [], [pk], P[:, c0:c1, 1:1 + nt], sq['psT'][c0:c1, :, 0:nt].rearrange("c p t -> p c t"))
                        else:
                            DMA('sp', pk, [], [pk], P[:, c0:c1, 0:1 + nt], sq['psT'][c0:c1, :, t0 - 1:t0 + nt].rearrange("c p t -> p c t"))
                    if t0 == 0:
                        if sn == 'p':
                            OP('pool', 'memset', [], [pk], P[:, :, 0:1], 0.0)
                        else:
                            for c0 in range(0, 25, 5):
                                DMA('sp', pk, [], [pk], P[:, c0:c0 + 5, 0:1], state_shift[c0 * 128:(c0 + 5) * 128].rearrange("(c p o) -> p c o", p=128, o=1), slow=True)
                    for X, (M, mk) in enumerate([(m_r, 'm_r'), (m_k, 'm_k'), (m_v, 'm_v')]):
                        cur = P[:, 8 * X:8 * X + 8, 1:1 + nt]; prev = P[:, 8 * X:8 * X + 8, 0:nt]
                        e = 'pool' if X == 1 else 'dve'
                        OP(e, 'tensor_tensor', [pk], [mk], out=v3(M, nt), in0=prev, in1=cur, op=ALU.subtract)
                        OP(e, 'tensor_tensor', [mk, 'cvec'], [mk], out=v3(M, nt), in0=v3(M, nt), in1=cbc(X, nt), op=ALU.mult)
                        OP(e, 'tensor_tensor', [mk, pk], [mk], out=v3(M, nt), in0=v3(M, nt), in1=cur, op=ALU.add)
                    OP('dve', 'tensor_tensor', [pk], ['m_wa'], out=m_wa[:, :nt], in0=P[:, 24, 0:nt], in1=P[:, 24, 1:1 + nt], op=ALU.subtract)
                    OP('dve', 'scalar_tensor_tensor', ['m_wa', 'muwa', pk], ['m_wa'], out=m_wa[:, :nt], in0=m_wa[:, :nt], scalar=muwa[:, 0:1],
                       in1=P[:, 24, 1:1 + nt], op0=ALU.mult, op1=ALU.add)
                    OP('act', 'activation', [pk], ['tmp2'], out=v3(tmp2, nt), in_=P[:, 25:33, 1:1 + nt], func=AF.Exp, scale=-1.0)
                    OP('pool', 'tensor_scalar_add', ['tmp2'], ['tmp2'], out=tmp2[:, :8 * nt], in0=tmp2[:, :8 * nt], scalar1=1.0)
                    OP('dve', 'reciprocal', ['tmp2'], ['tmp2'], out=tmp2[:, :8 * nt], in_=tmp2[:, :8 * nt])
                    OP('pool', 'tensor_tensor', ['tmp2', pk], ['sgB'], out=v3(sgB, nt), in0=v3(tmp2, nt), in1=P[:, 25:33, 1:1 + nt], op=ALU.mult)
                    OP('act', 'activation', ['m_wa'], ['th'], out=th[0:64, :nt], in_=m_wa[0:64, :nt], func=AF.Exp, scale=-2.0)
                    OP('dve', 'tensor_scalar_add', ['th'], ['th'], out=th[0:64, :nt], in0=th[0:64, :nt], scalar1=1.0)
                    OP('dve', 'reciprocal', ['th'], ['th'], out=th[0:64, :nt], in_=th[0:64, :nt])
                    OP('dve', 'tensor_scalar', ['th'], ['th'], out=th[0:64, :nt], in0=th[0:64, :nt], scalar1=2.0, scalar2=-1.0, op0=ALU.mult, op1=ALU.add)
                    pw = [banks[0], banks[1]]

                    def pwv(nt):
                        return [banks[0][:, :4 * nt].rearrange("p (g t) -> p g t", t=nt), banks[1][:, :4 * nt].rearrange("p (g t) -> p g t", t=nt)]
                    pv = pwv(nt)
                    for g in range(8):
                        OP('pe', 'matmul', ['LR', 'th'], ['bk%d' % (g // 4)], pv[g // 4][:, g % 4, :], lhsT=LR[0:64, g * 128:(g + 1) * 128],
                           rhs=th[0:64, :nt], start=True, stop=True)
                    for hf in range(2):
                        OP('dve', 'tensor_tensor', ['bk%d' % hf, 'cvec'], ['ewt'], out=v3(ewt, nt, 4 * hf, 4 * hf + 4), in0=pv[hf],
                           in1=cvec[:, CI['w0'], 4 * hf:4 * hf + 4].unsqueeze(2).to_broadcast([128, 4, nt]), op=ALU.add)
                    for g in range(8):
                        OP('pe', 'matmul', ['LR', 'm_wa'], ['bk%d' % (g // 4)], pv[g // 4][:, g % 4, :], lhsT=LR[64:128, g * 128:(g + 1) * 128],
                           rhs=m_wa[64:128, :nt], start=True, stop=True)
                    for hf in range(2):
                        OP('dve', 'tensor_tensor', ['bk%d' % hf, 'cvec'], ['alpha'], out=v3(alpha, nt, 4 * hf, 4 * hf + 4), in0=pv[hf],
                           in1=cvec[:, CI['a0'], 4 * hf:4 * hf + 4].unsqueeze(2).to_broadcast([128, 4, nt]), op=ALU.add)
                    W8 = 8 * nt
                    OP('act', 'activation', ['ewt'], ['ewt'], out=ewt[:, :W8], in_=ewt[:, :W8], func=AF.Exp, scale=-1.0)
                    OP('act', 'activation', ['ewt'], ['ewt'], out=ewt[:, :W8], in_=ewt[:, :W8], func=AF.Ln, bias=1.0)
                    OP('act', 'activation', ['ewt'], ['ewt'], out=ewt[:, :W8], in_=ewt[:, :W8], func=AF.Exp, scale=-1.0, bias=-0.5)
                    OP('act', 'activation', ['alpha'], ['alpha'], out=alpha[:, :W8], in_=alpha[:, :W8], func=AF.Exp, scale=-1.0)
                    OP('pool', 'tensor_scalar_add', ['alpha'], ['alpha'], out=alpha[:, :W8], in0=alpha[:, :W8], scalar1=1.0)
                    OP('dve', 'reciprocal', ['alpha'], ['alpha'], out=alpha[:, :W8], in_=alpha[:, :W8])
                    OP('pool', 'tensor_scalar_mul', ['ewt'], ['tmp'], out=tmp[:, :W8], in0=ewt[:, :W8], scalar1=-1.0)
                    OP('dve', 'tensor_tensor_scan', ['tmp', 'rmask'], ['Lt'], out=Lt[:, :W8], data0=rmask[:, :W8], data1=tmp[:, :W8], initial=0.0,
                       op0=ALU.mult, op1=ALU.add)
                    OP('dve', 'tensor_tensor', ['m_k', 'cvec'], ['kk'], out=v3(kk, nt), in0=v3(m_k, nt), in1=cbc(CI['k_k'], nt), op=ALU.mult)
                    OP('pool', 'tensor_tensor', ['kk'], ['tmp'], out=tmp[:, :W8], in0=kk[:, :W8], in1=kk[:, :W8], op=ALU.mult)
                    for hf in range(2):
                        OP('pe', 'matmul', ['bones', 'tmp'], ['bk%d' % hf], banks[hf][:, :4 * nt], lhsT=bones[:], rhs=tmp[:, hf * 4 * nt:(hf + 1) * 4 * nt],
                           start=True, stop=True)
                        OP('dve', 'tensor_scalar_max', ['bk%d' % hf], ['tmp2'], out=tmp2[:, hf * 4 * nt:(hf + 1) * 4 * nt], in0=banks[hf][:, :4 * nt], scalar1=1e-24)
                    OP('act', 'activation', ['tmp2'], ['tmp2'], out=tmp2[:, :W8], in_=tmp2[:, :W8], func=AF.Ln)
                    OP('act', 'activation', ['tmp2'], ['tmp2'], out=tmp2[:, :W8], in_=tmp2[:, :W8], func=AF.Exp, scale=-0.5)
                    OP('dve', 'tensor_tensor', ['kk', 'tmp2'], ['kk'], out=kk[:, :W8], in0=kk[:, :W8], in1=tmp2[:, :W8], op=ALU.mult)
                    OP('dve', 'scalar_tensor_tensor', ['alpha', 'cvec'], ['kmod'], out=v3(kmod, nt), in0=v3(alpha, nt), scalar=-1.0, in1=cbc(CI['k_a'], nt),
                       op0=ALU.add, op1=ALU.mult)
                    OP('dve', 'scalar_tensor_tensor', ['kmod', 'm_k'], ['kmod'], out=kmod[:, :W8], in0=kmod[:, :W8], scalar=1.0, in1=m_k[:, :W8],
                       op0=ALU.add, op1=ALU.mult)
                    OP('pool', 'tensor_tensor', ['m_r', 'kmod'], ['tmp'], out=tmp[:, :W8], in0=m_r[:, :W8], in1=kmod[:, :W8], op=ALU.mult)
                    OP('pool', 'tensor_tensor', ['tmp', 'cvec'], ['tmp'], out=v3(tmp, nt), in0=v3(tmp, nt), in1=cbc(CI['r_k'], nt), op=ALU.mult)
                    for hf in range(2):
                        OP('pe', 'matmul', ['bones', 'tmp'], ['bk%d' % hf], banks[hf][:, :4 * nt], lhsT=bones[:], rhs=tmp[:, hf * 4 * nt:(hf + 1) * 4 * nt],
                           start=True, stop=True)
                        OP('dve', 'tensor_tensor', ['bk%d' % hf, 'm_v'], ['bonusT'], out=bonusT[:, hf * 4 * nt:(hf + 1) * 4 * nt], in0=banks[hf][:, :4 * nt],
                           in1=m_v[:, hf * 4 * nt:(hf + 1) * 4 * nt], op=ALU.mult)
                    def v4(t):
                        return t[:, :W8].rearrange("p (g c t) -> p g c t", g=8, t=64)
                    OP('act', 'activation', ['Lt'], ['E1'], out=E1[:, :W8], in_=Lt[:, :W8], func=AF.Exp)
                    OP('dve', 'tensor_tensor', ['m_r', 'E1'], ['AR'], out=AR[:, :, :nch, 1, :], in0=v4(m_r), in1=v4(E1), op=ALU.mult)
                    OP('pool', 'tensor_copy', ['E1'], ['pc'], out=pc[:, :, :nch], in_=v4(E1)[:, :, :, 63])
                    OP('dve', 'tensor_tensor', ['Lt', 'ewt'], ['ewt'], out=ewt[:, :W8], in0=Lt[:, :W8], in1=ewt[:, :W8], op=ALU.add)
                    OP('act', 'activation', ['ewt'], ['ewt'], out=ewt[:, :W8], in_=ewt[:, :W8], func=AF.Exp)
                    OP('dve', 'scalar_tensor_tensor', ['kk', 'ewt'], ['AR'], out=AR[:, :, :nch, 0, :], in0=v4(kk), scalar=-1.0, in1=v4(ewt),
                       op0=ALU.mult, op1=ALU.mult)
                    OP('act', 'activation', ['Lt'], ['Lt'], out=Lt[:, :W8], in_=Lt[:, :W8], func=AF.Exp, scale=-1.0)
                    OP('pool', 'tensor_tensor', ['kk', 'alpha'], ['tmp'], out=tmp[:, :W8], in0=kk[:, :W8], in1=alpha[:, :W8], op=ALU.mult)
                    OP('dve', 'tensor_tensor', ['tmp', 'Lt'], ['BK'], out=BK[:, :, :nch, 0, :], in0=v4(tmp), in1=v4(Lt), op=ALU.mult)
                    OP('dve', 'tensor_tensor', ['kmod', 'Lt'], ['BK'], out=BK[:, :, :nch, 1, :], in0=v4(kmod), in1=v4(Lt), op=ALU.mult)

                    for ch in range(nch):
                        for (src, skey, dst, dkey) in [(None, 'm_v', Vtm, 'Vtm'), (0, 'BK', Btm, 'Btm'), (1, 'BK', Ktm, 'Ktm')]:
                            for g in range(8):
                                if src is None:
                                    in_ = v3(m_v, nt)[:, g, ch * 64:(ch + 1) * 64]
                                else:
                                    in_ = BK[:, g, ch, src, :]
                                OP('pe', 'transpose', [skey, 'idf'], ['bk%d' % (2 + g // 4)], out=banks[2 + g // 4][0:64, (g % 4) * 128:(g % 4 + 1) * 128],
                                   in_=in_, identity=idf[:])
                            OP('act', 'copy', ['bk2'], [dkey], out=dst[:, 0:512], in_=banks[2][0:64, :])
                            OP('dve', 'tensor_copy', ['bk3'], [dkey], out=dst[:, 512:1024], in_=banks[3][0:64, :])
                        Ytm4 = Ytm[:].rearrange("p (g a) t -> p g a t", a=2)
                        for hb in range(4):
                            gq = hb // 2; par = hb % 2
                            heads = [2 * (4 * gq + i) + par for i in range(4)]
                            hp = par * 64
                            sl = slice(hp, hp + 64)
                            pA = banks[6][0:64, 0:256].rearrange("p (h s) -> p h s", s=64)
                            pTT = banks[6][0:64, 256:512].rearrange("p (h s) -> p h s", s=64)
                            pB = banks[4][0:64, :].rearrange("p (h s) -> p h s", s=128)
                            pC = banks[5][0:64, :].rearrange("p (h s) -> p h s", s=128)
                            pX = banks[7][0:64, :].rearrange("p (h c s) -> p h c s", c=2, s=64)
                            for i, h in enumerate(heads):
                                g = h // 2
                                OP('pe', 'matmul', ['AR', 'BK'], ['pA'], pA[:, i, :], lhsT=AR[sl, g, ch, 0, :], rhs=BK[sl, g, ch, 0, :], start=True, stop=True)
                            for i, h in enumerate(heads):
                                g = h // 2
                                OP('pe', 'matmul', ['AR', 'BK'], ['bk4', 'bk4u'], pB[:, i, :], lhsT=BK[sl, g, ch, 0, :],
                                   rhs=AR[sl, g, ch, :, :].rearrange("p a t -> p (a t)"), start=True, stop=True)
                            for i, h in enumerate(heads):
                                g = h // 2
                                OP('pe', 'matmul', ['AR', 'BK'], ['bk5'], pC[:, i, :], lhsT=BK[sl, g, ch, 1, :],
                                   rhs=AR[sl, g, ch, :, :].rearrange("p a t -> p (a t)"), start=True, stop=True)
                            OP('dve', 'tensor_tensor', ['pA', 'masks'], ['XX0'], out=XX[0][:, :, 0, :], in0=pA, in1=bc4(m_cr_lt), op=ALU.mult)
                            OP('dve', 'tensor_tensor', ['bk4', 'masks'], ['XX0'], out=XX[0][:, :, 1, :], in0=pB[:, :, 0:64], in1=bc4(m_rc_lt), op=ALU.mult)
                            OP('dve', 'tensor_tensor', ['bk4', 'masks'], ['ARBT'], out=ARBT[:], in0=pB[:, :, 64:128], in1=bc4(m_rc_le), op=ALU.mult)
                            OP('dve', 'tensor_tensor', ['bk5', 'masks'], ['AKT'], out=AKT[:], in0=pC[:, :, 0:64], in1=bc4(m_rc_lt), op=ALU.mult)
                            OP('dve', 'tensor_tensor', ['bk5', 'masks'], ['ARKT'], out=ARKT[:], in0=pC[:, :, 64:128], in1=bc4(m_rc_le), op=ALU.mult)
                            OP('dve', 'tensor_tensor', ['XX0', 'masks'], ['TT'], out=TT[:], in0=XX[0][:, :, 1, :], in1=bc4(id64), op=ALU.add)
                            for k in range(1, 6):
                                xo = XX[(k - 1) % 2]; xn = XX[k % 2]
                                xok = 'XX%d' % ((k - 1) % 2); xnk = 'XX%d' % (k % 2)
                                for i in range(4):
                                    OP('pe', 'matmul', [xok], ['bk7'], pX[:, i, 0, :], lhsT=xo[:, i, 1, :], rhs=xo[:, i, 0, :], start=True, stop=True)
                                    if k < 5:
                                        OP('pe', 'matmul', [xok], ['bk7'], pX[:, i, 1, :], lhsT=xo[:, i, 0, :], rhs=xo[:, i, 1, :], start=True, stop=True)
                                if k < 5:
                                    OP('act', 'copy', ['bk7'], [xnk], out=xn[:], in_=pX)
                                else:
                                    OP('act', 'copy', ['bk7'], [xnk], out=xn[:, :, 0, :], in_=pX[:, :, 0, :])
                                for i in range(4):
                                    OP('pe', 'matmul', [xnk, 'TT'], ['pTT'], pTT[:, i, :], lhsT=xn[:, i, 0, :], rhs=TT[:, i, :], start=True, stop=True)
                                OP('dve', 'tensor_tensor', ['pTT', 'TT'], ['TT'], out=TT[:], in0=pTT, in1=TT[:], op=ALU.add)
                            hk = 'H%d' % hb
                            pW1 = banks[4][0:64, 0:256].rearrange("p (h s) -> p h s", s=64)
                            pU = banks[4][0:64, 256:512].rearrange("p (h s) -> p h s", s=64)
                            pW2 = banks[5][0:64, 256:512].rearrange("p (h s) -> p h s", s=64)
                            pH = banks[5][:, 0:256].rearrange("p (h s) -> p h s", s=64)
                            pY1 = banks[7][0:64, 0:256].rearrange("p (h s) -> p h s", s=64)
                            pY2 = banks[2 + hb // 2][0:64, (hb % 2) * 256:(hb % 2 + 1) * 256].rearrange("p (h s) -> p h s", s=64)
                            yk = 'bk%d' % (2 + hb // 2)
                            for i, h in enumerate(heads):
                                g = h // 2
                                OP('pe', 'matmul', ['AR', hk], ['bk4'], pW1[:, i, :], lhsT=AR[sl, g, ch, 0, :], rhs=H[sl, g, :], start=True, stop=True)
                            for i, h in enumerate(heads):
                                OP('pe', 'matmul', ['AKT', 'Vtm'], ['bk5'], pW2[:, i, :], lhsT=AKT[:, i, :], rhs=Vtm[:, h * 64:(h + 1) * 64], start=True, stop=True)
                            OP('act', 'copy', ['bk4'], ['Wsb'], out=Wsb[:], in_=pW1)
                            OP('dve', 'tensor_tensor', ['bk5', 'Wsb'], ['Wsb'], out=Wsb[:], in0=pW2, in1=Wsb[:], op=ALU.add)
                            for i, h in enumerate(heads):
                                OP('pe', 'matmul', ['TT', 'Wsb'], ['bk4u'], pU[:, i, :], lhsT=TT[:, i, :], rhs=Wsb[:, i, :], start=True, stop=True)
                            OP('act', 'copy', ['bk4u'], ['Usb'], out=Usb[:], in_=pU)
                            for i, h in enumerate(heads):
                                g = h // 2
                                OP('pe', 'matmul', ['AR', hk], ['bk7'], pY1[:, i, :], lhsT=AR[sl, g, ch, 1, :], rhs=H[sl, g, :], start=True, stop=True)
                            for i, h in enumerate(heads):
                                OP('pe', 'matmul', ['ARBT', 'Usb'], [yk], pY2[:, i, :], lhsT=ARBT[:, i, :], rhs=Usb[:, i, :], start=True, stop=False)
                                OP('pe', 'matmul', ['ARKT', 'Vtm'], [yk], pY2[:, i, :], lhsT=ARKT[:, i, :], rhs=Vtm[:, h * 64:(h + 1) * 64], start=False, stop=True)
                            OP('act', 'copy', ['bk7'], ['Ysq'], out=Ysq[:, 0:4, :], in_=pY1)
                            OP('dve', 'tensor_tensor', [yk, 'Ysq'], ['Ytm'], out=Ytm4[:, 4 * gq:4 * gq + 4, par, :], in0=pY2, in1=Ysq[:, 0:4, :], op=ALU.add)
                            for i, h in enumerate(heads):
                                g = h // 2
                                OP('pe', 'matmul', ['Btm', 'Usb'], ['bk5'], pH[:, i, :], lhsT=Btm[:, g * 128:(g + 1) * 128],
                                   rhs=Usb[:, i, :], start=True, stop=False)
                                OP('pe', 'matmul', ['Ktm', 'Vtm'], ['bk5'], pH[:, i, :], lhsT=Ktm[:, g * 128:(g + 1) * 128],
                                   rhs=Vtm[:, h * 64:(h + 1) * 64], start=False, stop=True)
                            Hs = H[sl, 4 * gq:4 * gq + 4, :]
                            OP('dve', 'tensor_tensor', ['bk5', hk], [hk], out=Hs, in0=pH[sl, :, :], in1=Hs, op=ALU.add)
                            OP('dve', 'tensor_tensor', [hk, 'pc'], [hk], out=Hs, in0=Hs,
                               in1=pc[sl, 4 * gq:4 * gq + 4, ch:ch + 1].to_broadcast([64, 4, 64]), op=ALU.mult)
                        OP('dve', 'tensor_reduce', ['Ytm'], ['gst'], out=gst[:, 0, :], in_=Ytm[:], axis=AX.X, op=ALU.add)
                        OP('pool', 'tensor_tensor', ['Ytm'], ['Ysq'], out=Ysq[:], in0=Ytm[:], in1=Ytm[:], op=ALU.mult)
                        OP('dve', 'tensor_reduce', ['Ysq'], ['gst'], out=gst[:, 1, :], in_=Ysq[:], axis=AX.X, op=ALU.add)
                        OP('dve', 'tensor_scalar_mul', ['gst'], ['gst'], out=gst[:, 0:2, :], in0=gst[:, 0:2, :], scalar1=1.0 / 64)
                        OP('dve', 'tensor_tensor', ['gst'], ['gst'], out=gst[:, 2, :], in0=gst[:, 0, :], in1=gst[:, 0, :], op=ALU.mult)
                        OP('dve', 'tensor_tensor', ['gst'], ['gst'], out=gst[:, 3, :], in0=gst[:, 1, :], in1=gst[:, 2, :], op=ALU.subtract)
                        OP('act', 'activation', ['gst'], ['gst'], out=gst[:, 3, :], in_=gst[:, 3, :], func=AF.Ln, bias=GN_EPS)
                        OP('act', 'activation', ['gst'], ['gst'], out=gst[:, 3, :], in_=gst[:, 3, :], func=AF.Exp, scale=-0.5)
                        OP('dve', 'tensor_tensor', ['Ytm', 'gst'], ['Ytm'], out=Ytm[:], in0=Ytm[:], in1=gst[:, 0, :].unsqueeze(2).to_broadcast([64, 16, 64]), op=ALU.subtract)
                        OP('dve', 'tensor_tensor', ['Ytm', 'gst'], ['Ytm'], out=Ytm[:], in0=Ytm[:], in1=gst[:, 3, :].unsqueeze(2).to_broadcast([64, 16, 64]), op=ALU.mult)
                        pYT = banks[0][:, :].rearrange("p (g t) -> p g t", t=64)
                        for g in range(8):
                            OP('pe', 'transpose', ['Ytm', 'idf'], ['bk0'], out=pYT[:, g, :], in_=Ytm[:, 2 * g:2 * g + 2, :].rearrange("p a t -> p (a t)"), identity=idf[:64, :64])
                        csl = slice(ch * 64, (ch + 1) * 64)
                        OP('dve', 'tensor_tensor', ['bk0', 'cvec'], ['yo'], out=yo[:], in0=pYT, in1=cbc(CI['lnw'], 64), op=ALU.mult)
                        OP('dve', 'tensor_tensor', ['yo', 'cvec'], ['yo'], out=yo[:], in0=yo[:], in1=cbc(CI['lnb'], 64), op=ALU.add)
                        OP('dve', 'tensor_tensor', ['yo', 'bonusT'], ['yo'], out=yo[:], in0=yo[:], in1=v3(bonusT, nt)[:, :, csl], op=ALU.add)
                        OP('dve', 'tensor_tensor', ['yo', 'sgB'], ['yob'], out=yob[:], in0=yo[:], in1=v3(sgB, nt)[:, :, csl], op=ALU.mult)
                        DMA('sp', 'yob_st', ['yob'], [], sq['mixT'][8:16, :, t0 + ch * 64:t0 + (ch + 1) * 64].rearrange("c p t -> p c t"), yob[:])
                for g in range(8):
                    OP('pe', 'transpose', ['H0', 'H1', 'H2', 'H3', 'idf'], ['bk1'], out=banks[1][0:64, (g % 4) * 128:(g % 4 + 1) * 128], in_=H[:, g, :], identity=idf[:])
                    if g % 4 == 3:
                        OP('dve', 'tensor_copy', ['bk1'], ['Ssb'], out=Ssb[:, (g - 3) * 2:(g + 1) * 2, :], in_=banks[1][0:64, :].rearrange("p (h j) -> p h j", j=64))
                DMA('sp', 'Ssb_st', ['Ssb'], [], sq['wkv_out'].rearrange("h i j -> i h j"), Ssb[:])
        S.barrier()

    def phase4():
        if True:
            sb, ps = mk_alloc()
            wo = sb("wo", [128, 16, D], BF16)
            wst = [sb("wst%d" % i, [128, D]) for i in range(2)]
            gpost = sb("gpost", [128, D])
            xt = [sb("xt%d" % i, [128, D]) for i in range(2)]
            mx = [sb("mx%d" % i, [128, 16, 128], BF16) for i in range(2)]
            po = [[ps("po%d%d" % (i, j), [128, 512]) for j in range(2)] for i in range(2)]
            junk = sb("junk", [128, 512]); ss = sb("ss", [128, 2]); rstd = sb("rstd", [128, 1])
            yt = [sb("yt%d" % i, [128, D]) for i in range(2)]
            DMA('sp', 'gpost', [], ['gpost'], gpost[:], norm_post.partition_broadcast(128))
            for kc in range(16):
                b = kc % 2
                DMA('sp' if b == 0 else 'pool', 'wst%d' % b, [], ['wst%d' % b], wst[b][:], w_out[kc * 128:(kc + 1) * 128, :])
                OP('dve' if b == 0 else 'pool', 'tensor_copy', ['wst%d' % b], ['wo'], out=wo[:, kc, :], in_=wst[b][:])
            for it, (sn, t0, nt) in enumerate(all_tiles):
                sq = seqs[sn]
                b = it % 2
                DMA('sp', 'xt%d' % b, [], ['xt%d' % b], xt[b][:nt], sq['x'][t0:t0 + nt, :])
                for c0 in (0, 8):
                    DMA('pool', 'mx%d' % b, [], ['mx%d' % b], mx[b][:, c0:c0 + 8, :nt], sq['mixT'][c0:c0 + 8, :, t0:t0 + nt].rearrange("c p t -> p c t"))
                for hf in range(2):
                    for kc in range(16):
                        OP('pe', 'matmul', ['mx%d' % b, 'wo'], ['po%d%d' % (b, hf)], po[b][hf][:nt, :], lhsT=mx[b][:, kc, :nt],
                           rhs=wo[:, kc, hf * 512:(hf + 1) * 512], start=(kc == 0), stop=(kc == 15))
                    OP('act', 'activation', ['po%d%d' % (b, hf)], ['junk', 'ss'], out=junk[:nt], in_=po[b][hf][:nt, :], func=AF.Square,
                       accum_out=ss[:nt, hf:hf + 1])
                OP('dve', 'tensor_tensor', ['ss'], ['ss'], out=ss[:nt, 0:1], in0=ss[:nt, 0:1], in1=ss[:nt, 1:2], op=ALU.add)
                rsqrt_col(rstd, ss[:, 0:1], 1.0 / D, NORM_EPS, ['ss'], ['rstd'], nt)
                for hf in range(2):
                    OP('dve', 'scalar_tensor_tensor', ['po%d%d' % (b, hf), 'rstd', 'gpost'], ['yt%d' % b], out=yt[b][:nt, hf * 512:(hf + 1) * 512],
                       in0=po[b][hf][:nt, :], scalar=rstd[:nt, 0:1], in1=gpost[:nt, hf * 512:(hf + 1) * 512], op0=ALU.mult, op1=ALU.mult)
                OP('pool', 'tensor_tensor', ['yt%d' % b, 'xt%d' % b], ['yt%d' % b], out=yt[b][:nt], in0=yt[b][:nt], in1=xt[b][:nt], op=ALU.add)
                DMA('sp', 'yt%d_st' % b, ['yt%d' % b], [], sq['y'][t0:t0 + nt, :], yt[b][:nt])

    import os
    ph = os.environ.get('KPH', '1A,1B,2,3,4').split(',')
    if '1A' in ph:
        phase1('A')
    if '1B' in ph:
        phase1('B')
    if '2' in ph:
        phase2()
    if '3' in ph:
        phase3()
    if '4' in ph:
        phase4()
    print('NREC', S.nrec, flush=True)
    S.emit()
    gstack.close()
    return nc


def _consts(T):
    ident = np.eye(128, dtype=np.float32)
    bones = np.zeros((128, 128), np.float32)
    bones[:64, :64] = 1.0
    bones[64:, 64:] = 1.0
    r = np.arange(64)[:, None]; c = np.arange(64)[None, :]
    masks = np.stack([(c < r), (r < c), (r <= c), (r == c)], axis=1).astype(np.float32)

    def cs(pos):
        inv = np.power(np.float32(500000.0), -np.arange(8, dtype=np.float32) * np.float32(2.0 / 16)).astype(np.float32)
        ang = pos.astype(np.float32)[:, None] * inv[None, :]
        return np.concatenate([np.cos(ang), np.sin(ang)], axis=1).astype(np.float32)
    return dict(c_ident=ident, c_bones=bones, c_masks=np.ascontiguousarray(masks),
                c_cs_p=cs(np.arange(T)), c_cs_s=cs(PAST + np.arange(DEC)))


_NC_CACHE = {}


def kernel(x_prompt, x_sample, cache_k, cache_v, state_wkv, state_shift, norm_pre, w_in,
           lam_q1, lam_k1, lam_q2, lam_k2, subln, mu_shift, w0, w_up, a0, a_up, k_k, k_a,
           r_k, ln_x_w, ln_x_b, w_out, norm_post):
    f = lambda a: np.ascontiguousarray(np.asarray(a, dtype=np.float32))
    x_prompt = f(x_prompt); x_sample = f(x_sample)
    B, T, _ = x_prompt.shape
    if T not in _NC_CACHE:
        _NC_CACHE[T] = build(T)
    nc = _NC_CACHE[T]
    consts = _consts(T)
    shared = dict(norm_pre=f(norm_pre)[0], w_in=f(w_in)[0], lam_q1=f(lam_q1)[0], lam_k1=f(lam_k1)[0],
                  lam_q2=f(lam_q2)[0], lam_k2=f(lam_k2)[0], subln=f(subln)[0], mu_shift=f(mu_shift)[0],
                  w0=f(w0)[0], w_up=f(w_up)[0], a0=f(a0)[0], a_up=f(a_up)[0], k_k=f(k_k)[0], k_a=f(k_a)[0],
                  r_k=f(r_k)[0].reshape(-1), ln_x_w=f(ln_x_w)[0], ln_x_b=f(ln_x_b)[0], w_out=f(w_out)[0],
                  norm_post=f(norm_post)[0])
    shared.update(consts)
    cache_k = f(cache_k); cache_v = f(cache_v); state_wkv = f(state_wkv); state_shift = f(state_shift)
    in_maps = []
    for c in range(B):
        m = dict(shared)
        m.update(x_p=x_prompt[c], x_s=x_sample[c], cache_k=cache_k[0, c].reshape(PAST, D),
                 cache_v=cache_v[0, c].reshape(PAST, D), state_wkv=state_wkv[0, c],
                 state_shift=state_shift[0, c, 0])
        in_maps.append(m)
    res = run_bass_kernel_spmd(nc, in_maps, core_ids=list(range(B)))
    R = res.results
    st = lambda k: np.stack([np.asarray(r[k], dtype=np.float32) for r in R], axis=0)
    y_p = st('y_p'); y_s = st('y_s')
    k_p = st('k_p').reshape(1, B, T, 8, 2, 64); v_p = st('v_p').reshape(1, B, T, 8, 128)
    wkv_p = st('wkv_p').reshape(1, B, 16, 64, 64); shift_p = st('shift_p').reshape(1, B, 1, 3200)
    k_s = st('k_s').reshape(1, B, DEC, 8, 2, 64); v_s = st('v_s').reshape(1, B, DEC, 8, 128)
    wkv_s = st('wkv_s').reshape(1, B, 16, 64, 64); shift_s = st('shift_s').reshape(1, B, 1, 3200)
    return (y_p, y_s, k_p, v_p, wkv_p, shift_p, k_s, v_s, wkv_s, shift_s)
```

```python
import contextlib
import os
import numpy as np
import concourse.bass as bass
import concourse.mybir as mybir
from concourse.bass_utils import run_bass_kernel_spmd

F32 = mybir.dt.float32
BF16 = mybir.dt.bfloat16
AF = mybir.ActivationFunctionType
ALU = mybir.AluOpType
AX = mybir.AxisListType

ARENA_W = 45056
D = 1024
SEQ = 8192
PAST = 1024
DEC = 64
NCORES = 8
LAMBDA_INIT = 0.2
NORM_EPS = 1e-6
SUBLN_EPS = 1e-5
GN_EPS = 64e-5


class Sched:
    ENGS = ['pe', 'act', 'dve', 'pool', 'sp']

    def __init__(self, nc, same_engine_sync=('act', 'dve', 'pool')):
        self.nc = nc
        self.ops = {e: [] for e in self.ENGS}
        self.res = {}
        self.chan_count = {}
        self.same_engine_sync = set(same_engine_sync)
        self.last_real = {}
        self.nrec = 0
        self.limit = int(os.environ.get('KMAXOPS', '1000000000'))

    def _deps(self, reads, writes):
        deps = set()
        for k in reads:
            st = self.res.get(k)
            if st and st['w'] is not None:
                deps.add(st['w'])
        for k in writes:
            st = self.res.get(k)
            if st:
                if st['w'] is not None:
                    deps.add(st['w'])
                for r in st['r']:
                    deps.add(('war',) + r)
        return deps

    def _commit(self, tok, reads, writes):
        for k in writes:
            self.res[k] = {'w': tok, 'r': []}
        for k in reads:
            st = self.res.setdefault(k, {'w': None, 'r': []})
            st['r'].append(tok)

    def op(self, eng, fn, reads=(), writes=()):
        self.nrec += 1
        if self.nrec > self.limit:
            return
        idx = len(self.ops[eng])
        deps = set()
        for d in self._deps(reads, writes):
            war = d[0] == 'war'
            if war:
                d = d[1:]
            if d[0] == 'eng' and d[1] == eng:
                if war or eng not in self.same_engine_sync:
                    continue
            deps.add(d)
        self.ops[eng].append({'fn': fn, 'deps': deps, 'dma': None})
        self.last_real[eng] = idx
        self._commit(('eng', eng, idx), reads, writes)

    def dma(self, eng, fn, reads=(), writes=(), chan=None):
        assert chan is not None
        self.nrec += 1
        if self.nrec > self.limit:
            return
        deps = set()
        for d in self._deps(reads, writes):
            if d[0] == 'war':
                d = d[1:]
            deps.add(d)
        n = self.chan_count.get(chan, 0) + 1
        self.chan_count[chan] = n
        self.ops[eng].append({'fn': fn, 'deps': deps, 'dma': (chan, n)})
        self._commit(('dma', chan, n), reads, writes)

    def barrier(self):
        toks = set()
        for e in self.ENGS:
            if e in self.last_real:
                toks.add(('eng', e, self.last_real[e]))
        for c, n in self.chan_count.items():
            toks.add(('dma', c, n))
        for e in self.ENGS:
            deps = set(t for t in toks if not (t[0] == 'eng' and t[1] == e))
            self.ops[e].append({'fn': None, 'deps': deps, 'dma': None})

    def emit(self):
        nc = self.nc
        sig = {e: set() for e in self.ENGS}
        for e in self.ENGS:
            for o in self.ops[e]:
                for d in o['deps']:
                    if d[0] == 'eng':
                        sig[d[1]].add(d[2])
        sigcount = {}
        for e in self.ENGS:
            c = 0
            m = {}
            for i, o in enumerate(self.ops[e]):
                if i in sig[e]:
                    assert o['fn'] is not None and o['dma'] is None
                    c += 1
                    m[i] = c
            sigcount[e] = m
        chans = sorted(self.chan_count.keys(), key=str)
        with contextlib.ExitStack() as es:
            esem = {e: es.enter_context(nc.semaphore("s_" + e)) for e in self.ENGS}
            csem = {c: es.enter_context(nc.semaphore("c%d" % i)) for i, c in enumerate(chans)}
            block = es.enter_context(nc.Block())
            engobj = {'pe': block.tensor, 'act': block.scalar, 'dve': block.vector,
                      'pool': block.gpsimd, 'sp': block.sync}

            def make(e):
                def body(eng):
                    waited = {}
                    for i, o in enumerate(self.ops[e]):
                        need = {}
                        for d in o['deps']:
                            if d[0] == 'eng':
                                s = esem[d[1]]
                                v = sigcount[d[1]][d[2]]
                            else:
                                s = csem[d[1]]
                                v = 16 * d[2]
                            if v > need.get(s.num, (None, 0))[1]:
                                need[s.num] = (s, v)
                        for key, (s, v) in need.items():
                            if waited.get(key, 0) >= v:
                                continue
                            waited[key] = v
                            eng.wait_ge(s, v)
                        if o['fn'] is None:
                            continue
                        ins = o['fn'](eng)
                        if o['dma'] is not None:
                            ins.then_inc(csem[o['dma'][0]], 16)
                        elif i in sig[e]:
                            ins.then_inc(esem[e], 1)
                    if e == 'sp':
                        for c, n in self.chan_count.items():
                            if waited.get(csem[c].num, 0) < 16 * n:
                                eng.wait_ge(csem[c], 16 * n)
                return body
            for e in self.ENGS:
                engobj[e](make(e))


def build(T):
    nc = bass.Bass("TRN2", target_bir_lowering=False)
    S = Sched(nc)
    gstack = contextlib.ExitStack()
    arena = gstack.enter_context(nc.sbuf_tensor("arena", [128, ARENA_W], F32))
    psum_all = gstack.enter_context(nc.psum_tensor("psum_all", [128, 4096], F32))
    banks_g = [psum_all[:, i * 512:(i + 1) * 512] for i in range(8)]

    def _shape_view(v, shape):
        free = int(np.prod(shape[1:]))
        v = v[:shape[0], :free]
        if len(shape) > 2:
            names = "abcd"[:len(shape) - 1]
            pat = "p (" + " ".join(names) + ") -> p " + " ".join(names)
            kw = {names[i]: int(shape[1 + i]) for i in range(1, len(names))}
            v = v.rearrange(pat, **kw)
        return v

    def mk_alloc():
        st = {'off': 0, 'bank': 0}

        def sb(name, shape, dt=F32):
            esz = 2 if dt == BF16 else 4
            free = int(np.prod(shape[1:]))
            n4 = (free * esz + 3) // 4
            assert st['off'] + n4 <= ARENA_W, (name, st['off'], n4)
            v = arena[:, st['off']:st['off'] + n4]
            st['off'] += n4
            if dt == BF16:
                v = v.bitcast(BF16)
            return _shape_view(v, shape)

        def ps(name, shape, dt=F32):
            esz = 2 if dt == BF16 else 4
            free = int(np.prod(shape[1:]))
            assert free * esz <= 2048 and st['bank'] < 8, name
            v = banks_g[st['bank']][:, :]
            st['bank'] += 1
            if dt == BF16:
                v = v.bitcast(BF16)
            return _shape_view(v, shape)
        return sb, ps

    def din(name, shape, dt=F32):
        return nc.dram_tensor(name, shape, dt, kind="ExternalInput").ap()

    def dout(name, shape):
        return nc.dram_tensor(name, shape, F32, kind="ExternalOutput").ap()

    def dscr(name, shape, dt):
        return nc.dram_tensor(name, shape, dt, kind="Internal").ap()

    x_p = din("x_p", [T, D]); x_s = din("x_s", [DEC, D])
    cache_k = din("cache_k", [PAST, D]); cache_v = din("cache_v", [PAST, D])
    state_wkv = din("state_wkv", [16, 64, 64]); state_shift = din("state_shift", [3200])
    norm_pre = din("norm_pre", [D]); w_in = din("w_in", [D, 8320])
    lam_q1 = din("lam_q1", [64]); lam_k1 = din("lam_k1", [64])
    lam_q2 = din("lam_q2", [64]); lam_k2 = din("lam_k2", [64])
    subln = din("subln", [128]); mu_shift = din("mu_shift", [3200])
    w0 = din("w0", [D]); w_up = din("w_up", [64, D]); a0 = din("a0", [D]); a_up = din("a_up", [64, D])
    k_k = din("k_k", [D]); k_a = din("k_a", [D]); r_k = din("r_k", [D])
    ln_x_w = din("ln_x_w", [D]); ln_x_b = din("ln_x_b", [D])
    w_out = din("w_out", [2048, D]); norm_post = din("norm_post", [D])
    c_ident = din("c_ident", [128, 128]); c_bones = din("c_bones", [128, 128])
    c_masks = din("c_masks", [64, 4, 64])
    c_cs_p = din("c_cs_p", [T, 16]); c_cs_s = din("c_cs_s", [DEC, 16])

    y_p = dout("y_p", [T, D]); y_s = dout("y_s", [DEC, D])
    k_po = dout("k_p", [T, D]); v_po = dout("v_p", [T, D])
    wkv_p = dout("wkv_p", [16, 64, 64]); shift_p = dout("shift_p", [3200])
    k_so = dout("k_s", [DEC, D]); v_so = dout("v_s", [DEC, D])
    wkv_s = dout("wkv_s", [16, 64, 64]); shift_s = dout("shift_s", [3200])

    NKS = PAST + DEC
    seqs = {
        'p': dict(x=x_p, n=T, cs=c_cs_p, k_out=k_po, v_out=v_po, y=y_p, nkeys=T, koff=0,
                  wkv_out=wkv_p, shift_out=shift_p),
        's': dict(x=x_s, n=DEC, cs=c_cs_s, k_out=k_so, v_out=v_so, y=y_s, nkeys=NKS, koff=PAST,
                  wkv_out=wkv_s, shift_out=shift_s),
    }
    for sn, sq in seqs.items():
        sq['qT'] = dscr("qT_" + sn, [8, 128, sq['n']], BF16)
        sq['kT'] = dscr("kT_" + sn, [8, 128, sq['nkeys']], BF16)
        sq['vb'] = dscr("vb_" + sn, [sq['nkeys'], D], BF16)
        sq['ga'] = dscr("ga_" + sn, [sq['n'], D], F32)
        sq['psT'] = dscr("psT_" + sn, [33, 128, sq['n']], F32)
        sq['mixT'] = dscr("mixT_" + sn, [16, 128, sq['n']], BF16)

    def tiles_of(sn):
        n = seqs[sn]['n']
        out = []
        t0 = 0
        while t0 < n:
            nt = min(128, n - t0)
            out.append((sn, t0, nt))
            t0 += nt
        return out
    all_tiles = tiles_of('p') + tiles_of('s')

    def OP(eng, meth, reads, writes, *a, **kw):
        S.op(eng, lambda e: getattr(e, meth)(*a, **kw), reads=reads, writes=writes)

    def DMA(eng, chan, reads, writes, out, in_, slow=False):
        if slow:
            S.dma(eng, lambda e: e.dma_start(out=out, in_=in_, allow_slow_non_contiguous=True),
                  reads=reads, writes=writes, chan=chan)
        else:
            S.dma(eng, lambda e: e.dma_start(out=out, in_=in_), reads=reads, writes=writes, chan=chan)

    def rsqrt_col(dst, src, scale, eps, key_r, key_w, n):
        OP('act', 'activation', key_r, key_w, out=dst[:n], in_=src[:n], func=AF.Ln, scale=scale, bias=eps)
        OP('act', 'activation', key_w, key_w, out=dst[:n], in_=dst[:n], func=AF.Exp, scale=-0.5)

    def phase1(which):
        ncols = 4096 if which == 'A' else 4224
        col0 = 0 if which == 'A' else 4096
        if True:
            sb, ps = mk_alloc()
            wres = sb("wres", [128, 8, ncols], BF16)
            wst = [sb("wst%d" % i, [128, 1024]) for i in range(2)]
            gpre = sb("gpre", [128, D])
            idf = sb("idf", [128, 128]); idb = sb("idb", [128, 128], BF16)
            xt = [sb("xt%d" % i, [128, D]) for i in range(2)]
            junk = sb("junk", [128, D], BF16)
            ss = sb("ss", [128, 1]); rstd = sb("rstd", [128, 1])
            hb = sb("hb", [128, D], BF16)
            hT = sb("hT", [128, 8, 128], BF16)
            pT = ps("pT", [128, 8, 128], BF16)
            DMA('sp', 'gpre', [], ['gpre'], gpre[:], norm_pre.partition_broadcast(128))
            DMA('sp', 'idf', [], ['idf'], idf[:], c_ident)
            OP('dve', 'tensor_copy', ['idf'], ['idb'], out=idb[:], in_=idf[:])
            i = 0
            for kc in range(8):
                for c0 in range(0, ncols, 1024):
                    cw = min(1024, ncols - c0)
                    b = i % 2
                    DMA('sp' if b == 0 else 'pool', 'wst%d' % b, [], ['wst%d' % b], wst[b][:, :cw],
                        w_in[kc * 128:(kc + 1) * 128, col0 + c0:col0 + c0 + cw])
                    OP('dve' if b == 0 else 'pool', 'tensor_copy', ['wst%d' % b], ['wres'],
                       out=wres[:, kc, c0:c0 + cw], in_=wst[b][:, :cw])
                    i += 1
            if which == 'A':
                pp = [ps("pp%d" % i, [128, 512]) for i in range(3)]
                pT2 = ps("pT2", [128, 8, 128], BF16)
                q_sb = sb("q_sb", [128, D]); k_sb = sb("k_sb", [128, D]); v_sb = sb("v_sb", [128, D])
                ga_sb = sb("ga_sb", [128, D]); gtmp = sb("gtmp", [128, 512])
                qb = sb("qb", [128, D], BF16); kb = sb("kb", [128, D], BF16); vbt = sb("vbt", [128, D], BF16)
                qTs = sb("qTs", [128, 8, 128], BF16); kTs = sb("kTs", [128, 8, 128], BF16)
                cs = sb("cs", [128, 16]); rt = sb("rt", [128, 4, 16, 8])
                ck = sb("ck", [128, D])
            else:
                pq = [ps("pq%d" % i, [128, 4, 128]) for i in range(2)]
                stg = sb("stg", [128, 33, 128])

            def norm_T(sn, t0, nt, it):
                sq = seqs[sn]
                b = it % 2
                xk = 'xt%d' % b
                DMA('sp', xk, [], [xk], xt[b][:nt], sq['x'][t0:t0 + nt, :])
                OP('act', 'activation', [xk], ['junk', 'ss'], out=junk[:nt], in_=xt[b][:nt], func=AF.Square,
                   accum_out=ss[:nt])
                rsqrt_col(rstd, ss, 1.0 / D, NORM_EPS, ['ss'], ['rstd'], nt)
                OP('dve', 'scalar_tensor_tensor', [xk, 'rstd', 'gpre'], ['hb'], out=hb[:nt], in0=xt[b][:nt],
                   scalar=rstd[:nt, 0:1], in1=gpre[:nt], op0=ALU.mult, op1=ALU.mult)
                for c in range(8):
                    OP('pe', 'transpose', ['hb', 'idb'], ['pT'], out=pT[:, c, :nt],
                       in_=hb[:nt, c * 128:(c + 1) * 128], identity=idb[:nt, :nt])
                OP('act', 'copy', ['pT'], ['hT'], out=hT[:, :, :nt], in_=pT[:, :, :nt])

            def transposes_to(src_b, srckey, pst, stage, stagekey, nt):
                for h in range(8):
                    OP('pe', 'transpose', [srckey, 'idb'], ['pT2'], out=pst[:, h, :nt],
                       in_=src_b[:nt, h * 128:(h + 1) * 128], identity=idb[:nt, :nt])
                OP('act', 'copy', ['pT2'], [stagekey], out=stage[:, :, :nt], in_=pst[:, :, :nt])

            it = 0
            if which == 'A':
                sq = seqs['s']
                for t0 in range(0, PAST, 128):
                    DMA('sp', 'ck', [], ['ck'], ck[:], cache_k[t0:t0 + 128, :])
                    OP('pool', 'tensor_copy', ['ck'], ['kb'], out=kb[:], in_=ck[:])
                    transposes_to(kb, 'kb', pT2, kTs, 'kTs', 128)
                    DMA('sp', 'kTs_st', ['kTs'], [], sq['kT'][:, :, t0:t0 + 128].rearrange("h p t -> p h t"), kTs[:])
                    DMA('sp', 'ck', [], ['ck'], ck[:], cache_v[t0:t0 + 128, :])
                    OP('pool', 'tensor_copy', ['ck'], ['vbt'], out=vbt[:], in_=ck[:])
                    DMA('sp', 'vbt_st', ['vbt'], [], sq['vb'][t0:t0 + 128, :], vbt[:])
            for (sn, t0, nt) in all_tiles:
                sq = seqs[sn]
                norm_T(sn, t0, nt, it)
                if which == 'A':
                    DMA('pool', 'cs', [], ['cs'], cs[:nt], sq['cs'][t0:t0 + nt, :])
                    for grp in range(8):
                        pk = 'pp%d' % (grp % 3)
                        P = pp[grp % 3]
                        for kc in range(8):
                            OP('pe', 'matmul', ['hT', 'wres'], [pk], P[:nt, :], lhsT=hT[:, kc, :nt],
                               rhs=wres[:, kc, grp * 512:(grp + 1) * 512], start=(kc == 0), stop=(kc == 7))
                        half = (grp % 2) * 512
                        if grp < 6:
                            dst, dk = [(q_sb, 'q_sb'), (k_sb, 'k_sb'), (v_sb, 'v_sb')][grp // 2]
                            OP('act', 'copy', [pk], [dk], out=dst[:nt, half:half + 512], in_=P[:nt, :])
                        else:
                            OP('act', 'activation', [pk], ['gtmp'], out=gtmp[:nt], in_=P[:nt, :], func=AF.Exp, scale=-1.0)
                            OP('dve', 'tensor_scalar_add', ['gtmp'], ['gtmp'], out=gtmp[:nt], in0=gtmp[:nt], scalar1=1.0)
                            OP('dve', 'reciprocal', ['gtmp'], ['gtmp'], out=gtmp[:nt], in_=gtmp[:nt])
                            OP('dve', 'tensor_tensor', ['gtmp', pk], ['ga_sb'], out=ga_sb[:nt, half:half + 512],
                               in0=P[:nt, :], in1=gtmp[:nt], op=ALU.mult)
                        if grp in (1, 3):
                            X, xk = (q_sb, 'q_sb') if grp == 1 else (k_sb, 'k_sb')
                            Xv = X[:nt].rearrange("p (a d) -> p a d", d=64)
                            x1 = Xv[:, :, 0:8]; x2 = Xv[:, :, 8:16]
                            cosb = cs[:nt, 0:8].unsqueeze(1).to_broadcast([nt, 16, 8])
                            sinb = cs[:nt, 8:16].unsqueeze(1).to_broadcast([nt, 16, 8])
                            e = 'dve'
                            OP(e, 'tensor_tensor', [xk, 'cs'], ['rt'], out=rt[:nt, 0], in0=x1, in1=cosb, op=ALU.mult)
                            OP(e, 'tensor_tensor', [xk, 'cs'], ['rt'], out=rt[:nt, 1], in0=x2, in1=sinb, op=ALU.mult)
                            OP(e, 'tensor_tensor', [xk, 'cs'], ['rt'], out=rt[:nt, 2], in0=x2, in1=cosb, op=ALU.mult)
                            OP(e, 'tensor_tensor', [xk, 'cs'], ['rt'], out=rt[:nt, 3], in0=x1, in1=sinb, op=ALU.mult)
                            OP(e, 'tensor_tensor', ['rt'], [xk], out=x1, in0=rt[:nt, 0], in1=rt[:nt, 1], op=ALU.subtract)
                            OP(e, 'tensor_tensor', ['rt'], [xk], out=x2, in0=rt[:nt, 2], in1=rt[:nt, 3], op=ALU.add)
                            if grp == 1:
                                OP('pool', 'tensor_copy', ['q_sb'], ['qb'], out=qb[:nt], in_=q_sb[:nt])
                                transposes_to(qb, 'qb', pT2, qTs, 'qTs', nt)
                                DMA('sp', 'qTs_st', ['qTs'], [], sq['qT'][:, :, t0:t0 + nt].rearrange("h p t -> p h t"),
                                    qTs[:, :, :nt])
                            else:
                                DMA('sp', 'k_st', ['k_sb'], [], sq['k_out'][t0:t0 + nt, :], k_sb[:nt])
                                OP('pool', 'tensor_copy', ['k_sb'], ['kb'], out=kb[:nt], in_=k_sb[:nt])
                                transposes_to(kb, 'kb', pT2, kTs, 'kTs', nt)
                                ko = sq['koff'] + t0
                                DMA('sp', 'kTs_st', ['kTs'], [], sq['kT'][:, :, ko:ko + nt].rearrange("h p t -> p h t"),
                                    kTs[:, :, :nt])
                        if grp == 5:
                            DMA('sp', 'v_st', ['v_sb'], [], sq['v_out'][t0:t0 + nt, :], v_sb[:nt])
                            OP('pool', 'tensor_copy', ['v_sb'], ['vbt'], out=vbt[:nt], in_=v_sb[:nt])
                            ko = sq['koff'] + t0
                            DMA('sp', 'vbt_st', ['vbt'], [], sq['vb'][ko:ko + nt, :], vbt[:nt])
                        if grp == 7:
                            DMA('sp', 'ga_st', ['ga_sb'], [], sq['ga'][t0:t0 + nt, :], ga_sb[:nt])
                else:
                    for cq in range(9):
                        pk = 'pq%d' % (cq % 2)
                        P = pq[cq % 2]
                        ncq = min(4, 33 - cq * 4)
                        for j in range(ncq):
                            cc = cq * 4 + j
                            for kc in range(8):
                                OP('pe', 'matmul', ['hT', 'wres'], [pk], P[:, j, :nt], lhsT=wres[:, kc, cc * 128:(cc + 1) * 128],
                                   rhs=hT[:, kc, :nt], start=(kc == 0), stop=(kc == 7))
                        OP('act' if cq % 2 == 0 else 'dve', 'copy' if cq % 2 == 0 else 'tensor_copy', [pk], ['stg%d' % cq],
                           out=stg[:, cq * 4:cq * 4 + ncq, :nt], in_=P[:, :ncq, :nt])
                        DMA('sp', 'stg_st%d' % cq, ['stg%d' % cq], [], sq['psT'][cq * 4:cq * 4 + ncq, :, t0:t0 + nt].rearrange("c p t -> p c t"),
                            stg[:, cq * 4:cq * 4 + ncq, :nt])
                it += 1
        S.barrier()

    def phase2():
        if True:
            sb, ps = mk_alloc()
            NKT = (T + 127) // 128
            NKTS = (NKS + 127) // 128
            MAXKT = max(NKT, NKTS)
            MAXK = max(T, NKS)
            kT = [sb("kT%d" % i, [128, MAXK], BF16) for i in range(2)]
            qT = [sb("qT%d" % i, [128, T], BF16) for i in range(2)]
            V = [sb("V%d" % i, [128, MAXKT, 132], BF16) for i in range(2)]
            pt = [sb("pt%d" % i, [128, 2, 128], BF16) for i in range(3)]
            pss = [psum_all[:, (2 * i) * 512:(2 * i + 2) * 512].rearrange("p (c x) -> p c x", c=2)[:, :, 0:128] for i in range(2)]
            ps("r0", [128, 512]); ps("r1", [128, 512]); ps("r2", [128, 512]); ps("r3", [128, 512])
            acc = [[ps("acc%d%d" % (i, c), [128, 132]) for c in range(2)] for i in range(1)]
            pty = ps("pty", [128, 4, 128], BF16)
            idf = sb("idf", [128, 128]); idb = sb("idb", [128, 128], BF16)
            lamt = sb("lamt", [128, 4, 64]); lamp = sb("lamp", [128, 2, 64]); lams = sb("lams", [128, 2])
            neglam = sb("neglam", [128, 1])
            sublnb = sb("sublnb", [128, 128])
            gat = sb("gat", [128, 4, 128])
            rs_ = [sb("rs%d" % i, [128, 2]) for i in range(2)]; nl2_ = [sb("nl2%d" % i, [128, 1]) for i in range(2)]
            o1_ = [sb("o1%d" % i, [128, 128]) for i in range(2)]; o_ = [sb("o%d" % i, [128, 128]) for i in range(2)]
            junk_ = [sb("junk%d" % i, [128, 128]) for i in range(2)]
            ssq_ = [sb("ssq%d" % i, [128, 1]) for i in range(2)]; rstd_ = [sb("rstd%d" % i, [128, 1]) for i in range(2)]
            yb_ = [sb("yb%d" % i, [128, 128], BF16) for i in range(2)]
            pending = []
            epc = [0]

            def run_due(p):
                keep = []
                for it_ in pending:
                    if it_[1] <= p:
                        it_[2]()
                    else:
                        keep.append(it_)
                pending[:] = keep

            def flush_ep(kmax):
                keep = []
                for it_ in pending:
                    if it_[0] <= kmax:
                        it_[2]()
                    else:
                        keep.append(it_)
                pending[:] = keep
            ystage = sb("ystage", [128, 512], BF16)

            DMA('sp', 'idf', [], ['idf'], idf[:], c_ident)
            OP('dve', 'tensor_copy', ['idf'], ['idb'], out=idb[:], in_=idf[:])
            for i, l in enumerate([lam_q1, lam_k1, lam_q2, lam_k2]):
                DMA('sp', 'lamt', [], ['lamt'], lamt[:, i, :], l.partition_broadcast(128))
            DMA('sp', 'sublnb', [], ['sublnb'], sublnb[:], subln.partition_broadcast(128))
            OP('dve', 'tensor_scalar_mul', ['sublnb'], ['sublnb'], out=sublnb[:], in0=sublnb[:], scalar1=1.0 - LAMBDA_INIT)
            OP('dve', 'tensor_tensor', ['lamt'], ['lamp'], out=lamp[:, 0, :], in0=lamt[:, 0, :], in1=lamt[:, 1, :], op=ALU.mult)
            OP('dve', 'tensor_tensor', ['lamt'], ['lamp'], out=lamp[:, 1, :], in0=lamt[:, 2, :], in1=lamt[:, 3, :], op=ALU.mult)
            OP('dve', 'tensor_reduce', ['lamp'], ['lams'], out=lams[:], in_=lamp[:], axis=AX.X, op=ALU.add)
            OP('act', 'activation', ['lams'], ['lams'], out=lams[:], in_=lams[:], func=AF.Exp)
            OP('dve', 'tensor_tensor', ['lams'], ['neglam'], out=neglam[:], in0=lams[:, 1:2], in1=lams[:, 0:1], op=ALU.subtract)
            OP('dve', 'tensor_scalar_add', ['neglam'], ['neglam'], out=neglam[:], in0=neglam[:], scalar1=-LAMBDA_INIT)
            for i in range(2):
                OP('pool', 'memset', [], ['V%d' % i], V[i][:, :, 128:132], 1.0)

            jobs = [(sn, h) for sn in ('p', 's') for h in range(8)]
            sidx = 0
            qidx = 0
            for ji, (sn, h) in enumerate(jobs):
                sq = seqs[sn]
                b = ji % 2
                nkeys = sq['nkeys']; nq = sq['n']
                nkt = (nkeys + 127) // 128
                kk_, qk_, vk_ = 'kT%d' % b, 'qT%d' % b, 'V%d' % b
                DMA('sp', kk_, [], [kk_], kT[b][:, :nkeys], sq['kT'][h])
                DMA('sp', qk_, [], [qk_], qT[b][:, :nq], sq['qT'][h])
                nfull = nkeys // 128
                for k0 in range(0, nfull, 8):
                    k1 = min(nfull, k0 + 8)
                    DMA('pool', vk_, [], [vk_], V[b][:, k0:k1, 0:128],
                        sq['vb'][k0 * 128:k1 * 128, h * 128:(h + 1) * 128].rearrange("(kt p) d -> p kt d", p=128))
                if nkeys % 128:
                    r = nkeys % 128
                    DMA('pool', vk_, [], [vk_], V[b][:r, nfull, 0:128], sq['vb'][nfull * 128:nkeys, h * 128:(h + 1) * 128])
                nqt = (nq + 127) // 128
                pairs = []
                for qt in range(nqt):
                    jmax = qt if sn == 'p' else nkt - 1
                    for j in range(jmax + 1):
                        pairs.append((qt, j, jmax))

                def emit_qk(pidx):
                    qt, j, jmax = pairs[pidx]
                    q0 = qt * 128
                    nqs = min(128, nq - q0)
                    nk = min(128, nkeys - j * 128)
                    si = (sidx0 + pidx) % 2
                    for c in range(2):
                        OP('pe', 'matmul', [kk_, qk_], ['pss%d' % si], pss[si][:nk, c, :nqs],
                           lhsT=kT[b][c * 64:(c + 1) * 64, j * 128:j * 128 + nk],
                           rhs=qT[b][c * 64:(c + 1) * 64, q0:q0 + nqs], start=True, stop=True)
                sidx0 = sidx
                sidx += len(pairs)
                emit_qk(0)
                for pidx, (qt, j, jmax) in enumerate(pairs):
                    q0 = qt * 128
                    nqs = min(128, nq - q0)
                    nk = min(128, nkeys - j * 128)
                    ab = 0
                    ak = ['acc%d%d' % (ab, c) for c in range(2)]
                    si = (sidx0 + pidx) % 2
                    pi = (sidx0 + pidx) % 3
                    sk = 'pss%d' % si; pk = 'pt%d' % pi
                    if pidx + 1 < len(pairs):
                        emit_qk(pidx + 1)
                    OP('act', 'activation', [sk], [pk], out=pt[pi][:nk, :, :nqs], in_=pss[si][:nk, :, :nqs], func=AF.Exp, scale=0.125)
                    if sn == 'p' and j == qt:
                        OP('pool', 'memset', [], [pk], pt[pi][64:128, :, 0:64], 0.0)
                    for c in range(2):
                        OP('pe', 'matmul', [pk, vk_], [ak[c]], acc[ab][c][:nqs, 0:130],
                           lhsT=pt[pi][:nk, c, :nqs], rhs=V[b][:nk, j, 0:130], start=(j == 0), stop=(j == jmax))
                    run_due(pidx)
                    if j != jmax:
                        continue
                    ek = epc[0]
                    epc[0] += 1
                    flush_ep(ek - 2)
                    e2 = ek % 2
                    A1 = acc[ab][0][:nqs, :]; A2 = acc[ab][1][:nqs, :]
                    qi = qt % 4
                    if qi == 0:
                        nqg = min(512, nq - q0)
                        nsub = (nqg + 127) // 128
                        pp_ = min(128, nqg)
                        DMA('sp', 'gat', [], ['gat'], gat[:pp_, :nsub, :],
                            sq['ga'][q0:q0 + nqg, h * 128:(h + 1) * 128].rearrange("(s p) d -> p s d", p=pp_))
                        g0 = q0
                    rs = rs_[e2]; nl2 = nl2_[e2]; o1 = o1_[e2]; o = o_[e2]; junk = junk_[e2]; ssq = ssq_[e2]; rstd = rstd_[e2]; yb = yb_[e2]
                    kr, kn, ko1, ko, kj, ks_, krs, kyb = ['%s%d' % (n_, e2) for n_ in ('rs', 'nl2', 'o1', 'o', 'junk', 'ssq', 'rstd', 'yb')]
                    OP('dve', 'tensor_copy', [ak[0]], [kr], out=rs[:nqs, 0:1], in_=A1[:, 128:129])
                    OP('dve', 'tensor_copy', [ak[1], kr], [kr], out=rs[:nqs, 1:2], in_=A2[:, 128:129])
                    OP('dve', 'reciprocal', [kr], [kr], out=rs[:nqs, :], in_=rs[:nqs, :])
                    OP('dve', 'tensor_tensor', [kr, 'neglam'], [kn], out=nl2[:nqs], in0=rs[:nqs, 1:2], in1=neglam[:nqs], op=ALU.mult)
                    OP('dve', 'tensor_scalar_mul', [ak[0], kr], [ko1], out=o1[:nqs], in0=A1[:, 0:128], scalar1=rs[:nqs, 0:1])
                    OP('dve', 'scalar_tensor_tensor', [ak[1], kn, ko1], [ko], out=o[:nqs], in0=A2[:, 0:128],
                       scalar=nl2[:nqs, 0:1], in1=o1[:nqs], op0=ALU.mult, op1=ALU.add)

                    def st2(nqs=nqs, o=o, junk=junk, ssq=ssq, rstd=rstd, ko=ko, kj=kj, ks_=ks_, krs=krs):
                        OP('act', 'activation', [ko], [kj, ks_], out=junk[:nqs], in_=o[:nqs], func=AF.Square, accum_out=ssq[:nqs])
                        rsqrt_col(rstd, ssq, 1.0 / 128, SUBLN_EPS, [ks_], [krs], nqs)

                    def st3(nqs=nqs, o=o, rstd=rstd, yb=yb, ko=ko, krs=krs, kyb=kyb, qi=qi):
                        OP('dve', 'scalar_tensor_tensor', [ko, krs, 'sublnb'], [ko], out=o[:nqs], in0=o[:nqs], scalar=rstd[:nqs, 0:1],
                           in1=sublnb[:nqs], op0=ALU.mult, op1=ALU.mult)
                        OP('dve', 'tensor_tensor', [ko, 'gat'], [kyb], out=yb[:nqs], in0=o[:nqs], in1=gat[:nqs, qi, :], op=ALU.mult)

                    def st4(nqs=nqs, yb=yb, kyb=kyb, qi=qi):
                        OP('pe', 'transpose', [kyb, 'idb'], ['pty'], out=pty[:, qi, :nqs], in_=yb[:nqs, :], identity=idb[:nqs, :nqs])

                    def st5(nqs=nqs, qi=qi, qt=qt, q0=q0, g0=g0, h=h, sq=sq, nqt=nqt):
                        OP('act', 'copy', ['pty'], ['ystage'], out=ystage[:, qi * 128:qi * 128 + nqs], in_=pty[:, qi, :nqs])
                        if qi == 3 or qt == nqt - 1:
                            ng = q0 + nqs - g0
                            DMA('sp', 'ystage_st', ['ystage'], [], sq['mixT'][h, :, g0:g0 + ng], ystage[:, :ng])
                    pending.append((ek, pidx + 2, st2))
                    pending.append((ek, pidx + 4, st3))
                    pending.append((ek, pidx + 6, st4))
                    pending.append((ek, pidx + 7, st5))
                flush_ep(10 ** 9)
        S.barrier()

    def phase3():
        if True:
            sb, ps = mk_alloc()
            banks = banks_g

            idf = sb("idf", [128, 128]); bones = sb("bones", [128, 128])
            masks = sb("masks", [64, 4, 64])
            cvec = sb("cvec", [128, 11, 8])
            muwa = sb("muwa", [128, 1])
            LR = sb("LR", [128, D])
            rmask = sb("rmask", [128, D])
            psb = [sb("psb%d" % i, [128, 33, 129]) for i in range(2)]
            tmp = sb("tmp", [128, D]); tmp2 = sb("tmp2", [128, D])
            m_r = sb("m_r", [128, D]); m_k = sb("m_k", [128, D]); m_v = sb("m_v", [128, D])
            m_wa = sb("m_wa", [128, 128]); th = sb("th", [128, 128])
            ewt = sb("ewt", [128, D]); Lt = sb("Lt", [128, D]); E1 = sb("E1", [128, D])
            kk = sb("kk", [128, D]); alpha = sb("alpha", [128, D]); kmod = sb("kmod", [128, D])
            bonusT = sb("bonusT", [128, D]); sgB = sb("sgB", [128, D])
            pc = sb("pc", [128, 8, 2])
            AR = sb("AR", [128, 8, 2, 2, 64]); BK = sb("BK", [128, 8, 2, 2, 64])
            Vtm = sb("Vtm", [64, D], BF16); Btm = sb("Btm", [64, D], BF16); Ktm = sb("Ktm", [64, D], BF16)
            XX = [sb("XX%d" % i, [64, 8, 2, 64], BF16) for i in range(2)]
            TT = sb("TT", [64, 8, 64], BF16)
            ARBT = sb("ARBT", [64, 8, 64], BF16); AKT = sb("AKT", [64, 8, 64], BF16); ARKT = sb("ARKT", [64, 8, 64], BF16)
            Wsb = sb("Wsb", [64, 8, 64], BF16); Usb = sb("Usb", [64, 8, 64], BF16)
            H = sb("H", [128, 8, 64])
            Ytm = sb("Ytm", [64, 16, 64]); Ysq = sb("Ysq", [64, 16, 64])
            gst = sb("gst", [64, 4, 16])
            yo = sb("yo", [128, 8, 64]); yob = sb("yob", [128, 8, 64], BF16)
            Ssb = sb("Ssb", [64, 16, 64])

            def v3(t, nt, g0=0, g1=8):
                return t[:, g0 * nt:g1 * nt].rearrange("p (g t) -> p g t", t=nt)

            def cbc(i, nt):
                return cvec[:, i, :].unsqueeze(2).to_broadcast([128, 8, nt])

            DMA('sp', 'idf', [], ['idf'], idf[:], c_ident)
            DMA('sp', 'bones', [], ['bones'], bones[:], c_bones)
            DMA('sp', 'masks', [], ['masks'], masks[:], c_masks)
            for i in range(3):
                DMA('sp', 'cvec', [], ['cvec'], cvec[:, i, :], mu_shift[i * 1024:(i + 1) * 1024].rearrange("(g p) -> p g", p=128), slow=True)
            for i, vec in enumerate([w0, a0, k_k, k_a, r_k, ln_x_w, ln_x_b]):
                DMA('sp', 'cvec', [], ['cvec'], cvec[:, 3 + i, :], vec.rearrange("(g p) -> p g", p=128), slow=True)
            DMA('sp', 'muwa', [], ['muwa'], muwa[:], mu_shift[3072:3200].rearrange("(p o) -> p o", o=1), slow=True)
            DMA('sp', 'LR', [], ['LR'], LR[0:64, :], w_up)
            DMA('sp', 'LR', [], ['LR'], LR[64:128, :], a_up)
            OP('pool', 'memset', [], ['rmask'], rmask[:], 1.0)
            OP('pool', 'memset', ['rmask'], ['rmask'], rmask[:].rearrange("p (a b) -> p a b", b=64)[:, :, 0:1], 0.0)
            CI = dict(mu_r=0, mu_k=1, mu_v=2, w0=3, a0=4, k_k=5, k_a=6, r_k=7, lnw=8, lnb=9)
            m_cr_lt = masks[:, 0, :]; m_rc_lt = masks[:, 1, :]; m_rc_le = masks[:, 2, :]; id64 = masks[:, 3, :]

            def bc4(m):
                return m.unsqueeze(1).to_broadcast([64, 8, 64])

            it = 0
            for sn in ('p', 's'):
                sq = seqs[sn]
                n = sq['n']
                for c0 in range(0, 25, 5):
                    DMA('pool', 'shift_st', [], [], sq['shift_out'][c0 * 128:(c0 + 5) * 128].rearrange("(c p o) -> c p o", p=128, o=1),
                        sq['psT'][c0:c0 + 5, :, n - 1:n], slow=True)
                if sn == 'p':
                    OP('pool', 'memset', [], ['H0', 'H1', 'H2', 'H3'], H[:], 0.0)
                else:
                    DMA('sp', 'Ssb', [], ['Ssb'], Ssb[:], state_wkv.rearrange("h i j -> i h j"))
                    for g in range(8):
                        OP('pe', 'transpose', ['Ssb', 'idf'], ['bk0'], out=banks[0][:, g * 64:(g + 1) * 64],
                           in_=Ssb[:, 2 * g:2 * g + 2, :].rearrange("p a t -> p (a t)"), identity=idf[:64, :64])
                    OP('dve', 'tensor_copy', ['bk0'], ['H0', 'H1', 'H2', 'H3'], out=H[:], in_=banks[0][:, :].rearrange("p (g i) -> p g i", i=64))
                for (_, t0, nt) in tiles_of(sn):
                    b = it % 2
                    it += 1
                    pk = 'psb%d' % b
                    P = psb[b]
                    nch = nt // 64
                    for c0 in range(0, 33, 4):
                        c1 = min(33, c0 + 4)
                        if t0 == 0:
                            DMA('sp', pk, [], [pk], P[:, c0:c1, 1:1 + nt], sq['psT'][c0:c1, :, 0:nt].rearrange("c p t -> p c t"))
                        else:
                            DMA('sp', pk, [], [pk], P[:, c0:c1, 0:1 + nt], sq['psT'][c0:c1, :, t0 - 1:t0 + nt].rearrange("c p t -> p c t"))
                    if t0 == 0:
                        if sn == 'p':
                            OP('pool', 'memset', [], [pk], P[:, :, 0:1], 0.0)
                        else:
                            for c0 in range(0, 25, 5):
                                DMA('sp', pk, [], [pk], P[:, c0:c0 + 5, 0:1], state_shift[c0 * 128:(c0 + 5) * 128].rearrange("(c p o) -> p c o", p=128, o=1), slow=True)
                    for X, (M, mk) in enumerate([(m_r, 'm_r'), (m_k, 'm_k'), (m_v, 'm_v')]):
                        cur = P[:, 8 * X:8 * X + 8, 1:1 + nt]; prev = P[:, 8 * X:8 * X + 8, 0:nt]
                        e = 'pool' if X == 1 else 'dve'
                        OP(e, 'tensor_tensor', [pk], [mk], out=v3(M, nt), in0=prev, in1=cur, op=ALU.subtract)
                        OP(e, 'tensor_tensor', [mk, 'cvec'], [mk], out=v3(M, nt), in0=v3(M, nt), in1=cbc(X, nt), op=ALU.mult)
                        OP(e, 'tensor_tensor', [mk, pk], [mk], out=v3(M, nt), in0=v3(M, nt), in1=cur, op=ALU.add)
                    OP('dve', 'tensor_tensor', [pk], ['m_wa'], out=m_wa[:, :nt], in0=P[:, 24, 0:nt], in1=P[:, 24, 1:1 + nt], op=ALU.subtract)
                    OP('dve', 'scalar_tensor_tensor', ['m_wa', 'muwa', pk], ['m_wa'], out=m_wa[:, :nt], in0=m_wa[:, :nt], scalar=muwa[:, 0:1],
                       in1=P[:, 24, 1:1 + nt], op0=ALU.mult, op1=ALU.add)
                    OP('act', 'activation', [pk], ['tmp2'], out=v3(tmp2, nt), in_=P[:, 25:33, 1:1 + nt], func=AF.Exp, scale=-1.0)
                    OP('pool', 'tensor_scalar_add', ['tmp2'], ['tmp2'], out=tmp2[:, :8 * nt], in0=tmp2[:, :8 * nt], scalar1=1.0)
                    OP('dve', 'reciprocal', ['tmp2'], ['tmp2'], out=tmp2[:, :8 * nt], in_=tmp2[:, :8 * nt])
                    OP('pool', 'tensor_tensor', ['tmp2', pk], ['sgB'], out=v3(sgB, nt), in0=v3(tmp2, nt), in1=P[:, 25:33, 1:1 + nt], op=ALU.mult)
                    OP('act', 'activation', ['m_wa'], ['th'], out=th[0:64, :nt], in_=m_wa[0:64, :nt], func=AF.Exp, scale=-2.0)
                    OP('dve', 'tensor_scalar_add', ['th'], ['th'], out=th[0:64, :nt], in0=th[0:64, :nt], scalar1=1.0)
                    OP('dve', 'reciprocal', ['th'], ['th'], out=th[0:64, :nt], in_=th[0:64, :nt])
                    OP('dve', 'tensor_scalar', ['th'], ['th'], out=th[0:64, :nt], in0=th[0:64, :nt], scalar1=2.0, scalar2=-1.0, op0=ALU.mult, op1=ALU.add)
                    pw = [banks[0], banks[1]]

                    def pwv(nt):
                        return [banks[0][:, :4 * nt].rearrange("p (g t) -> p g t", t=nt), banks[1][:, :4 * nt].rearrange("p (g t) -> p g t", t=nt)]
                    pv = pwv(nt)
                    for g in range(8):
                        OP('pe', 'matmul', ['LR', 'th'], ['bk%d' % (g // 4)], pv[g // 4][:, g % 4, :], lhsT=LR[0:64, g * 128:(g + 1) * 128],
                           rhs=th[0:64, :nt], start=True, stop=True)
                    for hf in range(2):
                        OP('dve', 'tensor_tensor', ['bk%d' % hf, 'cvec'], ['ewt'], out=v3(ewt, nt, 4 * hf, 4 * hf + 4), in0=pv[hf],
                           in1=cvec[:, CI['w0'], 4 * hf:4 * hf + 4].unsqueeze(2).to_broadcast([128, 4, nt]), op=ALU.add)
                    for g in range(8):
                        OP('pe', 'matmul', ['LR', 'm_wa'], ['bk%d' % (g // 4)], pv[g // 4][:, g % 4, :], lhsT=LR[64:128, g * 128:(g + 1) * 128],
                           rhs=m_wa[64:128, :nt], start=True, stop=True)
                    for hf in range(2):
                        OP('dve', 'tensor_tensor', ['bk%d' % hf, 'cvec'], ['alpha'], out=v3(alpha, nt, 4 * hf, 4 * hf + 4), in0=pv[hf],
                           in1=cvec[:, CI['a0'], 4 * hf:4 * hf + 4].unsqueeze(2).to_broadcast([128, 4, nt]), op=ALU.add)
                    W8 = 8 * nt
                    OP('act', 'activation', ['ewt'], ['ewt'], out=ewt[:, :W8], in_=ewt[:, :W8], func=AF.Exp, scale=-1.0)
                    OP('act', 'activation', ['ewt'], ['ewt'], out=ewt[:, :W8], in_=ewt[:, :W8], func=AF.Ln, bias=1.0)
                    OP('act', 'activation', ['ewt'], ['ewt'], out=ewt[:, :W8], in_=ewt[:, :W8], func=AF.Exp, scale=-1.0, bias=-0.5)
                    OP('act', 'activation', ['alpha'], ['alpha'], out=alpha[:, :W8], in_=alpha[:, :W8], func=AF.Exp, scale=-1.0)
                    OP('pool', 'tensor_scalar_add', ['alpha'], ['alpha'], out=alpha[:, :W8], in0=alpha[:, :W8], scalar1=1.0)
                    OP('dve', 'reciprocal', ['alpha'], ['alpha'], out=alpha[:, :W8], in_=alpha[:, :W8])
                    OP('pool', 'tensor_scalar_mul', ['ewt'], ['tmp'], out=tmp[:, :W8], in0=ewt[:, :W8], scalar1=-1.0)
                    OP('dve', 'tensor_tensor_scan', ['tmp', 'rmask'], ['Lt'], out=Lt[:, :W8], data0=rmask[:, :W8], data1=tmp[:, :W8], initial=0.0,
                       op0=ALU.mult, op1=ALU.add)
                    OP('dve', 'tensor_tensor', ['m_k', 'cvec'], ['kk'], out=v3(kk, nt), in0=v3(m_k, nt), in1=cbc(CI['k_k'], nt), op=ALU.mult)
                    OP('pool', 'tensor_tensor', ['kk'], ['tmp'], out=tmp[:, :W8], in0=kk[:, :W8], in1=kk[:, :W8], op=ALU.mult)
                    for hf in range(2):
                        OP('pe', 'matmul', ['bones', 'tmp'], ['bk%d' % hf], banks[hf][:, :4 * nt], lhsT=bones[:], rhs=tmp[:, hf * 4 * nt:(hf + 1) * 4 * nt],
                           start=True, stop=True)
                        OP('dve', 'tensor_scalar_max', ['bk%d' % hf], ['tmp2'], out=tmp2[:, hf * 4 * nt:(hf + 1) * 4 * nt], in0=banks[hf][:, :4 * nt], scalar1=1e-24)
                    OP('act', 'activation', ['tmp2'], ['tmp2'], out=tmp2[:, :W8], in_=tmp2[:, :W8], func=AF.Ln)
                    OP('act', 'activation', ['tmp2'], ['tmp2'], out=tmp2[:, :W8], in_=tmp2[:, :W8], func=AF.Exp, scale=-0.5)
                    OP('dve', 'tensor_tensor', ['kk', 'tmp2'], ['kk'], out=kk[:, :W8], in0=kk[:, :W8], in1=tmp2[:, :W8], op=ALU.mult)
                    OP('dve', 'scalar_tensor_tensor', ['alpha', 'cvec'], ['kmod'], out=v3(kmod, nt), in0=v3(alpha, nt), scalar=-1.0, in1=cbc(CI['k_a'], nt),
                       op0=ALU.add, op1=ALU.mult)
                    OP('dve', 'scalar_tensor_tensor', ['kmod', 'm_k'], ['kmod'], out=kmod[:, :W8], in0=kmod[:, :W8], scalar=1.0, in1=m_k[:, :W8],
                       op0=ALU.add, op1=ALU.mult)
                    OP('pool', 'tensor_tensor', ['m_r', 'kmod'], ['tmp'], out=tmp[:, :W8], in0=m_r[:, :W8], in1=kmod[:, :W8], op=ALU.mult)
                    OP('pool', 'tensor_tensor', ['tmp', 'cvec'], ['tmp'], out=v3(tmp, nt), in0=v3(tmp, nt), in1=cbc(CI['r_k'], nt), op=ALU.mult)
                    for hf in range(2):
                        OP('pe', 'matmul', ['bones', 'tmp'], ['bk%d' % hf], banks[hf][:, :4 * nt], lhsT=bones[:], rhs=tmp[:, hf * 4 * nt:(hf + 1) * 4 * nt],
                           start=True, stop=True)
                        OP('dve', 'tensor_tensor', ['bk%d' % hf, 'm_v'], ['bonusT'], out=bonusT[:, hf * 4 * nt:(hf + 1) * 4 * nt], in0=banks[hf][:, :4 * nt],
                           in1=m_v[:, hf * 4 * nt:(hf + 1) * 4 * nt], op=ALU.mult)
                    def v4(t):
                        return t[:, :W8].rearrange("p (g c t) -> p g c t", g=8, t=64)
                    OP('act', 'activation', ['Lt'], ['E1'], out=E1[:, :W8], in_=Lt[:, :W8], func=AF.Exp)
                    OP('dve', 'tensor_tensor', ['m_r', 'E1'], ['AR'], out=AR[:, :, :nch, 1, :], in0=v4(m_r), in1=v4(E1), op=ALU.mult)
                    OP('pool', 'tensor_copy', ['E1'], ['pc'], out=pc[:, :, :nch], in_=v4(E1)[:, :, :, 63])
                    OP('dve', 'tensor_tensor', ['Lt', 'ewt'], ['ewt'], out=ewt[:, :W8], in0=Lt[:, :W8], in1=ewt[:, :W8], op=ALU.add)
                    OP('act', 'activation', ['ewt'], ['ewt'], out=ewt[:, :W8], in_=ewt[:, :W8], func=AF.Exp)
                    OP('dve', 'scalar_tensor_tensor', ['kk', 'ewt'], ['AR'], out=AR[:, :, :nch, 0, :], in0=v4(kk), scalar=-1.0, in1=v4(ewt),
                       op0=ALU.mult, op1=ALU.mult)
                    OP('act', 'activation', ['Lt'], ['Lt'], out=Lt[:, :W8], in_=Lt[:, :W8], func=AF.Exp, scale=-1.0)
                    OP('pool', 'tensor_tensor', ['kk', 'alpha'], ['tmp'], out=tmp[:, :W8], in0=kk[:, :W8], in1=alpha[:, :W8], op=ALU.mult)
                    OP('dve', 'tensor_tensor', ['tmp', 'Lt'], ['BK'], out=BK[:, :, :nch, 0, :], in0=v4(tmp), in1=v4(Lt), op=ALU.mult)
                    OP('dve', 'tensor_tensor', ['kmod', 'Lt'], ['BK'], out=BK[:, :, :nch, 1, :], in0=v4(kmod), in1=v4(Lt), op=ALU.mult)

                    for ch in range(nch):
                        for (src, skey, dst, dkey) in [(None, 'm_v', Vtm, 'Vtm'), (0, 'BK', Btm, 'Btm'), (1, 'BK', Ktm, 'Ktm')]:
                            for g in range(8):
                                if src is None:
                                    in_ = v3(m_v, nt)[:, g, ch * 64:(ch + 1) * 64]
                                else:
                                    in_ = BK[:, g, ch, src, :]
                                OP('pe', 'transpose', [skey, 'idf'], ['bk%d' % (2 + g // 4)], out=banks[2 + g // 4][0:64, (g % 4) * 128:(g % 4 + 1) * 128],
                                   in_=in_, identity=idf[:])
                            OP('act', 'copy', ['bk2'], [dkey], out=dst[:, 0:512], in_=banks[2][0:64, :])
                            OP('dve', 'tensor_copy', ['bk3'], [dkey], out=dst[:, 512:1024], in_=banks[3][0:64, :])
                        Ytm4 = Ytm[:].rearrange("p (g a) t -> p g a t", a=2)
                        for hb in range(2):
                            par = hb
                            heads = [2 * g + par for g in range(8)]
                            hp = par * 64
                            sl = slice(hp, hp + 64)

                            def pv_(b0, nb, pat, **kw):
                                return psum_all[0:64, b0 * 512:(b0 + nb) * 512].rearrange(pat, **kw)
                            pA = pv_(4, 1, "p (h s) -> p h s", s=64)
                            pB = pv_(5, 2, "p (h s) -> p h s", s=128)
                            pC = pv_(2, 2, "p (h s) -> p h s", s=128)
                            pX = pv_(5, 2, "p (h c s) -> p h c s", c=2, s=64)
                            pTT = pv_(7, 1, "p (h s) -> p h s", s=64)
                            for i, h in enumerate(heads):
                                OP('pe', 'matmul', ['AR', 'BK'], ['bk4'], pA[:, i, :], lhsT=AR[sl, i, ch, 0, :], rhs=BK[sl, i, ch, 0, :], start=True, stop=True)
                            for i, h in enumerate(heads):
                                OP('pe', 'matmul', ['AR', 'BK'], ['bk5', 'bk6'], pB[:, i, :], lhsT=BK[sl, i, ch, 0, :],
                                   rhs=AR[sl, i, ch, :, :].rearrange("p a t -> p (a t)"), start=True, stop=True)
                            for i, h in enumerate(heads):
                                OP('pe', 'matmul', ['AR', 'BK'], ['bk2', 'bk3'], pC[:, i, :], lhsT=BK[sl, i, ch, 1, :],
                                   rhs=AR[sl, i, ch, :, :].rearrange("p a t -> p (a t)"), start=True, stop=True)
                            OP('dve', 'tensor_tensor', ['bk4', 'masks'], ['XX0'], out=XX[0][:, :, 0, :], in0=pA, in1=bc4(m_cr_lt), op=ALU.mult)
                            OP('dve', 'tensor_tensor', ['bk5', 'bk6', 'masks'], ['XX0'], out=XX[0][:, :, 1, :], in0=pB[:, :, 0:64], in1=bc4(m_rc_lt), op=ALU.mult)
                            OP('dve', 'tensor_tensor', ['bk5', 'bk6', 'masks'], ['ARBT'], out=ARBT[:], in0=pB[:, :, 64:128], in1=bc4(m_rc_le), op=ALU.mult)
                            OP('dve', 'tensor_tensor', ['bk2', 'bk3', 'masks'], ['AKT'], out=AKT[:], in0=pC[:, :, 0:64], in1=bc4(m_rc_lt), op=ALU.mult)
                            OP('dve', 'tensor_tensor', ['bk2', 'bk3', 'masks'], ['ARKT'], out=ARKT[:], in0=pC[:, :, 64:128], in1=bc4(m_rc_le), op=ALU.mult)
                            OP('dve', 'tensor_tensor', ['XX0', 'masks'], ['TT'], out=TT[:], in0=XX[0][:, :, 1, :], in1=bc4(id64), op=ALU.add)
                            for k in range(1, 6):
                                xo = XX[(k - 1) % 2]; xn = XX[k % 2]
                                xok = 'XX%d' % ((k - 1) % 2); xnk = 'XX%d' % (k % 2)
                                for i in range(8):
                                    OP('pe', 'matmul', [xok], ['bk5', 'bk6'], pX[:, i, 0, :], lhsT=xo[:, i, 1, :], rhs=xo[:, i, 0, :], start=True, stop=True)
                                    if k < 5:
                                        OP('pe', 'matmul', [xok], ['bk5', 'bk6'], pX[:, i, 1, :], lhsT=xo[:, i, 0, :], rhs=xo[:, i, 1, :], start=True, stop=True)
                                if k < 5:
                                    OP('act', 'copy', ['bk5', 'bk6'], [xnk], out=xn[:], in_=pX)
                                else:
                                    OP('act', 'copy', ['bk5', 'bk6'], [xnk], out=xn[:, :, 0, :], in_=pX[:, :, 0, :])
                                for i in range(8):
                                    OP('pe', 'matmul', [xnk, 'TT'], ['bk7'], pTT[:, i, :], lhsT=xn[:, i, 0, :], rhs=TT[:, i, :], start=True, stop=True)
                                OP('dve', 'tensor_tensor', ['bk7', 'TT'], ['TT'], out=TT[:], in0=pTT, in1=TT[:], op=ALU.add)
                            hk = 'H%d' % hb
                            pW1 = pv_(4, 1, "p (h s) -> p h s", s=64)
                            pW2 = pv_(7, 1, "p (h s) -> p h s", s=64)
                            pU = pv_(5, 1, "p (h s) -> p h s", s=64)
                            pY1 = pv_(4, 1, "p (h s) -> p h s", s=64)
                            pY2 = pv_(6, 1, "p (h s) -> p h s", s=64)
                            pH = psum_all[:, 7 * 512:8 * 512].rearrange("p (h s) -> p h s", s=64)
                            for i, h in enumerate(heads):
                                OP('pe', 'matmul', ['AR', hk], ['bk4'], pW1[:, i, :], lhsT=AR[sl, i, ch, 0, :], rhs=H[sl, i, :], start=True, stop=True)
                            for i, h in enumerate(heads):
                                OP('pe', 'matmul', ['AKT', 'Vtm'], ['bk7'], pW2[:, i, :], lhsT=AKT[:, i, :], rhs=Vtm[:, h * 64:(h + 1) * 64], start=True, stop=True)
                            OP('act', 'copy', ['bk4'], ['Wsb'], out=Wsb[:], in_=pW1)
                            OP('dve', 'tensor_tensor', ['bk7', 'Wsb'], ['Wsb'], out=Wsb[:], in0=pW2, in1=Wsb[:], op=ALU.add)
                            for i, h in enumerate(heads):
                                OP('pe', 'matmul', ['TT', 'Wsb'], ['bk5'], pU[:, i, :], lhsT=TT[:, i, :], rhs=Wsb[:, i, :], start=True, stop=True)
                            OP('act', 'copy', ['bk5'], ['Usb'], out=Usb[:], in_=pU)
                            for i, h in enumerate(heads):
                                OP('pe', 'matmul', ['AR', hk], ['bk4'], pY1[:, i, :], lhsT=AR[sl, i, ch, 1, :], rhs=H[sl, i, :], start=True, stop=True)
                            for i, h in enumerate(heads):
                                OP('pe', 'matmul', ['ARBT', 'Usb'], ['bk6'], pY2[:, i, :], lhsT=ARBT[:, i, :], rhs=Usb[:, i, :], start=True, stop=False)
                                OP('pe', 'matmul', ['ARKT', 'Vtm'], ['bk6'], pY2[:, i, :], lhsT=ARKT[:, i, :], rhs=Vtm[:, h * 64:(h + 1) * 64], start=False, stop=True)
                            OP('act', 'copy', ['bk4'], ['Ysq'], out=Ysq[:, 0:8, :], in_=pY1)
                            OP('dve', 'tensor_tensor', ['bk6', 'Ysq'], ['Ytm'], out=Ytm4[:, :, par, :], in0=pY2, in1=Ysq[:, 0:8, :], op=ALU.add)
                            for i, h in enumerate(heads):
                                OP('pe', 'matmul', ['Btm', 'Usb'], ['bk7'], pH[:, i, :], lhsT=Btm[:, i * 128:(i + 1) * 128],
                                   rhs=Usb[:, i, :], start=True, stop=False)
                                OP('pe', 'matmul', ['Ktm', 'Vtm'], ['bk7'], pH[:, i, :], lhsT=Ktm[:, i * 128:(i + 1) * 128],
                                   rhs=Vtm[:, h * 64:(h + 1) * 64], start=False, stop=True)
                            Hs = H[sl, :, :]
                            OP('dve', 'tensor_tensor', ['bk7', hk], [hk], out=Hs, in0=pH[sl, :, :], in1=Hs, op=ALU.add)
                            OP('dve', 'tensor_tensor', [hk, 'pc'], [hk], out=Hs, in0=Hs,
                               in1=pc[sl, :, ch:ch + 1].to_broadcast([64, 8, 64]), op=ALU.mult)
                        OP('dve', 'tensor_reduce', ['Ytm'], ['gst'], out=gst[:, 0, :], in_=Ytm[:], axis=AX.X, op=ALU.add)
                        OP('pool', 'tensor_tensor', ['Ytm'], ['Ysq'], out=Ysq[:], in0=Ytm[:], in1=Ytm[:], op=ALU.mult)
                        OP('dve', 'tensor_reduce', ['Ysq'], ['gst'], out=gst[:, 1, :], in_=Ysq[:], axis=AX.X, op=ALU.add)
                        OP('dve', 'tensor_scalar_mul', ['gst'], ['gst'], out=gst[:, 0:2, :], in0=gst[:, 0:2, :], scalar1=1.0 / 64)
                        OP('dve', 'tensor_tensor', ['gst'], ['gst'], out=gst[:, 2, :], in0=gst[:, 0, :], in1=gst[:, 0, :], op=ALU.mult)
                        OP('dve', 'tensor_tensor', ['gst'], ['gst'], out=gst[:, 3, :], in0=gst[:, 1, :], in1=gst[:, 2, :], op=ALU.subtract)
                        OP('act', 'activation', ['gst'], ['gst'], out=gst[:, 3, :], in_=gst[:, 3, :], func=AF.Ln, bias=GN_EPS)
                        OP('act', 'activation', ['gst'], ['gst'], out=gst[:, 3, :], in_=gst[:, 3, :], func=AF.Exp, scale=-0.5)
                        OP('dve', 'tensor_tensor', ['Ytm', 'gst'], ['Ytm'], out=Ytm[:], in0=Ytm[:], in1=gst[:, 0, :].unsqueeze(2).to_broadcast([64, 16, 64]), op=ALU.subtract)
                        OP('dve', 'tensor_tensor', ['Ytm', 'gst'], ['Ytm'], out=Ytm[:], in0=Ytm[:], in1=gst[:, 3, :].unsqueeze(2).to_broadcast([64, 16, 64]), op=ALU.mult)
                        pYT = banks[0][:, :].rearrange("p (g t) -> p g t", t=64)
                        for g in range(8):
                            OP('pe', 'transpose', ['Ytm', 'idf'], ['bk0'], out=pYT[:, g, :], in_=Ytm[:, 2 * g:2 * g + 2, :].rearrange("p a t -> p (a t)"), identity=idf[:64, :64])
                        csl = slice(ch * 64, (ch + 1) * 64)
                        OP('dve', 'tensor_tensor', ['bk0', 'cvec'], ['yo'], out=yo[:], in0=pYT, in1=cbc(CI['lnw'], 64), op=ALU.mult)
                        OP('dve', 'tensor_tensor', ['yo', 'cvec'], ['yo'], out=yo[:], in0=yo[:], in1=cbc(CI['lnb'], 64), op=ALU.add)
                        OP('dve', 'tensor_tensor', ['yo', 'bonusT'], ['yo'], out=yo[:], in0=yo[:], in1=v3(bonusT, nt)[:, :, csl], op=ALU.add)
                        OP('dve', 'tensor_tensor', ['yo', 'sgB'], ['yob'], out=yob[:], in0=yo[:], in1=v3(sgB, nt)[:, :, csl], op=ALU.mult)
                        DMA('sp', 'yob_st', ['yob'], [], sq['mixT'][8:16, :, t0 + ch * 64:t0 + (ch + 1) * 64].rearrange("c p t -> p c t"), yob[:])
                for g in range(8):
                    OP('pe', 'transpose', ['H0', 'H1', 'H2', 'H3', 'idf'], ['bk1'], out=banks[1][0:64, (g % 4) * 128:(g % 4 + 1) * 128], in_=H[:, g, :], identity=idf[:])
                    if g % 4 == 3:
                        OP('dve', 'tensor_copy', ['bk1'], ['Ssb'], out=Ssb[:, (g - 3) * 2:(g + 1) * 2, :], in_=banks[1][0:64, :].rearrange("p (h j) -> p h j", j=64))
                DMA('sp', 'Ssb_st', ['Ssb'], [], sq['wkv_out'].rearrange("h i j -> i h j"), Ssb[:])
        S.barrier()

    def phase4():
        if True:
            sb, ps = mk_alloc()
            wo = sb("wo", [128, 16, D], BF16)
            wst = [sb("wst%d" % i, [128, D]) for i in range(2)]
            gpost = sb("gpost", [128, D])
            xt = [sb("xt%d" % i, [128, D]) for i in range(2)]
            mx = [sb("mx%d" % i, [128, 16, 128], BF16) for i in range(2)]
            po = [[ps("po%d%d" % (i, j), [128, 512]) for j in range(2)] for i in range(2)]
            junk = sb("junk", [128, 512]); ss = sb("ss", [128, 2]); rstd = sb("rstd", [128, 1])
            yt = [sb("yt%d" % i, [128, D]) for i in range(2)]
            DMA('sp', 'gpost', [], ['gpost'], gpost[:], norm_post.partition_broadcast(128))
            for kc in range(16):
                b = kc % 2
                DMA('sp' if b == 0 else 'pool', 'wst%d' % b, [], ['wst%d' % b], wst[b][:], w_out[kc * 128:(kc + 1) * 128, :])
                OP('dve' if b == 0 else 'pool', 'tensor_copy', ['wst%d' % b], ['wo'], out=wo[:, kc, :], in_=wst[b][:])
            for it, (sn, t0, nt) in enumerate(all_tiles):
                sq = seqs[sn]
                b = it % 2
                DMA('sp', 'xt%d' % b, [], ['xt%d' % b], xt[b][:nt], sq['x'][t0:t0 + nt, :])
                for c0 in (0, 8):
                    DMA('pool', 'mx%d' % b, [], ['mx%d' % b], mx[b][:, c0:c0 + 8, :nt], sq['mixT'][c0:c0 + 8, :, t0:t0 + nt].rearrange("c p t -> p c t"))
                for hf in range(2):
                    for kc in range(16):
                        OP('pe', 'matmul', ['mx%d' % b, 'wo'], ['po%d%d' % (b, hf)], po[b][hf][:nt, :], lhsT=mx[b][:, kc, :nt],
                           rhs=wo[:, kc, hf * 512:(hf + 1) * 512], start=(kc == 0), stop=(kc == 15))
                    OP('act', 'activation', ['po%d%d' % (b, hf)], ['junk', 'ss'], out=junk[:nt], in_=po[b][hf][:nt, :], func=AF.Square,
                       accum_out=ss[:nt, hf:hf + 1])
                OP('dve', 'tensor_tensor', ['ss'], ['ss'], out=ss[:nt, 0:1], in0=ss[:nt, 0:1], in1=ss[:nt, 1:2], op=ALU.add)
                rsqrt_col(rstd, ss[:, 0:1], 1.0 / D, NORM_EPS, ['ss'], ['rstd'], nt)
                for hf in range(2):
                    OP('dve', 'scalar_tensor_tensor', ['po%d%d' % (b, hf), 'rstd', 'gpost'], ['yt%d' % b], out=yt[b][:nt, hf * 512:(hf + 1) * 512],
                       in0=po[b][hf][:nt, :], scalar=rstd[:nt, 0:1], in1=gpost[:nt, hf * 512:(hf + 1) * 512], op0=ALU.mult, op1=ALU.mult)
                OP('pool', 'tensor_tensor', ['yt%d' % b, 'xt%d' % b], ['yt%d' % b], out=yt[b][:nt], in0=yt[b][:nt], in1=xt[b][:nt], op=ALU.add)
                DMA('sp', 'yt%d_st' % b, ['yt%d' % b], [], sq['y'][t0:t0 + nt, :], yt[b][:nt])

    import os
    ph = os.environ.get('KPH', '1A,1B,2,3,4').split(',')
    if '1A' in ph:
        phase1('A')
    if '1B' in ph:
        phase1('B')
    if '2' in ph:
        phase2()
    if '3' in ph:
        phase3()
    if '4' in ph:
        phase4()
    print('NREC', S.nrec, flush=True)
    S.emit()
    gstack.close()
    return nc


def _consts(T):
    ident = np.eye(128, dtype=np.float32)
    bones = np.zeros((128, 128), np.float32)
    bones[:64, :64] = 1.0
    bones[64:, 64:] = 1.0
    r = np.arange(64)[:, None]; c = np.arange(64)[None, :]
    masks = np.stack([(c < r), (r < c), (r <= c), (r == c)], axis=1).astype(np.float32)

    def cs(pos):
        inv = np.power(np.float32(500000.0), -np.arange(8, dtype=np.float32) * np.float32(2.0 / 16)).astype(np.float32)
        ang = pos.astype(np.float32)[:, None] * inv[None, :]
        return np.concatenate([np.cos(ang), np.sin(ang)], axis=1).astype(np.float32)
    return dict(c_ident=ident, c_bones=bones, c_masks=np.ascontiguousarray(masks),
                c_cs_p=cs(np.arange(T)), c_cs_s=cs(PAST + np.arange(DEC)))


_NC_CACHE = {}


def kernel(x_prompt, x_sample, cache_k, cache_v, state_wkv, state_shift, norm_pre, w_in,
           lam_q1, lam_k1, lam_q2, lam_k2, subln, mu_shift, w0, w_up, a0, a_up, k_k, k_a,
           r_k, ln_x_w, ln_x_b, w_out, norm_post):
    f = lambda a: np.ascontiguousarray(np.asarray(a, dtype=np.float32))
    x_prompt = f(x_prompt); x_sample = f(x_sample)
    B, T, _ = x_prompt.shape
    if T not in _NC_CACHE:
        _NC_CACHE[T] = build(T)
    nc = _NC_CACHE[T]
    consts = _consts(T)
    shared = dict(norm_pre=f(norm_pre)[0], w_in=f(w_in)[0], lam_q1=f(lam_q1)[0], lam_k1=f(lam_k1)[0],
                  lam_q2=f(lam_q2)[0], lam_k2=f(lam_k2)[0], subln=f(subln)[0], mu_shift=f(mu_shift)[0],
                  w0=f(w0)[0], w_up=f(w_up)[0], a0=f(a0)[0], a_up=f(a_up)[0], k_k=f(k_k)[0], k_a=f(k_a)[0],
                  r_k=f(r_k)[0].reshape(-1), ln_x_w=f(ln_x_w)[0], ln_x_b=f(ln_x_b)[0], w_out=f(w_out)[0],
                  norm_post=f(norm_post)[0])
    shared.update(consts)
    cache_k = f(cache_k); cache_v = f(cache_v); state_wkv = f(state_wkv); state_shift = f(state_shift)
    in_maps = []
    for c in range(B):
        m = dict(shared)
        m.update(x_p=x_prompt[c], x_s=x_sample[c], cache_k=cache_k[0, c].reshape(PAST, D),
                 cache_v=cache_v[0, c].reshape(PAST, D), state_wkv=state_wkv[0, c],
                 state_shift=state_shift[0, c, 0])
        in_maps.append(m)
    res = run_bass_kernel_spmd(nc, in_maps, core_ids=list(range(B)))
    R = res.results
    st = lambda k: np.stack([np.asarray(r[k], dtype=np.float32) for r in R], axis=0)
    y_p = st('y_p'); y_s = st('y_s')
    k_p = st('k_p').reshape(1, B, T, 8, 2, 64); v_p = st('v_p').reshape(1, B, T, 8, 128)
    wkv_p = st('wkv_p').reshape(1, B, 16, 64, 64); shift_p = st('shift_p').reshape(1, B, 1, 3200)
    k_s = st('k_s').reshape(1, B, DEC, 8, 2, 64); v_s = st('v_s').reshape(1, B, DEC, 8, 128)
    wkv_s = st('wkv_s').reshape(1, B, 16, 64, 64); shift_s = st('shift_s').reshape(1, B, 1, 3200)
    return (y_p, y_s, k_p, v_p, wkv_p, shift_p, k_s, v_s, wkv_s, shift_s)
```

```python
import contextlib
import os
import numpy as np
import concourse.bass as bass
import concourse.mybir as mybir
from concourse.bass_utils import run_bass_kernel_spmd

F32 = mybir.dt.float32
BF16 = mybir.dt.bfloat16
AF = mybir.ActivationFunctionType
ALU = mybir.AluOpType
AX = mybir.AxisListType

ARENA_W = 45056
D = 1024
SEQ = 8192
PAST = 1024
DEC = 64
NCORES = 8
LAMBDA_INIT = 0.2
NORM_EPS = 1e-6
SUBLN_EPS = 1e-5
GN_EPS = 64e-5


class Sched:
    ENGS = ['pe', 'act', 'dve', 'pool', 'sp']

    def __init__(self, nc, same_engine_sync=('act', 'dve', 'pool')):
        self.nc = nc
        self.ops = {e: [] for e in self.ENGS}
        self.res = {}
        self.chan_count = {}
        self.same_engine_sync = set(same_engine_sync)
        self.last_real = {}
        self.nrec = 0
        self.limit = int(os.environ.get('KMAXOPS', '1000000000'))

    def _deps(self, reads, writes):
        deps = set()
        for k in reads:
            st = self.res.get(k)
            if st and st['w'] is not None:
                deps.add(st['w'])
        for k in writes:
            st = self.res.get(k)
            if st:
                if st['w'] is not None:
                    deps.add(st['w'])
                for r in st['r']:
                    deps.add(('war',) + r)
        return deps

    def _commit(self, tok, reads, writes):
        for k in writes:
            self.res[k] = {'w': tok, 'r': []}
        for k in reads:
            st = self.res.setdefault(k, {'w': None, 'r': []})
            st['r'].append(tok)

    def op(self, eng, fn, reads=(), writes=()):
        self.nrec += 1
        if self.nrec > self.limit:
            return
        idx = len(self.ops[eng])
        deps = set()
        for d in self._deps(reads, writes):
            war = d[0] == 'war'
            if war:
                d = d[1:]
            if d[0] == 'eng' and d[1] == eng:
                if war or eng not in self.same_engine_sync:
                    continue
            deps.add(d)
        self.ops[eng].append({'fn': fn, 'deps': deps, 'dma': None})
        self.last_real[eng] = idx
        self._commit(('eng', eng, idx), reads, writes)

    def dma(self, eng, fn, reads=(), writes=(), chan=None):
        assert chan is not None
        self.nrec += 1
        if self.nrec > self.limit:
            return
        deps = set()
        for d in self._deps(reads, writes):
            if d[0] == 'war':
                d = d[1:]
            deps.add(d)
        n = self.chan_count.get(chan, 0) + 1
        self.chan_count[chan] = n
        self.ops[eng].append({'fn': fn, 'deps': deps, 'dma': (chan, n)})
        self._commit(('dma', chan, n), reads, writes)

    def barrier(self):
        toks = set()
        for e in self.ENGS:
            if e in self.last_real:
                toks.add(('eng', e, self.last_real[e]))
        for c, n in self.chan_count.items():
            toks.add(('dma', c, n))
        for e in self.ENGS:
            deps = set(t for t in toks if not (t[0] == 'eng' and t[1] == e))
            self.ops[e].append({'fn': None, 'deps': deps, 'dma': None})

    def emit(self):
        nc = self.nc
        sig = {e: set() for e in self.ENGS}
        for e in self.ENGS:
            for o in self.ops[e]:
                for d in o['deps']:
                    if d[0] == 'eng':
                        sig[d[1]].add(d[2])
        sigcount = {}
        for e in self.ENGS:
            c = 0
            m = {}
            for i, o in enumerate(self.ops[e]):
                if i in sig[e]:
                    assert o['fn'] is not None and o['dma'] is None
                    c += 1
                    m[i] = c
            sigcount[e] = m
        chans = sorted(self.chan_count.keys(), key=str)
        with contextlib.ExitStack() as es:
            esem = {e: es.enter_context(nc.semaphore("s_" + e)) for e in self.ENGS}
            csem = {c: es.enter_context(nc.semaphore("c%d" % i)) for i, c in enumerate(chans)}
            block = es.enter_context(nc.Block())
            engobj = {'pe': block.tensor, 'act': block.scalar, 'dve': block.vector,
                      'pool': block.gpsimd, 'sp': block.sync}

            def make(e):
                def body(eng):
                    waited = {}
                    for i, o in enumerate(self.ops[e]):
                        need = {}
                        for d in o['deps']:
                            if d[0] == 'eng':
                                s = esem[d[1]]
                                v = sigcount[d[1]][d[2]]
                            else:
                                s = csem[d[1]]
                                v = 16 * d[2]
                            if v > need.get(s.num, (None, 0))[1]:
                                need[s.num] = (s, v)
                        for key, (s, v) in need.items():
                            if waited.get(key, 0) >= v:
                                continue
                            waited[key] = v
                            eng.wait_ge(s, v)
                        if o['fn'] is None:
                            continue
                        ins = o['fn'](eng)
                        if o['dma'] is not None:
                            ins.then_inc(csem[o['dma'][0]], 16)
                        elif i in sig[e]:
                            ins.then_inc(esem[e], 1)
                    if e == 'sp':
                        for c, n in self.chan_count.items():
                            if waited.get(csem[c].num, 0) < 16 * n:
                                eng.wait_ge(csem[c], 16 * n)
                return body
            for e in self.ENGS:
                engobj[e](make(e))


def build(T):
    nc = bass.Bass("TRN2", target_bir_lowering=False)
    S = Sched(nc)
    gstack = contextlib.ExitStack()
    arena = gstack.enter_context(nc.sbuf_tensor("arena", [128, ARENA_W], F32))
    psum_all = gstack.enter_context(nc.psum_tensor("psum_all", [128, 4096], F32))
    banks_g = [psum_all[:, i * 512:(i + 1) * 512] for i in range(8)]

    def _shape_view(v, shape):
        free = int(np.prod(shape[1:]))
        v = v[:shape[0], :free]
        if len(shape) > 2:
            names = "abcd"[:len(shape) - 1]
            pat = "p (" + " ".join(names) + ") -> p " + " ".join(names)
            kw = {names[i]: int(shape[1 + i]) for i in range(1, len(names))}
            v = v.rearrange(pat, **kw)
        return v

    def mk_alloc():
        st = {'off': 0, 'bank': 0}

        def sb(name, shape, dt=F32):
            esz = 2 if dt == BF16 else 4
            free = int(np.prod(shape[1:]))
            n4 = (free * esz + 3) // 4
            assert st['off'] + n4 <= ARENA_W, (name, st['off'], n4)
            v = arena[:, st['off']:st['off'] + n4]
            st['off'] += n4
            if dt == BF16:
                v = v.bitcast(BF16)
            return _shape_view(v, shape)

        def ps(name, shape, dt=F32):
            esz = 2 if dt == BF16 else 4
            free = int(np.prod(shape[1:]))
            assert free * esz <= 2048 and st['bank'] < 8, name
            v = banks_g[st['bank']][:, :]
            st['bank'] += 1
            if dt == BF16:
                v = v.bitcast(BF16)
            return _shape_view(v, shape)
        return sb, ps

    def din(name, shape, dt=F32):
        return nc.dram_tensor(name, shape, dt, kind="ExternalInput").ap()

    def dout(name, shape):
        return nc.dram_tensor(name, shape, F32, kind="ExternalOutput").ap()

    def dscr(name, shape, dt):
        return nc.dram_tensor(name, shape, dt, kind="Internal").ap()

    x_p = din("x_p", [T, D]); x_s = din("x_s", [DEC, D])
    cache_k = din("cache_k", [PAST, D]); cache_v = din("cache_v", [PAST, D])
    state_wkv = din("state_wkv", [16, 64, 64]); state_shift = din("state_shift", [3200])
    norm_pre = din("norm_pre", [D]); w_in = din("w_in", [D, 8320])
    lam_q1 = din("lam_q1", [64]); lam_k1 = din("lam_k1", [64])
    lam_q2 = din("lam_q2", [64]); lam_k2 = din("lam_k2", [64])
    subln = din("subln", [128]); mu_shift = din("mu_shift", [3200])
    w0 = din("w0", [D]); w_up = din("w_up", [64, D]); a0 = din("a0", [D]); a_up = din("a_up", [64, D])
    k_k = din("k_k", [D]); k_a = din("k_a", [D]); r_k = din("r_k", [D])
    ln_x_w = din("ln_x_w", [D]); ln_x_b = din("ln_x_b", [D])
    w_out = din("w_out", [2048, D]); norm_post = din("norm_post", [D])
    c_ident = din("c_ident", [128, 128]); c_bones = din("c_bones", [128, 128])
    c_masks = din("c_masks", [64, 4, 64])
    c_cs_p = din("c_cs_p", [T, 16]); c_cs_s = din("c_cs_s", [DEC, 16])

    y_p = dout("y_p", [T, D]); y_s = dout("y_s", [DEC, D])
    k_po = dout("k_p", [T, D]); v_po = dout("v_p", [T, D])
    wkv_p = dout("wkv_p", [16, 64, 64]); shift_p = dout("shift_p", [3200])
    k_so = dout("k_s", [DEC, D]); v_so = dout("v_s", [DEC, D])
    wkv_s = dout("wkv_s", [16, 64, 64]); shift_s = dout("shift_s", [3200])

    NKS = PAST + DEC
    seqs = {
        'p': dict(x=x_p, n=T, cs=c_cs_p, k_out=k_po, v_out=v_po, y=y_p, nkeys=T, koff=0,
                  wkv_out=wkv_p, shift_out=shift_p),
        's': dict(x=x_s, n=DEC, cs=c_cs_s, k_out=k_so, v_out=v_so, y=y_s, nkeys=NKS, koff=PAST,
                  wkv_out=wkv_s, shift_out=shift_s),
    }
    for sn, sq in seqs.items():
        sq['qT'] = dscr("qT_" + sn, [8, 128, sq['n']], BF16)
        sq['kT'] = dscr("kT_" + sn, [8, 128, sq['nkeys']], BF16)
        sq['vb'] = dscr("vb_" + sn, [sq['nkeys'], D], BF16)
        sq['ga'] = dscr("ga_" + sn, [sq['n'], D], F32)
        sq['psT'] = dscr("psT_" + sn, [33, 128, sq['n']], F32)
        sq['mixT'] = dscr("mixT_" + sn, [16, 128, sq['n']], BF16)

    def tiles_of(sn):
        n = seqs[sn]['n']
        out = []
        t0 = 0
        while t0 < n:
            nt = min(128, n - t0)
            out.append((sn, t0, nt))
            t0 += nt
        return out
    all_tiles = tiles_of('p') + tiles_of('s')

    def OP(eng, meth, reads, writes, *a, **kw):
        S.op(eng, lambda e: getattr(e, meth)(*a, **kw), reads=reads, writes=writes)

    def DMA(eng, chan, reads, writes, out, in_, slow=False):
        if slow:
            S.dma(eng, lambda e: e.dma_start(out=out, in_=in_, allow_slow_non_contiguous=True),
                  reads=reads, writes=writes, chan=chan)
        else:
            S.dma(eng, lambda e: e.dma_start(out=out, in_=in_), reads=reads, writes=writes, chan=chan)

    def rsqrt_col(dst, src, scale, eps, key_r, key_w, n):
        OP('act', 'activation', key_r, key_w, out=dst[:n], in_=src[:n], func=AF.Ln, scale=scale, bias=eps)
        OP('act', 'activation', key_w, key_w, out=dst[:n], in_=dst[:n], func=AF.Exp, scale=-0.5)

    def phase1(which):
        ncols = 4096 if which == 'A' else 4224
        col0 = 0 if which == 'A' else 4096
        if True:
            sb, ps = mk_alloc()
            wres = sb("wres", [128, 8, ncols], BF16)
            wst = [sb("wst%d" % i, [128, 1024]) for i in range(2)]
            gpre = sb("gpre", [128, D])
            idf = sb("idf", [128, 128]); idb = sb("idb", [128, 128], BF16)
            xt = [sb("xt%d" % i, [128, D]) for i in range(2)]
            junk = sb("junk", [128, D], BF16)
            ss = sb("ss", [128, 1]); rstd = sb("rstd", [128, 1])
            hb = sb("hb", [128, D], BF16)
            hT = sb("hT", [128, 8, 128], BF16)
            pT = ps("pT", [128, 8, 128], BF16)
            DMA('sp', 'gpre', [], ['gpre'], gpre[:], norm_pre.partition_broadcast(128))
            DMA('sp', 'idf', [], ['idf'], idf[:], c_ident)
            OP('dve', 'tensor_copy', ['idf'], ['idb'], out=idb[:], in_=idf[:])
            i = 0
            for kc in range(8):
                for c0 in range(0, ncols, 1024):
                    cw = min(1024, ncols - c0)
                    b = i % 2
                    DMA('sp' if b == 0 else 'pool', 'wst%d' % b, [], ['wst%d' % b], wst[b][:, :cw],
                        w_in[kc * 128:(kc + 1) * 128, col0 + c0:col0 + c0 + cw])
                    OP('dve' if b == 0 else 'pool', 'tensor_copy', ['wst%d' % b], ['wres'],
                       out=wres[:, kc, c0:c0 + cw], in_=wst[b][:, :cw])
                    i += 1
            if which == 'A':
                pp = [ps("pp%d" % i, [128, 512]) for i in range(3)]
                pT2 = ps("pT2", [128, 8, 128], BF16)
                q_sb = sb("q_sb", [128, D]); k_sb = sb("k_sb", [128, D]); v_sb = sb("v_sb", [128, D])
                ga_sb = sb("ga_sb", [128, D]); gtmp = sb("gtmp", [128, 512])
                qb = sb("qb", [128, D], BF16); kb = sb("kb", [128, D], BF16); vbt = sb("vbt", [128, D], BF16)
                qTs = sb("qTs", [128, 8, 128], BF16); kTs = sb("kTs", [128, 8, 128], BF16)
                cs = sb("cs", [128, 16]); rt = sb("rt", [128, 4, 16, 8])
                ck = sb("ck", [128, D])
            else:
                pq = [ps("pq%d" % i, [128, 4, 128]) for i in range(2)]
                stg = sb("stg", [128, 33, 128])

            def norm_T(sn, t0, nt, it):
                sq = seqs[sn]
                b = it % 2
                xk = 'xt%d' % b
                DMA('sp', xk, [], [xk], xt[b][:nt], sq['x'][t0:t0 + nt, :])
                OP('act', 'activation', [xk], ['junk', 'ss'], out=junk[:nt], in_=xt[b][:nt], func=AF.Square,
                   accum_out=ss[:nt])
                rsqrt_col(rstd, ss, 1.0 / D, NORM_EPS, ['ss'], ['rstd'], nt)
                OP('dve', 'scalar_tensor_tensor', [xk, 'rstd', 'gpre'], ['hb'], out=hb[:nt], in0=xt[b][:nt],
                   scalar=rstd[:nt, 0:1], in1=gpre[:nt], op0=ALU.mult, op1=ALU.mult)
                for c in range(8):
                    OP('pe', 'transpose', ['hb', 'idb'], ['pT'], out=pT[:, c, :nt],
                       in_=hb[:nt, c * 128:(c + 1) * 128], identity=idb[:nt, :nt])
                OP('act', 'copy', ['pT'], ['hT'], out=hT[:, :, :nt], in_=pT[:, :, :nt])

            def transposes_to(src_b, srckey, pst, stage, stagekey, nt):
                for h in range(8):
                    OP('pe', 'transpose', [srckey, 'idb'], ['pT2'], out=pst[:, h, :nt],
                       in_=src_b[:nt, h * 128:(h + 1) * 128], identity=idb[:nt, :nt])
                OP('act', 'copy', ['pT2'], [stagekey], out=stage[:, :, :nt], in_=pst[:, :, :nt])

            it = 0
            if which == 'A':
                sq = seqs['s']
                for t0 in range(0, PAST, 128):
                    DMA('sp', 'ck', [], ['ck'], ck[:], cache_k[t0:t0 + 128, :])
                    OP('pool', 'tensor_copy', ['ck'], ['kb'], out=kb[:], in_=ck[:])
                    transposes_to(kb, 'kb', pT2, kTs, 'kTs', 128)
                    DMA('sp', 'kTs_st', ['kTs'], [], sq['kT'][:, :, t0:t0 + 128].rearrange("h p t -> p h t"), kTs[:])
                    DMA('sp', 'ck', [], ['ck'], ck[:], cache_v[t0:t0 + 128, :])
                    OP('pool', 'tensor_copy', ['ck'], ['vbt'], out=vbt[:], in_=ck[:])
                    DMA('sp', 'vbt_st', ['vbt'], [], sq['vb'][t0:t0 + 128, :], vbt[:])
            for (sn, t0, nt) in all_tiles:
                sq = seqs[sn]
                norm_T(sn, t0, nt, it)
                if which == 'A':
                    DMA('pool', 'cs', [], ['cs'], cs[:nt], sq['cs'][t0:t0 + nt, :])
                    for grp in range(8):
                        pk = 'pp%d' % (grp % 3)
                        P = pp[grp % 3]
                        for kc in range(8):
                            OP('pe', 'matmul', ['hT', 'wres'], [pk], P[:nt, :], lhsT=hT[:, kc, :nt],
                               rhs=wres[:, kc, grp * 512:(grp + 1) * 512], start=(kc == 0), stop=(kc == 7))
                        half = (grp % 2) * 512
                        if grp < 6:
                            dst, dk = [(q_sb, 'q_sb'), (k_sb, 'k_sb'), (v_sb, 'v_sb')][grp // 2]
                            OP('act', 'copy', [pk], [dk], out=dst[:nt, half:half + 512], in_=P[:nt, :])
                        else:
                            OP('act', 'activation', [pk], ['gtmp'], out=gtmp[:nt], in_=P[:nt, :], func=AF.Exp, scale=-1.0)
                            OP('dve', 'tensor_scalar_add', ['gtmp'], ['gtmp'], out=gtmp[:nt], in0=gtmp[:nt], scalar1=1.0)
                            OP('dve', 'reciprocal', ['gtmp'], ['gtmp'], out=gtmp[:nt], in_=gtmp[:nt])
                            OP('dve', 'tensor_tensor', ['gtmp', pk], ['ga_sb'], out=ga_sb[:nt, half:half + 512],
                               in0=P[:nt, :], in1=gtmp[:nt], op=ALU.mult)
                        if grp in (1, 3):
                            X, xk = (q_sb, 'q_sb') if grp == 1 else (k_sb, 'k_sb')
                            Xv = X[:nt].rearrange("p (a d) -> p a d", d=64)
                            x1 = Xv[:, :, 0:8]; x2 = Xv[:, :, 8:16]
                            cosb = cs[:nt, 0:8].unsqueeze(1).to_broadcast([nt, 16, 8])
                            sinb = cs[:nt, 8:16].unsqueeze(1).to_broadcast([nt, 16, 8])
                            e = 'dve'
                            OP(e, 'tensor_tensor', [xk, 'cs'], ['rt'], out=rt[:nt, 0], in0=x1, in1=cosb, op=ALU.mult)
                            OP(e, 'tensor_tensor', [xk, 'cs'], ['rt'], out=rt[:nt, 1], in0=x2, in1=sinb, op=ALU.mult)
                            OP(e, 'tensor_tensor', [xk, 'cs'], ['rt'], out=rt[:nt, 2], in0=x2, in1=cosb, op=ALU.mult)
                            OP(e, 'tensor_tensor', [xk, 'cs'], ['rt'], out=rt[:nt, 3], in0=x1, in1=sinb, op=ALU.mult)
                            OP(e, 'tensor_tensor', ['rt'], [xk], out=x1, in0=rt[:nt, 0], in1=rt[:nt, 1], op=ALU.subtract)
                            OP(e, 'tensor_tensor', ['rt'], [xk], out=x2, in0=rt[:nt, 2], in1=rt[:nt, 3], op=ALU.add)
                            if grp == 1:
                                OP('pool', 'tensor_copy', ['q_sb'], ['qb'], out=qb[:nt], in_=q_sb[:nt])
                                transposes_to(qb, 'qb', pT2, qTs, 'qTs', nt)
                                DMA('sp', 'qTs_st', ['qTs'], [], sq['qT'][:, :, t0:t0 + nt].rearrange("h p t -> p h t"),
                                    qTs[:, :, :nt])
                            else:
                                DMA('sp', 'k_st', ['k_sb'], [], sq['k_out'][t0:t0 + nt, :], k_sb[:nt])
                                OP('pool', 'tensor_copy', ['k_sb'], ['kb'], out=kb[:nt], in_=k_sb[:nt])
                                transposes_to(kb, 'kb', pT2, kTs, 'kTs', nt)
                                ko = sq['koff'] + t0
                                DMA('sp', 'kTs_st', ['kTs'], [], sq['kT'][:, :, ko:ko + nt].rearrange("h p t -> p h t"),
                                    kTs[:, :, :nt])
                        if grp == 5:
                            DMA('sp', 'v_st', ['v_sb'], [], sq['v_out'][t0:t0 + nt, :], v_sb[:nt])
                            OP('pool', 'tensor_copy', ['v_sb'], ['vbt'], out=vbt[:nt], in_=v_sb[:nt])
                            ko = sq['koff'] + t0
                            DMA('sp', 'vbt_st', ['vbt'], [], sq['vb'][ko:ko + nt, :], vbt[:nt])
                        if grp == 7:
                            DMA('sp', 'ga_st', ['ga_sb'], [], sq['ga'][t0:t0 + nt, :], ga_sb[:nt])
                else:
                    for cq in range(9):
                        pk = 'pq%d' % (cq % 2)
                        P = pq[cq % 2]
                        ncq = min(4, 33 - cq * 4)
                        for j in range(ncq):
                            cc = cq * 4 + j
                            for kc in range(8):
                                OP('pe', 'matmul', ['hT', 'wres'], [pk], P[:, j, :nt], lhsT=wres[:, kc, cc * 128:(cc + 1) * 128],
                                   rhs=hT[:, kc, :nt], start=(kc == 0), stop=(kc == 7))
                        OP('act' if cq % 2 == 0 else 'dve', 'copy' if cq % 2 == 0 else 'tensor_copy', [pk], ['stg%d' % cq],
                           out=stg[:, cq * 4:cq * 4 + ncq, :nt], in_=P[:, :ncq, :nt])
                        DMA('sp', 'stg_st%d' % cq, ['stg%d' % cq], [], sq['psT'][cq * 4:cq * 4 + ncq, :, t0:t0 + nt].rearrange("c p t -> p c t"),
                            stg[:, cq * 4:cq * 4 + ncq, :nt])
                it += 1
        S.barrier()

    def phase2():
        if True:
            sb, ps = mk_alloc()
            NKT = (T + 127) // 128
            NKTS = (NKS + 127) // 128
            MAXKT = max(NKT, NKTS)
            MAXK = max(T, NKS)
            kT = [sb("kT%d" % i, [128, MAXK], BF16) for i in range(2)]
            qT = [sb("qT%d" % i, [128, T], BF16) for i in range(2)]
            V = [sb("V%d" % i, [128, MAXKT, 132], BF16) for i in range(2)]
            pt = [sb("pt%d" % i, [128, 2, 128], BF16) for i in range(3)]
            pss = [psum_all[:, (2 * i) * 512:(2 * i + 2) * 512].rearrange("p (c x) -> p c x", c=2)[:, :, 0:128] for i in range(2)]
            ps("r0", [128, 512]); ps("r1", [128, 512]); ps("r2", [128, 512]); ps("r3", [128, 512])
            acc = [[ps("acc%d%d" % (i, c), [128, 132]) for c in range(2)] for i in range(1)]
            pty = ps("pty", [128, 4, 128], BF16)
            idf = sb("idf", [128, 128]); idb = sb("idb", [128, 128], BF16)
            lamt = sb("lamt", [128, 4, 64]); lamp = sb("lamp", [128, 2, 64]); lams = sb("lams", [128, 2])
            neglam = sb("neglam", [128, 1])
            sublnb = sb("sublnb", [128, 128])
            gat = sb("gat", [128, 4, 128])
            rs_ = [sb("rs%d" % i, [128, 2]) for i in range(2)]; nl2_ = [sb("nl2%d" % i, [128, 1]) for i in range(2)]
            o1_ = [sb("o1%d" % i, [128, 128]) for i in range(2)]; o_ = [sb("o%d" % i, [128, 128]) for i in range(2)]
            junk_ = [sb("junk%d" % i, [128, 128]) for i in range(2)]
            ssq_ = [sb("ssq%d" % i, [128, 1]) for i in range(2)]; rstd_ = [sb("rstd%d" % i, [128, 1]) for i in range(2)]
            yb_ = [sb("yb%d" % i, [128, 128], BF16) for i in range(2)]
            pending = []
            epc = [0]

            def run_due(p):
                keep = []
                for it_ in pending:
                    if it_[1] <= p:
                        it_[2]()
                    else:
                        keep.append(it_)
                pending[:] = keep

            def flush_ep(kmax):
                keep = []
                for it_ in pending:
                    if it_[0] <= kmax:
                        it_[2]()
                    else:
                        keep.append(it_)
                pending[:] = keep
            ystage = sb("ystage", [128, 512], BF16)

            DMA('sp', 'idf', [], ['idf'], idf[:], c_ident)
            OP('dve', 'tensor_copy', ['idf'], ['idb'], out=idb[:], in_=idf[:])
            for i, l in enumerate([lam_q1, lam_k1, lam_q2, lam_k2]):
                DMA('sp', 'lamt', [], ['lamt'], lamt[:, i, :], l.partition_broadcast(128))
            DMA('sp', 'sublnb', [], ['sublnb'], sublnb[:], subln.partition_broadcast(128))
            OP('dve', 'tensor_scalar_mul', ['sublnb'], ['sublnb'], out=sublnb[:], in0=sublnb[:], scalar1=1.0 - LAMBDA_INIT)
            OP('dve', 'tensor_tensor', ['lamt'], ['lamp'], out=lamp[:, 0, :], in0=lamt[:, 0, :], in1=lamt[:, 1, :], op=ALU.mult)
            OP('dve', 'tensor_tensor', ['lamt'], ['lamp'], out=lamp[:, 1, :], in0=lamt[:, 2, :], in1=lamt[:, 3, :], op=ALU.mult)
            OP('dve', 'tensor_reduce', ['lamp'], ['lams'], out=lams[:], in_=lamp[:], axis=AX.X, op=ALU.add)
            OP('act', 'activation', ['lams'], ['lams'], out=lams[:], in_=lams[:], func=AF.Exp)
            OP('dve', 'tensor_tensor', ['lams'], ['neglam'], out=neglam[:], in0=lams[:, 1:2], in1=lams[:, 0:1], op=ALU.subtract)
            OP('dve', 'tensor_scalar_add', ['neglam'], ['neglam'], out=neglam[:], in0=neglam[:], scalar1=-LAMBDA_INIT)
            for i in range(2):
                OP('pool', 'memset', [], ['V%d' % i], V[i][:, :, 128:132], 1.0)

            jobs = [(sn, h) for sn in ('p', 's') for h in range(8)]
            sidx = 0
            qidx = 0
            for ji, (sn, h) in enumerate(jobs):
                sq = seqs[sn]
                b = ji % 2
                nkeys = sq['nkeys']; nq = sq['n']
                nkt = (nkeys + 127) // 128
                kk_, qk_, vk_ = 'kT%d' % b, 'qT%d' % b, 'V%d' % b
                DMA('sp', kk_, [], [kk_], kT[b][:, :nkeys], sq['kT'][h])
                DMA('sp', qk_, [], [qk_], qT[b][:, :nq], sq['qT'][h])
                nfull = nkeys // 128
                for k0 in range(0, nfull, 8):
                    k1 = min(nfull, k0 + 8)
                    DMA('pool', vk_, [], [vk_], V[b][:, k0:k1, 0:128],
                        sq['vb'][k0 * 128:k1 * 128, h * 128:(h + 1) * 128].rearrange("(kt p) d -> p kt d", p=128))
                if nkeys % 128:
                    r = nkeys % 128
                    DMA('pool', vk_, [], [vk_], V[b][:r, nfull, 0:128], sq['vb'][nfull * 128:nkeys, h * 128:(h + 1) * 128])
                nqt = (nq + 127) // 128
                pairs = []
                for qt in range(nqt):
                    jmax = qt if sn == 'p' else nkt - 1
                    for j in range(jmax + 1):
                        pairs.append((qt, j, jmax))

                def emit_qk(pidx):
                    qt, j, jmax = pairs[pidx]
                    q0 = qt * 128
                    nqs = min(128, nq - q0)
                    nk = min(128, nkeys - j * 128)
                    si = (sidx0 + pidx) % 2
                    for c in range(2):
                        OP('pe', 'matmul', [kk_, qk_], ['pss%d' % si], pss[si][:nk, c, :nqs],
                           lhsT=kT[b][c * 64:(c + 1) * 64, j * 128:j * 128 + nk],
                           rhs=qT[b][c * 64:(c + 1) * 64, q0:q0 + nqs], start=True, stop=True)
                sidx0 = sidx
                sidx += len(pairs)
                emit_qk(0)
                for pidx, (qt, j, jmax) in enumerate(pairs):
                    q0 = qt * 128
                    nqs = min(128, nq - q0)
                    nk = min(128, nkeys - j * 128)
                    ab = 0
                    ak = ['acc%d%d' % (ab, c) for c in range(2)]
                    si = (sidx0 + pidx) % 2
                    pi = (sidx0 + pidx) % 3
                    sk = 'pss%d' % si; pk = 'pt%d' % pi
                    if pidx + 1 < len(pairs):
                        emit_qk(pidx + 1)
                    OP('act', 'activation', [sk], [pk], out=pt[pi][:nk, :, :nqs], in_=pss[si][:nk, :, :nqs], func=AF.Exp, scale=0.125)
                    if sn == 'p' and j == qt:
                        OP('pool', 'memset', [], [pk], pt[pi][64:128, :, 0:64], 0.0)
                    for c in range(2):
                        OP('pe', 'matmul', [pk, vk_], [ak[c]], acc[ab][c][:nqs, 0:130],
                           lhsT=pt[pi][:nk, c, :nqs], rhs=V[b][:nk, j, 0:130], start=(j == 0), stop=(j == jmax))
                    run_due(pidx)
                    if j != jmax:
                        continue
                    ek = epc[0]
                    epc[0] += 1
                    flush_ep(ek - 2)
                    e2 = ek % 2
                    A1 = acc[ab][0][:nqs, :]; A2 = acc[ab][1][:nqs, :]
                    qi = qt % 4
                    if qi == 0:
                        nqg = min(512, nq - q0)
                        nsub = (nqg + 127) // 128
                        pp_ = min(128, nqg)
                        DMA('sp', 'gat', [], ['gat'], gat[:pp_, :nsub, :],
                            sq['ga'][q0:q0 + nqg, h * 128:(h + 1) * 128].rearrange("(s p) d -> p s d", p=pp_))
                        g0 = q0
                    rs = rs_[e2]; nl2 = nl2_[e2]; o1 = o1_[e2]; o = o_[e2]; junk = junk_[e2]; ssq = ssq_[e2]; rstd = rstd_[e2]; yb = yb_[e2]
                    kr, kn, ko1, ko, kj, ks_, krs, kyb = ['%s%d' % (n_, e2) for n_ in ('rs', 'nl2', 'o1', 'o', 'junk', 'ssq', 'rstd', 'yb')]
                    OP('dve', 'tensor_copy', [ak[0]], [kr], out=rs[:nqs, 0:1], in_=A1[:, 128:129])
                    OP('dve', 'tensor_copy', [ak[1], kr], [kr], out=rs[:nqs, 1:2], in_=A2[:, 128:129])
                    OP('dve', 'reciprocal', [kr], [kr], out=rs[:nqs, :], in_=rs[:nqs, :])
                    OP('dve', 'tensor_tensor', [kr, 'neglam'], [kn], out=nl2[:nqs], in0=rs[:nqs, 1:2], in1=neglam[:nqs], op=ALU.mult)
                    OP('dve', 'tensor_scalar_mul', [ak[0], kr], [ko1], out=o1[:nqs], in0=A1[:, 0:128], scalar1=rs[:nqs, 0:1])
                    OP('dve', 'scalar_tensor_tensor', [ak[1], kn, ko1], [ko], out=o[:nqs], in0=A2[:, 0:128],
                       scalar=nl2[:nqs, 0:1], in1=o1[:nqs], op0=ALU.mult, op1=ALU.add)

                    def st2(nqs=nqs, o=o, junk=junk, ssq=ssq, rstd=rstd, ko=ko, kj=kj, ks_=ks_, krs=krs):
                        OP('act', 'activation', [ko], [kj, ks_], out=junk[:nqs], in_=o[:nqs], func=AF.Square, accum_out=ssq[:nqs])
                        rsqrt_col(rstd, ssq, 1.0 / 128, SUBLN_EPS, [ks_], [krs], nqs)

                    def st3(nqs=nqs, o=o, rstd=rstd, yb=yb, ko=ko, krs=krs, kyb=kyb, qi=qi):
                        OP('dve', 'scalar_tensor_tensor', [ko, krs, 'sublnb'], [ko], out=o[:nqs], in0=o[:nqs], scalar=rstd[:nqs, 0:1],
                           in1=sublnb[:nqs], op0=ALU.mult, op1=ALU.mult)
                        OP('dve', 'tensor_tensor', [ko, 'gat'], [kyb], out=yb[:nqs], in0=o[:nqs], in1=gat[:nqs, qi, :], op=ALU.mult)

                    def st4(nqs=nqs, yb=yb, kyb=kyb, qi=qi):
                        OP('pe', 'transpose', [kyb, 'idb'], ['pty'], out=pty[:, qi, :nqs], in_=yb[:nqs, :], identity=idb[:nqs, :nqs])

                    def st5(nqs=nqs, qi=qi, qt=qt, q0=q0, g0=g0, h=h, sq=sq, nqt=nqt):
                        OP('act', 'copy', ['pty'], ['ystage'], out=ystage[:, qi * 128:qi * 128 + nqs], in_=pty[:, qi, :nqs])
                        if qi == 3 or qt == nqt - 1:
                            ng = q0 + nqs - g0
                            DMA('sp', 'ystage_st', ['ystage'], [], sq['mixT'][h, :, g0:g0 + ng], ystage[:, :ng])
                    pending.append((ek, pidx + 2, st2))
                    pending.append((ek, pidx + 4, st3))
                    pending.append((ek, pidx + 6, st4))
                    pending.append((ek, pidx + 7, st5))
                flush_ep(10 ** 9)
        S.barrier()

    def phase3():
        if True:
            sb, ps = mk_alloc()
            banks = banks_g

            idf = sb("idf", [128, 128]); bones = sb("bones", [128, 128])
            masks = sb("masks", [64, 4, 64])
            cvec = sb("cvec", [128, 11, 8])
            muwa = sb("muwa", [128, 1])
            LR = sb("LR", [128, D])
            rmask = sb("rmask", [128, D])
            psb = [sb("psb%d" % i, [128, 33, 129]) for i in range(2)]
            tmp = sb("tmp", [128, D]); tmp2 = sb("tmp2", [128, D])
            m_r = sb("m_r", [128, D]); m_k = sb("m_k", [128, D]); m_v = sb("m_v", [128, D])
            m_wa = sb("m_wa", [128, 128]); th = sb("th", [128, 128])
            ewt = sb("ewt", [128, D]); Lt = sb("Lt", [128, D]); E1 = sb("E1", [128, D])
            kk = sb("kk", [128, D]); alpha = sb("alpha", [128, D]); kmod = sb("kmod", [128, D])
            bonusT = sb("bonusT", [128, D]); sgB = sb("sgB", [128, D])
            pc = sb("pc", [128, 8, 2])
            AR = sb("AR", [128, 8, 2, 2, 64]); BK = sb("BK", [128, 8, 2, 2, 64])
            ARb = sb("ARb", [128, 8, 2, 2, 64], BF16); BKb = sb("BKb", [128, 8, 2, 2, 64], BF16)
            Vtm = sb("Vtm", [64, D], BF16); Btm = sb("Btm", [64, D], BF16); Ktm = sb("Ktm", [64, D], BF16)
            XX = [sb("XX%d" % i, [64, 8, 2, 64], BF16) for i in range(2)]
            TT = sb("TT", [64, 8, 64], BF16)
            ARBT = sb("ARBT", [64, 8, 64], BF16); AKT = sb("AKT", [64, 8, 64], BF16); ARKT = sb("ARKT", [64, 8, 64], BF16)
            Wsb = sb("Wsb", [64, 8, 64], BF16); Usb = sb("Usb", [64, 8, 64], BF16)
            H = sb("H", [128, 8, 64])
            Ytm = sb("Ytm", [64, 16, 64]); Ysq = sb("Ysq", [64, 16, 64])
            gst = sb("gst", [64, 4, 16])
            yo = sb("yo", [128, 8, 64]); yob = sb("yob", [128, 8, 64], BF16)
            Ssb = sb("Ssb", [64, 16, 64])

            def v3(t, nt, g0=0, g1=8):
                return t[:, g0 * nt:g1 * nt].rearrange("p (g t) -> p g t", t=nt)

            def cbc(i, nt):
                return cvec[:, i, :].unsqueeze(2).to_broadcast([128, 8, nt])

            DMA('sp', 'idf', [], ['idf'], idf[:], c_ident)
            DMA('sp', 'bones', [], ['bones'], bones[:], c_bones)
            DMA('sp', 'masks', [], ['masks'], masks[:], c_masks)
            for i in range(3):
                DMA('sp', 'cvec', [], ['cvec'], cvec[:, i, :], mu_shift[i * 1024:(i + 1) * 1024].rearrange("(g p) -> p g", p=128), slow=True)
            for i, vec in enumerate([w0, a0, k_k, k_a, r_k, ln_x_w, ln_x_b]):
                DMA('sp', 'cvec', [], ['cvec'], cvec[:, 3 + i, :], vec.rearrange("(g p) -> p g", p=128), slow=True)
            DMA('sp', 'muwa', [], ['muwa'], muwa[:], mu_shift[3072:3200].rearrange("(p o) -> p o", o=1), slow=True)
            DMA('sp', 'LR', [], ['LR'], LR[0:64, :], w_up)
            DMA('sp', 'LR', [], ['LR'], LR[64:128, :], a_up)
            OP('pool', 'memset', [], ['rmask'], rmask[:], 1.0)
            OP('pool', 'memset', ['rmask'], ['rmask'], rmask[:].rearrange("p (a b) -> p a b", b=64)[:, :, 0:1], 0.0)
            CI = dict(mu_r=0, mu_k=1, mu_v=2, w0=3, a0=4, k_k=5, k_a=6, r_k=7, lnw=8, lnb=9)
            m_cr_lt = masks[:, 0, :]; m_rc_lt = masks[:, 1, :]; m_rc_le = masks[:, 2, :]; id64 = masks[:, 3, :]

            def bc4(m):
                return m.unsqueeze(1).to_broadcast([64, 8, 64])

            it = 0
            for sn in ('p', 's'):
                sq = seqs[sn]
                n = sq['n']
                for c0 in range(0, 25, 5):
                    DMA('pool', 'shift_st', [], [], sq['shift_out'][c0 * 128:(c0 + 5) * 128].rearrange("(c p o) -> c p o", p=128, o=1),
                        sq['psT'][c0:c0 + 5, :, n - 1:n], slow=True)
                if sn == 'p':
                    OP('pool', 'memset', [], ['H0', 'H1', 'H2', 'H3'], H[:], 0.0)
                else:
                    DMA('sp', 'Ssb', [], ['Ssb'], Ssb[:], state_wkv.rearrange("h i j -> i h j"))
                    for g in range(8):
                        OP('pe', 'transpose', ['Ssb', 'idf'], ['bk0'], out=banks[0][:, g * 64:(g + 1) * 64],
                           in_=Ssb[:, 2 * g:2 * g + 2, :].rearrange("p a t -> p (a t)"), identity=idf[:64, :64])
                    OP('dve', 'tensor_copy', ['bk0'], ['H0', 'H1', 'H2', 'H3'], out=H[:], in_=banks[0][:, :].rearrange("p (g i) -> p g i", i=64))
                for (_, t0, nt) in tiles_of(sn):
                    b = it % 2
                    it += 1
                    pk = 'psb%d' % b
                    P = psb[b]
                    nch = nt // 64
                    for c0 in range(0, 33, 4):
                        c1 = min(33, c0 + 4)
                        if t0 == 0:
                            DMA('sp', pk, [], [pk], P[:, c0:c1, 1:1 + nt], sq['psT'][c0:c1, :, 0:nt].rearrange("c p t -> p c t"))
                        else:
                            DMA('sp', pk, [], [pk], P[:, c0:c1, 0:1 + nt], sq['psT'][c0:c1, :, t0 - 1:t0 + nt].rearrange("c p t -> p c t"))
                    if t0 == 0:
                        if sn == 'p':
                            OP('pool', 'memset', [], [pk], P[:, :, 0:1], 0.0)
                        else:
                            for c0 in range(0, 25, 5):
                                DMA('sp', pk, [], [pk], P[:, c0:c0 + 5, 0:1], state_shift[c0 * 128:(c0 + 5) * 128].rearrange("(c p o) -> p c o", p=128, o=1), slow=True)
                    for X, (M, mk) in enumerate([(m_r, 'm_r'), (m_k, 'm_k'), (m_v, 'm_v')]):
                        cur = P[:, 8 * X:8 * X + 8, 1:1 + nt]; prev = P[:, 8 * X:8 * X + 8, 0:nt]
                        e = 'pool' if X == 1 else 'dve'
                        OP(e, 'tensor_tensor', [pk], [mk], out=v3(M, nt), in0=prev, in1=cur, op=ALU.subtract)
                        OP(e, 'tensor_tensor', [mk, 'cvec'], [mk], out=v3(M, nt), in0=v3(M, nt), in1=cbc(X, nt), op=ALU.mult)
                        OP(e, 'tensor_tensor', [mk, pk], [mk], out=v3(M, nt), in0=v3(M, nt), in1=cur, op=ALU.add)
                    OP('dve', 'tensor_tensor', [pk], ['m_wa'], out=m_wa[:, :nt], in0=P[:, 24, 0:nt], in1=P[:, 24, 1:1 + nt], op=ALU.subtract)
                    OP('dve', 'scalar_tensor_tensor', ['m_wa', 'muwa', pk], ['m_wa'], out=m_wa[:, :nt], in0=m_wa[:, :nt], scalar=muwa[:, 0:1],
                       in1=P[:, 24, 1:1 + nt], op0=ALU.mult, op1=ALU.add)
                    OP('act', 'activation', [pk], ['tmp2'], out=v3(tmp2, nt), in_=P[:, 25:33, 1:1 + nt], func=AF.Exp, scale=-1.0)
                    OP('pool', 'tensor_scalar_add', ['tmp2'], ['tmp2'], out=tmp2[:, :8 * nt], in0=tmp2[:, :8 * nt], scalar1=1.0)
                    OP('dve', 'reciprocal', ['tmp2'], ['tmp2'], out=tmp2[:, :8 * nt], in_=tmp2[:, :8 * nt])
                    OP('pool', 'tensor_tensor', ['tmp2', pk], ['sgB'], out=v3(sgB, nt), in0=v3(tmp2, nt), in1=P[:, 25:33, 1:1 + nt], op=ALU.mult)
                    OP('act', 'activation', ['m_wa'], ['th'], out=th[0:64, :nt], in_=m_wa[0:64, :nt], func=AF.Exp, scale=-2.0)
                    OP('dve', 'tensor_scalar_add', ['th'], ['th'], out=th[0:64, :nt], in0=th[0:64, :nt], scalar1=1.0)
                    OP('dve', 'reciprocal', ['th'], ['th'], out=th[0:64, :nt], in_=th[0:64, :nt])
                    OP('dve', 'tensor_scalar', ['th'], ['th'], out=th[0:64, :nt], in0=th[0:64, :nt], scalar1=2.0, scalar2=-1.0, op0=ALU.mult, op1=ALU.add)
                    pw = [banks[0], banks[1]]

                    def pwv(nt):
                        return [banks[0][:, :4 * nt].rearrange("p (g t) -> p g t", t=nt), banks[1][:, :4 * nt].rearrange("p (g t) -> p g t", t=nt)]
                    pv = pwv(nt)
                    for g in range(8):
                        OP('pe', 'matmul', ['LR', 'th'], ['bk%d' % (g // 4)], pv[g // 4][:, g % 4, :], lhsT=LR[0:64, g * 128:(g + 1) * 128],
                           rhs=th[0:64, :nt], start=True, stop=True)
                    for hf in range(2):
                        OP('dve', 'tensor_tensor', ['bk%d' % hf, 'cvec'], ['ewt'], out=v3(ewt, nt, 4 * hf, 4 * hf + 4), in0=pv[hf],
                           in1=cvec[:, CI['w0'], 4 * hf:4 * hf + 4].unsqueeze(2).to_broadcast([128, 4, nt]), op=ALU.add)
                    for g in range(8):
                        OP('pe', 'matmul', ['LR', 'm_wa'], ['bk%d' % (g // 4)], pv[g // 4][:, g % 4, :], lhsT=LR[64:128, g * 128:(g + 1) * 128],
                           rhs=m_wa[64:128, :nt], start=True, stop=True)
                    for hf in range(2):
                        OP('dve', 'tensor_tensor', ['bk%d' % hf, 'cvec'], ['alpha'], out=v3(alpha, nt, 4 * hf, 4 * hf + 4), in0=pv[hf],
                           in1=cvec[:, CI['a0'], 4 * hf:4 * hf + 4].unsqueeze(2).to_broadcast([128, 4, nt]), op=ALU.add)
                    W8 = 8 * nt
                    OP('act', 'activation', ['ewt'], ['ewt'], out=ewt[:, :W8], in_=ewt[:, :W8], func=AF.Exp, scale=-1.0)
                    OP('act', 'activation', ['ewt'], ['ewt'], out=ewt[:, :W8], in_=ewt[:, :W8], func=AF.Ln, bias=1.0)
                    OP('act', 'activation', ['ewt'], ['ewt'], out=ewt[:, :W8], in_=ewt[:, :W8], func=AF.Exp, scale=-1.0, bias=-0.5)
                    OP('act', 'activation', ['alpha'], ['alpha'], out=alpha[:, :W8], in_=alpha[:, :W8], func=AF.Exp, scale=-1.0)
                    OP('pool', 'tensor_scalar_add', ['alpha'], ['alpha'], out=alpha[:, :W8], in0=alpha[:, :W8], scalar1=1.0)
                    OP('dve', 'reciprocal', ['alpha'], ['alpha'], out=alpha[:, :W8], in_=alpha[:, :W8])
                    OP('pool', 'tensor_scalar_mul', ['ewt'], ['tmp'], out=tmp[:, :W8], in0=ewt[:, :W8], scalar1=-1.0)
                    OP('dve', 'tensor_tensor_scan', ['tmp', 'rmask'], ['Lt'], out=Lt[:, :W8], data0=rmask[:, :W8], data1=tmp[:, :W8], initial=0.0,
                       op0=ALU.mult, op1=ALU.add)
                    OP('dve', 'tensor_tensor', ['m_k', 'cvec'], ['kk'], out=v3(kk, nt), in0=v3(m_k, nt), in1=cbc(CI['k_k'], nt), op=ALU.mult)
                    OP('pool', 'tensor_tensor', ['kk'], ['tmp'], out=tmp[:, :W8], in0=kk[:, :W8], in1=kk[:, :W8], op=ALU.mult)
                    for hf in range(2):
                        OP('pe', 'matmul', ['bones', 'tmp'], ['bk%d' % hf], banks[hf][:, :4 * nt], lhsT=bones[:], rhs=tmp[:, hf * 4 * nt:(hf + 1) * 4 * nt],
                           start=True, stop=True)
                        OP('dve', 'tensor_scalar_max', ['bk%d' % hf], ['tmp2'], out=tmp2[:, hf * 4 * nt:(hf + 1) * 4 * nt], in0=banks[hf][:, :4 * nt], scalar1=1e-24)
                    OP('act', 'activation', ['tmp2'], ['tmp2'], out=tmp2[:, :W8], in_=tmp2[:, :W8], func=AF.Ln)
                    OP('act', 'activation', ['tmp2'], ['tmp2'], out=tmp2[:, :W8], in_=tmp2[:, :W8], func=AF.Exp, scale=-0.5)
                    OP('dve', 'tensor_tensor', ['kk', 'tmp2'], ['kk'], out=kk[:, :W8], in0=kk[:, :W8], in1=tmp2[:, :W8], op=ALU.mult)
                    OP('dve', 'scalar_tensor_tensor', ['alpha', 'cvec'], ['kmod'], out=v3(kmod, nt), in0=v3(alpha, nt), scalar=-1.0, in1=cbc(CI['k_a'], nt),
                       op0=ALU.add, op1=ALU.mult)
                    OP('dve', 'scalar_tensor_tensor', ['kmod', 'm_k'], ['kmod'], out=kmod[:, :W8], in0=kmod[:, :W8], scalar=1.0, in1=m_k[:, :W8],
                       op0=ALU.add, op1=ALU.mult)
                    OP('pool', 'tensor_tensor', ['m_r', 'kmod'], ['tmp'], out=tmp[:, :W8], in0=m_r[:, :W8], in1=kmod[:, :W8], op=ALU.mult)
                    OP('pool', 'tensor_tensor', ['tmp', 'cvec'], ['tmp'], out=v3(tmp, nt), in0=v3(tmp, nt), in1=cbc(CI['r_k'], nt), op=ALU.mult)
                    for hf in range(2):
                        OP('pe', 'matmul', ['bones', 'tmp'], ['bk%d' % hf], banks[hf][:, :4 * nt], lhsT=bones[:], rhs=tmp[:, hf * 4 * nt:(hf + 1) * 4 * nt],
                           start=True, stop=True)
                        OP('dve', 'tensor_tensor', ['bk%d' % hf, 'm_v'], ['bonusT'], out=bonusT[:, hf * 4 * nt:(hf + 1) * 4 * nt], in0=banks[hf][:, :4 * nt],
                           in1=m_v[:, hf * 4 * nt:(hf + 1) * 4 * nt], op=ALU.mult)
                    def v4(t):
                        return t[:, :W8].rearrange("p (g c t) -> p g c t", g=8, t=64)
                    OP('act', 'activation', ['Lt'], ['E1'], out=E1[:, :W8], in_=Lt[:, :W8], func=AF.Exp)
                    OP('dve', 'tensor_tensor', ['m_r', 'E1'], ['AR'], out=AR[:, :, :nch, 1, :], in0=v4(m_r), in1=v4(E1), op=ALU.mult)
                    OP('pool', 'tensor_copy', ['E1'], ['pc'], out=pc[:, :, :nch], in_=v4(E1)[:, :, :, 63])
                    OP('dve', 'tensor_tensor', ['Lt', 'ewt'], ['ewt'], out=ewt[:, :W8], in0=Lt[:, :W8], in1=ewt[:, :W8], op=ALU.add)
                    OP('act', 'activation', ['ewt'], ['ewt'], out=ewt[:, :W8], in_=ewt[:, :W8], func=AF.Exp)
                    OP('dve', 'scalar_tensor_tensor', ['kk', 'ewt'], ['AR'], out=AR[:, :, :nch, 0, :], in0=v4(kk), scalar=-1.0, in1=v4(ewt),
                       op0=ALU.mult, op1=ALU.mult)
                    OP('act', 'activation', ['Lt'], ['Lt'], out=Lt[:, :W8], in_=Lt[:, :W8], func=AF.Exp, scale=-1.0)
                    OP('pool', 'tensor_tensor', ['kk', 'alpha'], ['tmp'], out=tmp[:, :W8], in0=kk[:, :W8], in1=alpha[:, :W8], op=ALU.mult)
                    OP('dve', 'tensor_tensor', ['tmp', 'Lt'], ['BK'], out=BK[:, :, :nch, 0, :], in0=v4(tmp), in1=v4(Lt), op=ALU.mult)
                    OP('dve', 'tensor_tensor', ['kmod', 'Lt'], ['BK'], out=BK[:, :, :nch, 1, :], in0=v4(kmod), in1=v4(Lt), op=ALU.mult)
                    OP('pool', 'tensor_copy', ['AR'], ['ARb'], out=ARb[:, :, :nch], in_=AR[:, :, :nch])
                    OP('pool', 'tensor_copy', ['BK'], ['BKb'], out=BKb[:, :, :nch], in_=BK[:, :, :nch])

                    for ch in range(nch):
                        for (src, skey, dst, dkey) in [(None, 'm_v', Vtm, 'Vtm'), (0, 'BK', Btm, 'Btm'), (1, 'BK', Ktm, 'Ktm')]:
                            for g in range(8):
                                if src is None:
                                    in_ = v3(m_v, nt)[:, g, ch * 64:(ch + 1) * 64]
                                else:
                                    in_ = BK[:, g, ch, src, :]
                                OP('pe', 'transpose', [skey, 'idf'], ['bk%d' % (2 + g // 4)], out=banks[2 + g // 4][0:64, (g % 4) * 128:(g % 4 + 1) * 128],
                                   in_=in_, identity=idf[:])
                            OP('act', 'copy', ['bk2'], [dkey], out=dst[:, 0:512], in_=banks[2][0:64, :])
                            OP('dve', 'tensor_copy', ['bk3'], [dkey], out=dst[:, 512:1024], in_=banks[3][0:64, :])
                        Ytm4 = Ytm[:].rearrange("p (g a) t -> p g a t", a=2)
                        for hb in range(2):
                            par = hb
                            heads = [2 * g + par for g in range(8)]
                            hp = par * 64
                            sl = slice(hp, hp + 64)

                            def pv_(b0, nb, pat, **kw):
                                return psum_all[0:64, b0 * 512:(b0 + nb) * 512].rearrange(pat, **kw)
                            pA = pv_(4, 1, "p (h s) -> p h s", s=64)
                            pB = pv_(5, 2, "p (h s) -> p h s", s=128)
                            pC = pv_(2, 2, "p (h s) -> p h s", s=128)
                            pX = pv_(5, 2, "p (h c s) -> p h c s", c=2, s=64)
                            pTT = pv_(7, 1, "p (h s) -> p h s", s=64)
                            for i, h in enumerate(heads):
                                OP('pe', 'matmul', ['ARb', 'BKb'], ['bk4'], pA[:, i, :], lhsT=ARb[sl, i, ch, 0, :], rhs=BKb[sl, i, ch, 0, :], start=True, stop=True)
                            for i, h in enumerate(heads):
                                OP('pe', 'matmul', ['ARb', 'BKb'], ['bk5', 'bk6'], pB[:, i, :], lhsT=BKb[sl, i, ch, 0, :],
                                   rhs=ARb[sl, i, ch, :, :].rearrange("p a t -> p (a t)"), start=True, stop=True)
                            for i, h in enumerate(heads):
                                OP('pe', 'matmul', ['ARb', 'BKb'], ['bk2', 'bk3'], pC[:, i, :], lhsT=BKb[sl, i, ch, 1, :],
                                   rhs=ARb[sl, i, ch, :, :].rearrange("p a t -> p (a t)"), start=True, stop=True)
                            OP('dve', 'tensor_tensor', ['bk4', 'masks'], ['XX0'], out=XX[0][:, :, 0, :], in0=pA, in1=bc4(m_cr_lt), op=ALU.mult)
                            OP('dve', 'tensor_tensor', ['bk5', 'bk6', 'masks'], ['XX0'], out=XX[0][:, :, 1, :], in0=pB[:, :, 0:64], in1=bc4(m_rc_lt), op=ALU.mult)
                            OP('dve', 'tensor_tensor', ['bk5', 'bk6', 'masks'], ['ARBT'], out=ARBT[:], in0=pB[:, :, 64:128], in1=bc4(m_rc_le), op=ALU.mult)
                            OP('dve', 'tensor_tensor', ['bk2', 'bk3', 'masks'], ['AKT'], out=AKT[:], in0=pC[:, :, 0:64], in1=bc4(m_rc_lt), op=ALU.mult)
                            OP('dve', 'tensor_tensor', ['bk2', 'bk3', 'masks'], ['ARKT'], out=ARKT[:], in0=pC[:, :, 64:128], in1=bc4(m_rc_le), op=ALU.mult)
                            OP('dve', 'tensor_tensor', ['XX0', 'masks'], ['TT'], out=TT[:], in0=XX[0][:, :, 1, :], in1=bc4(id64), op=ALU.add)
                            for k in range(1, 6):
                                xo = XX[(k - 1) % 2]; xn = XX[k % 2]
                                xok = 'XX%d' % ((k - 1) % 2); xnk = 'XX%d' % (k % 2)
                                for i in range(8):
                                    OP('pe', 'matmul', [xok], ['bk5', 'bk6'], pX[:, i, 0, :], lhsT=xo[:, i, 1, :], rhs=xo[:, i, 0, :], start=True, stop=True)
                                    if k < 5:
                                        OP('pe', 'matmul', [xok], ['bk5', 'bk6'], pX[:, i, 1, :], lhsT=xo[:, i, 0, :], rhs=xo[:, i, 1, :], start=True, stop=True)
                                if k < 5:
                                    OP('act', 'copy', ['bk5', 'bk6'], [xnk], out=xn[:], in_=pX)
                                else:
                                    OP('act', 'copy', ['bk5', 'bk6'], [xnk], out=xn[:, :, 0, :], in_=pX[:, :, 0, :])
                                for i in range(8):
                                    OP('pe', 'matmul', [xnk, 'TT'], ['bk7'], pTT[:, i, :], lhsT=xn[:, i, 0, :], rhs=TT[:, i, :], start=True, stop=True)
                                OP('dve', 'tensor_tensor', ['bk7', 'TT'], ['TT'], out=TT[:], in0=pTT, in1=TT[:], op=ALU.add)
                            hk = 'H%d' % hb
                            pW1 = pv_(4, 1, "p (h s) -> p h s", s=64)
                            pW2 = pv_(7, 1, "p (h s) -> p h s", s=64)
                            pU = pv_(5, 1, "p (h s) -> p h s", s=64)
                            pY1 = pv_(4, 1, "p (h s) -> p h s", s=64)
                            pY2 = pv_(6, 1, "p (h s) -> p h s", s=64)
                            pH = psum_all[:, 7 * 512:8 * 512].rearrange("p (h s) -> p h s", s=64)
                            for i, h in enumerate(heads):
                                OP('pe', 'matmul', ['AR', hk], ['bk4'], pW1[:, i, :], lhsT=AR[sl, i, ch, 0, :], rhs=H[sl, i, :], start=True, stop=True)
                            for i, h in enumerate(heads):
                                OP('pe', 'matmul', ['AKT', 'Vtm'], ['bk7'], pW2[:, i, :], lhsT=AKT[:, i, :], rhs=Vtm[:, h * 64:(h + 1) * 64], start=True, stop=True)
                            OP('act', 'copy', ['bk4'], ['Wsb'], out=Wsb[:], in_=pW1)
                            OP('dve', 'tensor_tensor', ['bk7', 'Wsb'], ['Wsb'], out=Wsb[:], in0=pW2, in1=Wsb[:], op=ALU.add)
                            for i, h in enumerate(heads):
                                OP('pe', 'matmul', ['TT', 'Wsb'], ['bk5'], pU[:, i, :], lhsT=TT[:, i, :], rhs=Wsb[:, i, :], start=True, stop=True)
                            OP('act', 'copy', ['bk5'], ['Usb'], out=Usb[:], in_=pU)
                            for i, h in enumerate(heads):
                                OP('pe', 'matmul', ['AR', hk], ['bk4'], pY1[:, i, :], lhsT=AR[sl, i, ch, 1, :], rhs=H[sl, i, :], start=True, stop=True)
                            for i, h in enumerate(heads):
                                OP('pe', 'matmul', ['ARBT', 'Usb'], ['bk6'], pY2[:, i, :], lhsT=ARBT[:, i, :], rhs=Usb[:, i, :], start=True, stop=False)
                                OP('pe', 'matmul', ['ARKT', 'Vtm'], ['bk6'], pY2[:, i, :], lhsT=ARKT[:, i, :], rhs=Vtm[:, h * 64:(h + 1) * 64], start=False, stop=True)
                            OP('act', 'copy', ['bk4'], ['Ysq'], out=Ysq[:, 0:8, :], in_=pY1)
                            OP('dve', 'tensor_tensor', ['bk6', 'Ysq'], ['Ytm'], out=Ytm4[:, :, par, :], in0=pY2, in1=Ysq[:, 0:8, :], op=ALU.add)
                            for i, h in enumerate(heads):
                                OP('pe', 'matmul', ['Btm', 'Usb'], ['bk7'], pH[:, i, :], lhsT=Btm[:, i * 128:(i + 1) * 128],
                                   rhs=Usb[:, i, :], start=True, stop=False)
                                OP('pe', 'matmul', ['Ktm', 'Vtm'], ['bk7'], pH[:, i, :], lhsT=Ktm[:, i * 128:(i + 1) * 128],
                                   rhs=Vtm[:, h * 64:(h + 1) * 64], start=False, stop=True)
                            Hs = H[sl, :, :]
                            OP('dve', 'tensor_tensor', ['bk7', hk], [hk], out=Hs, in0=pH[sl, :, :], in1=Hs, op=ALU.add)
                            OP('dve', 'tensor_tensor', [hk, 'pc'], [hk], out=Hs, in0=Hs,
                               in1=pc[sl, :, ch:ch + 1].to_broadcast([64, 8, 64]), op=ALU.mult)
                        OP('dve', 'tensor_reduce', ['Ytm'], ['gst'], out=gst[:, 0, :], in_=Ytm[:], axis=AX.X, op=ALU.add)
                        OP('pool', 'tensor_tensor', ['Ytm'], ['Ysq'], out=Ysq[:], in0=Ytm[:], in1=Ytm[:], op=ALU.mult)
                        OP('dve', 'tensor_reduce', ['Ysq'], ['gst'], out=gst[:, 1, :], in_=Ysq[:], axis=AX.X, op=ALU.add)
                        OP('dve', 'tensor_scalar_mul', ['gst'], ['gst'], out=gst[:, 0:2, :], in0=gst[:, 0:2, :], scalar1=1.0 / 64)
                        OP('dve', 'tensor_tensor', ['gst'], ['gst'], out=gst[:, 2, :], in0=gst[:, 0, :], in1=gst[:, 0, :], op=ALU.mult)
                        OP('dve', 'tensor_tensor', ['gst'], ['gst'], out=gst[:, 3, :], in0=gst[:, 1, :], in1=gst[:, 2, :], op=ALU.subtract)
                        OP('act', 'activation', ['gst'], ['gst'], out=gst[:, 3, :], in_=gst[:, 3, :], func=AF.Ln, bias=GN_EPS)
                        OP('act', 'activation', ['gst'], ['gst'], out=gst[:, 3, :], in_=gst[:, 3, :], func=AF.Exp, scale=-0.5)
                        OP('dve', 'tensor_tensor', ['Ytm', 'gst'], ['Ytm'], out=Ytm[:], in0=Ytm[:], in1=gst[:, 0, :].unsqueeze(2).to_broadcast([64, 16, 64]), op=ALU.subtract)
                        OP('dve', 'tensor_tensor', ['Ytm', 'gst'], ['Ytm'], out=Ytm[:], in0=Ytm[:], in1=gst[:, 3, :].unsqueeze(2).to_broadcast([64, 16, 64]), op=ALU.mult)
                        pYT = banks[0][:, :].rearrange("p (g t) -> p g t", t=64)
                        for g in range(8):
                            OP('pe', 'transpose', ['Ytm', 'idf'], ['bk0'], out=pYT[:, g, :], in_=Ytm[:, 2 * g:2 * g + 2, :].rearrange("p a t -> p (a t)"), identity=idf[:64, :64])
                        csl = slice(ch * 64, (ch + 1) * 64)
                        OP('dve', 'tensor_tensor', ['bk0', 'cvec'], ['yo'], out=yo[:], in0=pYT, in1=cbc(CI['lnw'], 64), op=ALU.mult)
                        OP('dve', 'tensor_tensor', ['yo', 'cvec'], ['yo'], out=yo[:], in0=yo[:], in1=cbc(CI['lnb'], 64), op=ALU.add)
                        OP('dve', 'tensor_tensor', ['yo', 'bonusT'], ['yo'], out=yo[:], in0=yo[:], in1=v3(bonusT, nt)[:, :, csl], op=ALU.add)
                        OP('dve', 'tensor_tensor', ['yo', 'sgB'], ['yob'], out=yob[:], in0=yo[:], in1=v3(sgB, nt)[:, :, csl], op=ALU.mult)
                        DMA('sp', 'yob_st', ['yob'], [], sq['mixT'][8:16, :, t0 + ch * 64:t0 + (ch + 1) * 64].rearrange("c p t -> p c t"), yob[:])
                for g in range(8):
                    OP('pe', 'transpose', ['H0', 'H1', 'H2', 'H3', 'idf'], ['bk1'], out=banks[1][0:64, (g % 4) * 128:(g % 4 + 1) * 128], in_=H[:, g, :], identity=idf[:])
                    if g % 4 == 3:
                        OP('dve', 'tensor_copy', ['bk1'], ['Ssb'], out=Ssb[:, (g - 3) * 2:(g + 1) * 2, :], in_=banks[1][0:64, :].rearrange("p (h j) -> p h j", j=64))
                DMA('sp', 'Ssb_st', ['Ssb'], [], sq['wkv_out'].rearrange("h i j -> i h j"), Ssb[:])
        S.barrier()

    def phase4():
        if True:
            sb, ps = mk_alloc()
            wo = sb("wo", [128, 16, D], BF16)
            wst = [sb("wst%d" % i, [128, D]) for i in range(2)]
            gpost = sb("gpost", [128, D])
            xt = [sb("xt%d" % i, [128, D]) for i in range(2)]
            mx = [sb("mx%d" % i, [128, 16, 128], BF16) for i in range(2)]
            po = [[ps("po%d%d" % (i, j), [128, 512]) for j in range(2)] for i in range(2)]
            junk = sb("junk", [128, 512]); ss = sb("ss", [128, 2]); rstd = sb("rstd", [128, 1])
            yt = [sb("yt%d" % i, [128, D]) for i in range(2)]
            DMA('sp', 'gpost', [], ['gpost'], gpost[:], norm_post.partition_broadcast(128))
            for kc in range(16):
                b = kc % 2
                DMA('sp' if b == 0 else 'pool', 'wst%d' % b, [], ['wst%d' % b], wst[b][:], w_out[kc * 128:(kc + 1) * 128, :])
                OP('dve' if b == 0 else 'pool', 'tensor_copy', ['wst%d' % b], ['wo'], out=wo[:, kc, :], in_=wst[b][:])
            for it, (sn, t0, nt) in enumerate(all_tiles):
                sq = seqs[sn]
                b = it % 2
                DMA('sp', 'xt%d' % b, [], ['xt%d' % b], xt[b][:nt], sq['x'][t0:t0 + nt, :])
                for c0 in (0, 8):
                    DMA('pool', 'mx%d' % b, [], ['mx%d' % b], mx[b][:, c0:c0 + 8, :nt], sq['mixT'][c0:c0 + 8, :, t0:t0 + nt].rearrange("c p t -> p c t"))
                for hf in range(2):
                    for kc in range(16):
                        OP('pe', 'matmul', ['mx%d' % b, 'wo'], ['po%d%d' % (b, hf)], po[b][hf][:nt, :], lhsT=mx[b][:, kc, :nt],
                           rhs=wo[:, kc, hf * 512:(hf + 1) * 512], start=(kc == 0), stop=(kc == 15))
                    OP('act', 'activation', ['po%d%d' % (b, hf)], ['junk', 'ss'], out=junk[:nt], in_=po[b][hf][:nt, :], func=AF.Square,
                       accum_out=ss[:nt, hf:hf + 1])
                OP('dve', 'tensor_tensor', ['ss'], ['ss'], out=ss[:nt, 0:1], in0=ss[:nt, 0:1], in1=ss[:nt, 1:2], op=ALU.add)
                rsqrt_col(rstd, ss[:, 0:1], 1.0 / D, NORM_EPS, ['ss'], ['rstd'], nt)
                for hf in range(2):
                    OP('dve', 'scalar_tensor_tensor', ['po%d%d' % (b, hf), 'rstd', 'gpost'], ['yt%d' % b], out=yt[b][:nt, hf * 512:(hf + 1) * 512],
                       in0=po[b][hf][:nt, :], scalar=rstd[:nt, 0:1], in1=gpost[:nt, hf * 512:(hf + 1) * 512], op0=ALU.mult, op1=ALU.mult)
                OP('pool', 'tensor_tensor', ['yt%d' % b, 'xt%d' % b], ['yt%d' % b], out=yt[b][:nt], in0=yt[b][:nt], in1=xt[b][:nt], op=ALU.add)
                DMA('sp', 'yt%d_st' % b, ['yt%d' % b], [], sq['y'][t0:t0 + nt, :], yt[b][:nt])

    import os
    ph = os.environ.get('KPH', '1A,1B,2,3,4').split(',')
    if '1A' in ph:
        phase1('A')
    if '1B' in ph:
        phase1('B')
    if '2' in ph:
        phase2()
    if '3' in ph:
        phase3()
    if '4' in ph:
        phase4()
    print('NREC', S.nrec, flush=True)
    S.emit()
    gstack.close()
    return nc


def _consts(T):
    ident = np.eye(128, dtype=np.float32)
    bones = np.zeros((128, 128), np.float32)
    bones[:64, :64] = 1.0
    bones[64:, 64:] = 1.0
    r = np.arange(64)[:, None]; c = np.arange(64)[None, :]
    masks = np.stack([(c < r), (r < c), (r <= c), (r == c)], axis=1).astype(np.float32)

    def cs(pos):
        inv = np.power(np.float32(500000.0), -np.arange(8, dtype=np.float32) * np.float32(2.0 / 16)).astype(np.float32)
        ang = pos.astype(np.float32)[:, None] * inv[None, :]
        return np.concatenate([np.cos(ang), np.sin(ang)], axis=1).astype(np.float32)
    return dict(c_ident=ident, c_bones=bones, c_masks=np.ascontiguousarray(masks),
                c_cs_p=cs(np.arange(T)), c_cs_s=cs(PAST + np.arange(DEC)))


_NC_CACHE = {}


def kernel(x_prompt, x_sample, cache_k, cache_v, state_wkv, state_shift, norm_pre, w_in,
           lam_q1, lam_k1, lam_q2, lam_k2, subln, mu_shift, w0, w_up, a0, a_up, k_k, k_a,
           r_k, ln_x_w, ln_x_b, w_out, norm_post):
    f = lambda a: np.ascontiguousarray(np.asarray(a, dtype=np.float32))
    x_prompt = f(x_prompt); x_sample = f(x_sample)
    B, T, _ = x_prompt.shape
    if T not in _NC_CACHE:
        _NC_CACHE[T] = build(T)
    nc = _NC_CACHE[T]
    consts = _consts(T)
    shared = dict(norm_pre=f(norm_pre)[0], w_in=f(w_in)[0], lam_q1=f(lam_q1)[0], lam_k1=f(lam_k1)[0],
                  lam_q2=f(lam_q2)[0], lam_k2=f(lam_k2)[0], subln=f(subln)[0], mu_shift=f(mu_shift)[0],
                  w0=f(w0)[0], w_up=f(w_up)[0], a0=f(a0)[0], a_up=f(a_up)[0], k_k=f(k_k)[0], k_a=f(k_a)[0],
                  r_k=f(r_k)[0].reshape(-1), ln_x_w=f(ln_x_w)[0], ln_x_b=f(ln_x_b)[0], w_out=f(w_out)[0],
                  norm_post=f(norm_post)[0])
    shared.update(consts)
    cache_k = f(cache_k); cache_v = f(cache_v); state_wkv = f(state_wkv); state_shift = f(state_shift)
    in_maps = []
    for c in range(B):
        m = dict(shared)
        m.update(x_p=x_prompt[c], x_s=x_sample[c], cache_k=cache_k[0, c].reshape(PAST, D),
                 cache_v=cache_v[0, c].reshape(PAST, D), state_wkv=state_wkv[0, c],
                 state_shift=state_shift[0, c, 0])
        in_maps.append(m)
    res = run_bass_kernel_spmd(nc, in_maps, core_ids=list(range(B)))
    R = res.results
    st = lambda k: np.stack([np.asarray(r[k], dtype=np.float32) for r in R], axis=0)
    y_p = st('y_p'); y_s = st('y_s')
    k_p = st('k_p').reshape(1, B, T, 8, 2, 64); v_p = st('v_p').reshape(1, B, T, 8, 128)
    wkv_p = st('wkv_p').reshape(1, B, 16, 64, 64); shift_p = st('shift_p').reshape(1, B, 1, 3200)
    k_s = st('k_s').reshape(1, B, DEC, 8, 2, 64); v_s = st('v_s').reshape(1, B, DEC, 8, 128)
    wkv_s = st('wkv_s').reshape(1, B, 16, 64, 64); shift_s = st('shift_s').reshape(1, B, 1, 3200)
    return (y_p, y_s, k_p, v_p, wkv_p, shift_p, k_s, v_s, wkv_s, shift_s)
```
